# Optimizing a Trainium2 kernel written in Bass

```python
import jax, jax.numpy as jnp
from jax import lax
import numpy as np

D_MODEL = 1024
BATCH = 16
SEQ = 2048
DEPTH = 2
DEC_BATCH = 16
DEC_SEQ = 32
PAST_LEN = 1024

CHUNK = 64
EPS = 1e-6
CONV_WIDTH = 4
MB_INNER = 1024
MB_HEADDIM = 64
MB_HEADS = MB_INNER // MB_HEADDIM
MB_GROUPS = 4
MB_HPG = MB_HEADS // MB_GROUPS
MB_STATE = 128
MB_CONV_CH = MB_INNER + 2 * MB_GROUPS * MB_STATE
MB_NORM_GROUP = MB_INNER // MB_GROUPS
HG_WIDTH = 1024
HG_EXPAND = 128
HG_HEADS = HG_WIDTH // HG_EXPAND
HG_DK = HG_EXPAND
HG_DV = HG_WIDTH // HG_HEADS
RG_WIDTH = 1024
RG_BLOCKS = 16
RG_BLOCK = RG_WIDTH // RG_BLOCKS
RG_C = 8.0
N_BRANCH = 3
BRANCH_WIDTH = 1024
FF_HIDDEN = -(-8 * D_MODEL // (3 * 256)) * 256
IN_SIZES = (MB_INNER, MB_CONV_CH, MB_HEADS,
            HG_HEADS * HG_DK, HG_HEADS * HG_DK, HG_WIDTH, HG_WIDTH,
            RG_WIDTH, RG_WIDTH,
            N_BRANCH * D_MODEL)
IN_WIDTH = sum(IN_SIZES)

kernel_name = 'hybrid_ssd_hgrn2_rglru_stream_step'


def rmsnorm(x, w):
    xf = x.astype(jnp.float32)
    y = xf * lax.rsqrt(jnp.mean(xf * xf, axis=-1, keepdims=True) + EPS)
    return (y * w.astype(jnp.float32)).astype(x.dtype)


def causal_conv(u, hist, w, b):
    L = u.shape[1]
    up = jnp.concatenate([hist.astype(u.dtype), u], axis=1)
    y = b + up[:, 0:L] * w[0]
    for k in range(1, CONV_WIDTH):
        y = y + up[:, k:k + L] * w[k]
    return y, up[:, L:]


def chunk_len(L):
    if L % CHUNK == 0:
        return CHUNK
    assert L <= CHUNK
    return L


def to_chunks(a, q):
    b, l = a.shape[:2]
    return jnp.moveaxis(a.reshape((b, l // q, q) + a.shape[2:]), 1, 0)


def from_chunks(a):
    c, b, q = a.shape[:3]
    return jnp.moveaxis(a, 0, 1).reshape((b, c * q) + a.shape[3:])


def masked_exp(mask, v):
    return jnp.where(mask, jnp.exp(jnp.where(mask, v, 0.0)), 0.0)


def ssd_scan(xs, dt, a, bm, cm, h0):
    q = chunk_len(xs.shape[1])
    causal = jnp.tril(jnp.ones((q, q), dtype=bool))[None, :, :, None, None]

    def body(h, inp):
        xc, dtc, bc, cc = inp
        acs = jnp.cumsum(dtc * a, axis=1)
        seg = acs[:, :, None] - acs[:, None]
        lmat = masked_exp(causal, seg)
        cb = jnp.einsum('bign,bjgn->bijg', cc, bc)
        m = cb[..., None] * lmat * dtc[:, None]
        y = jnp.einsum('bijgr,bjgrp->bigrp', m, xc)
        y = y + jnp.einsum('bign,bigr,bgrpn->bigrp', cc, jnp.exp(acs), h)
        last = acs[:, -1]
        w_end = jnp.exp(last[:, None] - acs) * dtc
        h = jnp.exp(last)[..., None, None] * h + jnp.einsum('bjgn,bjgr,bjgrp->bgrpn', bc, w_end, xc)
        return h, y

    h_t, ys = lax.scan(body, h0, (to_chunks(xs, q), to_chunks(dt, q), to_chunks(bm, q), to_chunks(cm, q)))
    return from_chunks(ys), h_t


def hgrn_scan(qh, kh, log_g, vh, s0):
    q = chunk_len(qh.shape[1])
    causal = jnp.tril(jnp.ones((q, q), dtype=bool))[None, :, :, None, None]

    def body(s, inp):
        qc, kc, gc, vc = inp
        gcs = jnp.cumsum(gc, axis=1)
        dec = masked_exp(causal, gcs[:, :, None] - gcs[:, None])
        att = jnp.einsum('bihk,bijhk->bijh', qc, dec * kc[:, None])
        o = jnp.einsum('bijh,bjhv->bihv', att, vc)
        o = o + jnp.einsum('bihk,bhkv->bihv', qc * jnp.exp(gcs), s)
        last = gcs[:, -1]
        s = jnp.exp(last)[..., None] * s + jnp.einsum('bjhk,bjhv->bhkv', kc * jnp.exp(last[:, None] - gcs), vc)
        return s, o

    s_t, os_ = lax.scan(body, s0, (to_chunks(qh, q), to_chunks(kh, q), to_chunks(log_g, q), to_chunks(vh, q)))
    return from_chunks(os_), s_t


def rglru(xc, h0, w_a, b_a, w_i, b_i, lam):
    b, l, _ = xc.shape
    xb = xc.reshape(b, l, RG_BLOCKS, RG_BLOCK)
    r = jax.nn.sigmoid(jnp.einsum('blhi,hij->blhj', xb, w_a).reshape(b, l, RG_WIDTH) + b_a)
    i = jax.nn.sigmoid(jnp.einsum('blhi,hij->blhj', xb, w_i).reshape(b, l, RG_WIDTH) + b_i)
    log_a = -RG_C * jax.nn.softplus(-lam) * r
    a = jnp.exp(log_a)
    u = jnp.sqrt(-jnp.expm1(2.0 * log_a)) * (i * xc)
    u = u.at[:, 0].add(a[:, 0] * h0)

    def combine(left, right):
        a1, b1 = left
        a2, b2 = right
        return a1 * a2, a2 * b1 + b2

    _, h = lax.associative_scan(combine, (a, u), axis=1)
    return h, h[:, -1]


def zero_states(b, dtype):
    return (jnp.zeros((b, MB_HEADS, MB_HEADDIM, MB_STATE), jnp.float32),
            jnp.zeros((b, CONV_WIDTH - 1, MB_CONV_CH), dtype),
            jnp.zeros((b, HG_HEADS, HG_DK, HG_DV), jnp.float32),
            jnp.zeros((b, RG_WIDTH), jnp.float32),
            jnp.zeros((b, CONV_WIDTH - 1, RG_WIDTH), dtype))


def layer(x, st, p):
    f32 = jnp.float32
    ssm0, conv_mb0, hg0, rg0, conv_rg0 = st
    b, L, _ = x.shape
    h = rmsnorm(x, p['norm_mix'])
    proj = h @ p['w_in']
    split_pts = [int(v) for v in np.cumsum(IN_SIZES)[:-1]]
    z, xbc, dt_raw, hq, hf, hi, hgate, rgate, rx, mg = jnp.split(proj, split_pts, axis=-1)

    xbc, conv_mb_new = causal_conv(xbc, conv_mb0, p['mb_conv_w'], p['mb_conv_b'])
    xbc = jax.nn.silu(xbc.astype(f32))
    xs, bm, cm = jnp.split(xbc, [MB_INNER, MB_INNER + MB_GROUPS * MB_STATE], axis=-1)
    xs = xs.reshape(b, L, MB_GROUPS, MB_HPG, MB_HEADDIM)
    bm = bm.reshape(b, L, MB_GROUPS, MB_STATE)
    cm = cm.reshape(b, L, MB_GROUPS, MB_STATE)
    dt = jax.nn.softplus(dt_raw.astype(f32) + p['mb_dt_bias'].astype(f32)).reshape(b, L, MB_GROUPS, MB_HPG)
    a = -jnp.exp(p['mb_a_log'].astype(f32)).reshape(MB_GROUPS, MB_HPG)
    h0 = ssm0.astype(f32).reshape(b, MB_GROUPS, MB_HPG, MB_HEADDIM, MB_STATE)
    y, ssm_new = ssd_scan(xs, dt, a, bm, cm, h0)
    y = y + p['mb_d'].astype(f32).reshape(MB_GROUPS, MB_HPG)[..., None] * xs
    y = y.reshape(b, L, MB_GROUPS, MB_NORM_GROUP) * jax.nn.silu(z.astype(f32)).reshape(b, L, MB_GROUPS, MB_NORM_GROUP)
    y_mb = rmsnorm(y, p['mb_norm_w'].reshape(MB_GROUPS, MB_NORM_GROUP)).reshape(b, L, MB_INNER)
    ssm_new = ssm_new.reshape(b, MB_HEADS, MB_HEADDIM, MB_STATE)

    lb = p['hg_lb'].astype(f32).reshape(HG_HEADS, HG_DK)
    qh = jax.nn.silu(hq.astype(f32)).reshape(b, L, HG_HEADS, HG_DK)
    fh = hf.astype(f32).reshape(b, L, HG_HEADS, HG_DK)
    log_g = jnp.log(lb + (1.0 - lb) * jax.nn.sigmoid(fh))
    kh = (1.0 - lb) * jax.nn.sigmoid(-fh)
    vh = hi.astype(f32).reshape(b, L, HG_HEADS, HG_DV)
    o, hg_new = hgrn_scan(qh, kh, log_g, vh, hg0.astype(f32))
    y_hg = rmsnorm(o, p['hg_norm_w']).reshape(b, L, HG_WIDTH) * jax.nn.silu(hgate.astype(f32))

    xr, conv_rg_new = causal_conv(rx, conv_rg0, p['rg_conv_w'], p['rg_conv_b'])
    hr, rg_new = rglru(xr.astype(f32), rg0.astype(f32), p['rg_w_a'], p['rg_b_a'], p['rg_w_i'], p['rg_b_i'], p['rg_lambda'])
    y_rg = jax.nn.gelu(rgate.astype(f32)) * hr

    branches = jnp.stack([y_mb, y_hg, y_rg], axis=0).astype(x.dtype)
    pb = jnp.einsum('kblc,kcd->kbld', branches, p['w_branch'])
    gates = jax.nn.sigmoid(mg.reshape(b, L, N_BRANCH, D_MODEL))
    merged = jnp.einsum('kbld,blkd->bld', pb, gates)
    x = x + merged @ p['w_out']

    h2 = rmsnorm(x, p['norm_ffn'])
    gu = h2 @ p['w_ffn_in']
    g, u = jnp.split(gu, [FF_HIDDEN], axis=-1)
    x = x + (jax.nn.silu(g) * u) @ p['w_ffn_out']
    return x, (ssm_new, conv_mb_new, hg_new, rg_new, conv_rg_new)


def setup_inputs(seed: int = 0) -> dict:
    key = jax.random.key(seed)
    ks = iter(jax.random.split(key, 40))
    nrm = lambda shape, s=1.0: s * jax.random.normal(next(ks), shape, jnp.float32)
    dt0 = jnp.exp(jax.random.uniform(next(ks), (DEPTH, MB_HEADS), jnp.float32, np.log(1e-3), np.log(1e-1)))
    a0 = jax.random.uniform(next(ks), (DEPTH, RG_WIDTH), jnp.float32, 0.9, 0.999)
    pa = a0 ** (1.0 / RG_C)
    return {
        'x_prompt': nrm((BATCH, SEQ, D_MODEL)),
        'x_sample': nrm((DEC_BATCH, DEC_SEQ, D_MODEL)),
        'state_ssm': nrm((DEPTH, DEC_BATCH, MB_HEADS, MB_HEADDIM, MB_STATE), 0.1),
        'state_ssm_conv': nrm((DEPTH, DEC_BATCH, CONV_WIDTH - 1, MB_CONV_CH)),
        'state_hgrn': nrm((DEPTH, DEC_BATCH, HG_HEADS, HG_DK, HG_DV), 0.1),
        'state_rglru': nrm((DEPTH, DEC_BATCH, RG_WIDTH), 0.5),
        'state_rglru_conv': nrm((DEPTH, DEC_BATCH, CONV_WIDTH - 1, RG_WIDTH)),
        'norm_mix': 1.0 + nrm((DEPTH, D_MODEL), 0.05),
        'w_in': nrm((DEPTH, D_MODEL, IN_WIDTH), D_MODEL ** -0.5),
        'mb_conv_w': nrm((DEPTH, CONV_WIDTH, MB_CONV_CH), CONV_WIDTH ** -0.5),
        'mb_conv_b': nrm((DEPTH, MB_CONV_CH), 0.02),
        'mb_dt_bias': dt0 + jnp.log(-jnp.expm1(-dt0)),
        'mb_a_log': jnp.log(jax.random.uniform(next(ks), (DEPTH, MB_HEADS), jnp.float32, 1.0, 16.0)),
        'mb_d': 1.0 + nrm((DEPTH, MB_HEADS), 0.1),
        'mb_norm_w': 1.0 + nrm((DEPTH, MB_INNER), 0.05),
        'hg_lb_logits': nrm((DEPTH, HG_HEADS * HG_DK), 0.1),
        'hg_norm_w': 1.0 + nrm((DEPTH, HG_DV), 0.05),
        'rg_conv_w': nrm((DEPTH, CONV_WIDTH, RG_WIDTH), CONV_WIDTH ** -0.5),
        'rg_conv_b': nrm((DEPTH, RG_WIDTH), 0.02),
        'rg_w_a': nrm((DEPTH, RG_BLOCKS, RG_BLOCK, RG_BLOCK), RG_BLOCK ** -0.5),
        'rg_b_a': nrm((DEPTH, RG_WIDTH), 0.02),
        'rg_w_i': nrm((DEPTH, RG_BLOCKS, RG_BLOCK, RG_BLOCK), RG_BLOCK ** -0.5),
        'rg_b_i': nrm((DEPTH, RG_WIDTH), 0.02),
        'rg_lambda': jnp.log(pa) - jnp.log1p(-pa),
        'w_branch': nrm((DEPTH, N_BRANCH, BRANCH_WIDTH, D_MODEL), BRANCH_WIDTH ** -0.5),
        'w_out': nrm((DEPTH, D_MODEL, D_MODEL), D_MODEL ** -0.5),
        'norm_ffn': 1.0 + nrm((DEPTH, D_MODEL), 0.05),
        'w_ffn_in': nrm((DEPTH, D_MODEL, 2 * FF_HIDDEN), D_MODEL ** -0.5),
        'w_ffn_out': nrm((DEPTH, FF_HIDDEN, D_MODEL), FF_HIDDEN ** -0.5),
        'norm_final': 1.0 + nrm((D_MODEL,), 0.05),
    }


def reference(x_prompt, x_sample, state_ssm, state_ssm_conv, state_hgrn, state_rglru, state_rglru_conv,
              norm_mix, w_in, mb_conv_w, mb_conv_b, mb_dt_bias, mb_a_log, mb_d, mb_norm_w,
              hg_lb_logits, hg_norm_w, rg_conv_w, rg_conv_b, rg_w_a, rg_b_a, rg_w_i, rg_b_i, rg_lambda,
              w_branch, w_out, norm_ffn, w_ffn_in, w_ffn_out, norm_final):
    sm = jax.nn.softmax(hg_lb_logits.astype(jnp.float32), axis=0)
    lbs = jnp.cumsum(sm, axis=0) - sm[0]
    yp, ys = x_prompt, x_sample
    new_p = [[], [], [], [], []]
    new_s = [[], [], [], [], []]
    for l in range(DEPTH):
        p = dict(norm_mix=norm_mix[l], w_in=w_in[l], mb_conv_w=mb_conv_w[l], mb_conv_b=mb_conv_b[l],
                 mb_dt_bias=mb_dt_bias[l], mb_a_log=mb_a_log[l], mb_d=mb_d[l], mb_norm_w=mb_norm_w[l],
                 hg_lb=lbs[l], hg_norm_w=hg_norm_w[l], rg_conv_w=rg_conv_w[l], rg_conv_b=rg_conv_b[l],
                 rg_w_a=rg_w_a[l], rg_b_a=rg_b_a[l], rg_w_i=rg_w_i[l], rg_b_i=rg_b_i[l], rg_lambda=rg_lambda[l],
                 w_branch=w_branch[l], w_out=w_out[l], norm_ffn=norm_ffn[l], w_ffn_in=w_ffn_in[l],
                 w_ffn_out=w_ffn_out[l])
        yp, sp = layer(yp, zero_states(yp.shape[0], yp.dtype), p)
        ys, ss = layer(ys, (state_ssm[l], state_ssm_conv[l], state_hgrn[l], state_rglru[l], state_rglru_conv[l]), p)
        for j in range(5):
            new_p[j].append(sp[j])
            new_s[j].append(ss[j])
    y_prompt = rmsnorm(yp, norm_final)
    y_sample = rmsnorm(ys, norm_final)
    new_ssm_p = jnp.stack(new_p[0])
    new_ssm_conv_p = jnp.stack(new_p[1])
    new_hgrn_p = jnp.stack(new_p[2])
    new_rglru_p = jnp.stack(new_p[3])
    new_rglru_conv_p = jnp.stack(new_p[4])
    new_ssm_s = jnp.stack(new_s[0])
    new_ssm_conv_s = jnp.stack(new_s[1])
    new_hgrn_s = jnp.stack(new_s[2])
    new_rglru_s = jnp.stack(new_s[3])
    new_rglru_conv_s = jnp.stack(new_s[4])
    return (y_prompt, y_sample, new_ssm_p, new_ssm_conv_p, new_hgrn_p, new_rglru_p, new_rglru_conv_p,
            new_ssm_s, new_ssm_conv_s, new_hgrn_s, new_rglru_s, new_rglru_conv_s)
```

```python
import contextlib
import numpy as np
import concourse.bass as bass
import concourse.mybir as mybir
from concourse.bass_utils import run_bass_kernel_spmd

F32 = mybir.dt.float32
BF16 = mybir.dt.bfloat16
AF = mybir.ActivationFunctionType
ALU = mybir.AluOpType

D = 1024
SEQ = 2048
DEC_SEQ = 32
EPS = 1e-6
FF = 2816
INW = 12304
OFF_Z, OFF_XS, OFF_B, OFF_C, OFF_DT = 0, 1024, 2048, 2560, 3072
OFF_HQ, OFF_HF, OFF_HI, OFF_HG, OFF_RGATE, OFF_RX, OFF_MG = 3088, 4112, 5136, 6160, 7184, 8208, 9232
NEG = -30000.0


class V:
    __slots__ = ("ap", "keys")

    def __init__(self, ap, keys):
        self.ap = ap
        self.keys = keys

    def r(self, pat, **kw):
        return V(self.ap.rearrange(pat, **kw), self.keys)

    def __getitem__(self, idx):
        return V(self.ap[idx], self.keys)

    def bc(self, shape):
        return V(self.ap.to_broadcast(list(shape)), self.keys)

    def us(self, axis):
        return V(self.ap.unsqueeze(axis), self.keys)


class Buf:
    def __init__(self, name, t, free, leaf):
        self.name, self.t, self.free, self.leaf = name, t, free, leaf

    def v(self, f0, f1, p0=0, p1=128):
        assert 0 <= f0 < f1 <= self.free, (self.name, f0, f1, self.free)
        keys = tuple((self.name, j) for j in range(f0 // self.leaf, (f1 - 1) // self.leaf + 1))
        return V(self.t[p0:p1, f0:f1], keys)


class Op:
    __slots__ = ("eng", "fn", "reads", "writes", "dma", "deps", "needed", "sig", "waits")

    def __init__(self, eng, fn, reads, writes, dma):
        self.eng, self.fn, self.reads, self.writes, self.dma = eng, fn, reads, writes, dma
        self.deps = ()
        self.needed = False
        self.sig = None
        self.waits = ()


NS_DMA = 8


class Rec:
    def __init__(self):
        self.ops = []

    def add(self, eng, fn, reads=(), writes=(), dma=False):
        rk = []
        for x in reads:
            if x is not None and not isinstance(x, (int, float)):
                rk.extend(x.keys)
        wk = []
        for x in writes:
            if x is not None:
                wk.extend(x.keys)
        self.ops.append(Op(eng, fn, tuple(rk), tuple(wk), dma))

    def mm(self, out, lhsT, rhs, start=True, stop=True):
        rd = [lhsT, rhs] + ([] if start else [out])
        self.add("pe", lambda e: e.matmul(out.ap, lhsT.ap, rhs.ap, start=start, stop=stop), rd, [out])

    def tr(self, out, in_, ident):
        if in_.ap.dtype == F32:
            self.add("pe", lambda e: e.matmul(out.ap, in_.ap, ident.ap, start=True, stop=True), [in_, ident], [out])
        else:
            self.add("pe", lambda e: e.transpose(out.ap, in_.ap, ident.ap), [in_, ident], [out])

    def act(self, out, in_, func, bias=None, scale=None, accum=None):
        kw = {}
        rd = [in_]
        if bias is not None:
            if isinstance(bias, V):
                kw["bias"] = bias.ap
                rd.append(bias)
            else:
                kw["bias"] = float(bias)
        if scale is not None:
            if isinstance(scale, V):
                kw["scale"] = scale.ap
                rd.append(scale)
            else:
                kw["scale"] = float(scale)
        wr = [out]
        if accum is not None:
            kw["accum_out"] = accum.ap
            wr.append(accum)
        self.add("act", lambda e: e.activation(out=out.ap, in_=in_.ap, func=func, **kw), rd, wr)

    def tt(self, out, in0, in1, op, eng="dve"):
        self.add(eng, lambda e: e.tensor_tensor(out=out.ap, in0=in0.ap, in1=in1.ap, op=op), [in0, in1], [out])

    def ts(self, out, in0, s1, s2, op0, op1=None, eng="dve"):
        rd = [in0]
        a1 = s1
        a2 = s2
        if isinstance(s1, V):
            rd.append(s1)
            a1 = s1.ap
        if isinstance(s2, V):
            rd.append(s2)
            a2 = s2.ap
        if op1 is None:
            self.add(eng, lambda e: e.tensor_scalar(out=out.ap, in0=in0.ap, scalar1=a1, scalar2=None, op0=op0), rd, [out])
        else:
            self.add(eng, lambda e: e.tensor_scalar(out=out.ap, in0=in0.ap, scalar1=a1, scalar2=a2, op0=op0, op1=op1), rd, [out])

    def stt(self, out, in0, scalar, in1, op0, op1):
        rd = [in0, in1]
        sc = scalar
        if isinstance(scalar, V):
            rd.append(scalar)
            sc = scalar.ap
        self.add("dve", lambda e: e.scalar_tensor_tensor(out=out.ap, in0=in0.ap, scalar=sc, in1=in1.ap, op0=op0, op1=op1), rd, [out])

    def scan(self, out, d0, d1, init, op0, op1):
        rd = [d0, d1]
        iv = init
        if isinstance(init, V):
            rd.append(init)
            iv = init.ap
        self.add("dve", lambda e: e.tensor_tensor_scan(out=out.ap, data0=d0.ap, data1=d1.ap, initial=iv, op0=op0, op1=op1), rd, [out])

    def copy(self, out, in_, eng="dve"):
        if eng == "act":
            self.add("act", lambda e: e.copy(out.ap, in_.ap), [in_], [out])
        else:
            self.add(eng, lambda e: e.tensor_copy(out=out.ap, in_=in_.ap), [in_], [out])

    def memset(self, out, val, eng="dve"):
        self.add(eng, lambda e: e.memset(out.ap, val), [], [out])

    def recip(self, out, in_):
        self.add("dve", lambda e: e.reciprocal(out.ap, in_.ap), [in_], [out])

    def dma(self, q, out, in_, slow=False):
        if slow:
            self.add(q, lambda e: e.dma_start(out=out.ap, in_=in_.ap, allow_slow_non_contiguous=True), [in_], [out], dma=True)
        else:
            self.add(q, lambda e: e.dma_start(out=out.ap, in_=in_.ap), [in_], [out], dma=True)

    def finalize(self, nc, es):
        ops = self.ops
        last_w = {}
        readers = {}
        for i, op in enumerate(ops):
            deps = set()
            for k in op.reads:
                w = last_w.get(k)
                if w is not None:
                    deps.add(w)
                if k[0][0] == "P" and k[0] != "PC":
                    for r_ in readers.get(k, ()):
                        if ops[r_].eng != op.eng:
                            deps.add(r_)
            for k in op.writes:
                w = last_w.get(k)
                if w is not None:
                    deps.add(w)
                rs = readers.get(k)
                if rs:
                    deps.update(rs)
            deps.discard(i)
            if op.eng == "pe":
                deps = {d for d in deps if ops[d].eng != "pe" or ops[d].dma}
            op.deps = sorted(deps)
            for d in op.deps:
                ops[d].needed = True
            for k in op.writes:
                last_w[k] = i
                readers[k] = []
            for k in op.reads:
                if k not in op.writes:
                    readers.setdefault(k, []).append(i)
        engs = ["sp", "act", "pe", "dve", "pool"]
        csem = {e: es.enter_context(nc.semaphore("c_" + e)) for e in engs}
        dsem = {e: [es.enter_context(nc.semaphore("d_%s%d" % (e, j))) for j in range(NS_DMA)] for e in ("sp", "pool")}
        ccount = {e: 0 for e in engs}
        dcount = {e: 0 for e in dsem}
        waited = {e: {} for e in engs}
        for op in ops:
            w = []
            wd = waited[op.eng]
            need = {}
            for d in op.deps:
                sem, val = ops[d].sig
                if need.get(id(sem), (None, 0))[1] < val:
                    need[id(sem)] = (sem, val)
            for sid, (sem, val) in need.items():
                if wd.get(sid, 0) < val:
                    wd[sid] = val
                    w.append((sem, val))
            if op.dma:
                n = dcount[op.eng]
                dcount[op.eng] = n + 1
                sem = dsem[op.eng][n % NS_DMA]
                rnd = n // NS_DMA + 1
                if rnd > 1 and wd.get(id(sem), 0) < 16 * (rnd - 1):
                    wd[id(sem)] = 16 * (rnd - 1)
                    w.append((sem, 16 * (rnd - 1)))
                op.sig = (sem, 16 * rnd)
            elif op.needed:
                ccount[op.eng] += 1
                op.sig = (csem[op.eng], ccount[op.eng])
            op.waits = w
        per = {e: [op for op in ops if op.eng == e] for e in engs}
        self.stats = {e: len(per[e]) for e in engs}

        def emit(e, name):
            for op in per[name]:
                for sem, val in op.waits:
                    e.wait_ge(sem, val)
                ins = op.fn(e)
                if op.dma:
                    ins.then_inc(op.sig[0], 16)
                elif op.sig is not None:
                    ins.then_inc(op.sig[0], 1)
            if name in dsem:
                n = dcount[name]
                for j in range(NS_DMA):
                    cnt = (n - j + NS_DMA - 1) // NS_DMA if n > j else 0
                    if cnt > 0:
                        e.wait_ge(dsem[name][j], 16 * cnt)

        with nc.Block() as block:
            @block.sync
            def _(e):
                emit(e, "sp")

            @block.scalar
            def _(e):
                emit(e, "act")

            @block.tensor
            def _(e):
                emit(e, "pe")

            @block.vector
            def _(e):
                emit(e, "dve")

            @block.gpsimd
            def _(e):
                emit(e, "pool")


def _tables(q):
    P = 2 * q
    t = np.arange(P)
    U2 = ((t[:, None] // q == t[None, :] // q) & (t[:, None] <= t[None, :])).astype(np.float32)
    Uq = ((t[:, None] % q) <= np.arange(q)[None, :]).astype(np.float32)
    BD1 = (t[:, None] // q == t[None, :] // q).astype(np.float32)
    m01 = (np.arange(q)[None, :] >= (t[:, None] % q)).astype(np.float32)
    neg = np.where(m01 > 0, 0.0, NEG).astype(np.float32)
    sel0 = np.repeat((t[:, None] // q == 0).astype(np.float32), 128, axis=1)
    sel1 = np.repeat((t[:, None] // q == 1).astype(np.float32), 128, axis=1)
    cols = [U2, Uq, BD1, neg, m01, sel0, sel1]
    out = np.zeros((128, sum(c.shape[1] for c in cols)), np.float32)
    o = 0
    offs = {}
    for nm, c in zip(["U2", "Uq", "BD1", "neg", "m01", "sel0", "sel1"], cols):
        out[:P, o:o + c.shape[1]] = c
        offs[nm] = (o, c.shape[1])
        o += c.shape[1]
    return out, offs


def make_ctab():
    ident = np.eye(128, dtype=np.float32)
    ones = np.ones((128, 512), np.float32)
    t64, o64 = _tables(64)
    t32, o32 = _tables(32)
    tab = np.concatenate([ident, ones, t64, t32], axis=1)
    base64 = 128 + 512
    base32 = base64 + t64.shape[1]
    offs = {"ident": (0, 128), "ones": (128, 512)}
    for k, (o, n) in o64.items():
        offs[(64, k)] = (base64 + o, n)
    for k, (o, n) in o32.items():
        offs[(32, k)] = (base32 + o, n)
    return np.ascontiguousarray(tab), offs


CTAB, COFF = make_ctab()
NCT = CTAB.shape[1]


class Seg:
    def __init__(self, kind, seqs, tok0, n_piece, q, first, last):
        self.kind = kind
        self.seqs = seqs
        self.tok0 = tok0
        self.n_piece = n_piece
        self.npc = len(seqs)
        self.N = n_piece * len(seqs)
        self.q = q
        self.P = 2 * q
        self.NT = self.N // self.P
        self.first = first
        self.last = last

    def chunk_info(self, tt, c):
        if self.kind == "p":
            return self.seqs[0], (self.first and tt == 0 and c == 0), (self.last and tt == self.NT - 1 and c == 1)
        return self.seqs[c], True, True


def build(nseg_p=4, nseq_p=2, with_sample=True, nlayers=2, dbg=None, phases="BCDEF", stop=99):
    nc = bass.Bass("TRN2", target_bir_lowering=False)
    R = Rec()
    es = contextlib.ExitStack()
    with es:
        def din(name, shape):
            return nc.dram_tensor(name, list(shape), F32, kind="ExternalInput").ap()

        def dout(name, shape):
            return nc.dram_tensor(name, list(shape), F32, kind="ExternalOutput").ap()

        def DV(ap):
            return V(ap, ())

        xp = din("xp", (2, SEQ, D))
        xs = din("xs", (2, DEC_SEQ, D))
        st_ssm = din("st_ssm", (2, 2, 16, 64, 128))
        st_sc = din("st_sc", (2, 2, 3, 2048))
        st_hg = din("st_hg", (2, 2, 8, 128, 128))
        st_rg = din("st_rg", (2, 2, 1024))
        st_rc = din("st_rc", (2, 2, 3, 1024))
        norm_mix = din("norm_mix", (2, D))
        w_in = din("w_in", (2, D, INW))
        mb_conv_w = din("mb_conv_w", (2, 4, 2048))
        mb_conv_b = din("mb_conv_b", (2, 2048))
        mb_dt_bias = din("mb_dt_bias", (2, 16))
        mb_a_log = din("mb_a_log", (2, 16))
        mb_d = din("mb_d", (2, 16))
        mb_norm_w = din("mb_norm_w", (2, 1024))
        hg_lb_logits = din("hg_lb_logits", (2, 1024))
        hg_norm_w = din("hg_norm_w", (2, 128))
        rg_conv_w = din("rg_conv_w", (2, 4, 1024))
        rg_conv_b = din("rg_conv_b", (2, 1024))
        rg_w_a = din("rg_w_a", (2, 16, 64, 64))
        rg_b_a = din("rg_b_a", (2, 1024))
        rg_w_i = din("rg_w_i", (2, 16, 64, 64))
        rg_b_i = din("rg_b_i", (2, 1024))
        rg_lambda = din("rg_lambda", (2, 1024))
        w_branch = din("w_branch", (2, 3, D, D))
        w_out = din("w_out", (2, D, D))
        norm_ffn = din("norm_ffn", (2, D))
        w_ffn_in = din("w_ffn_in", (2, D, 2 * FF))
        w_ffn_out = din("w_ffn_out", (2, FF, D))
        norm_final = din("norm_final", (D,))
        ctab = din("ctab", (128, NCT))

        yp = dout("yp", (2, SEQ, D))
        ys = dout("ys", (2, DEC_SEQ, D))
        outs = {}
        for g in ("p", "s"):
            outs[g] = dict(
                ssm=dout("o_ssm_" + g, (2, 2, 16, 64, 128)),
                sc=dout("o_sc_" + g, (2, 2, 3, 2048)),
                hg=dout("o_hg_" + g, (2, 2, 8, 128, 128)),
                rg=dout("o_rg_" + g, (2, 2, 1024)),
                rc=dout("o_rc_" + g, (2, 2, 3, 1024)),
            )
        dbg_out = None
        if dbg:
            dbg_out = dout("dbg", (128, dbg))

        def sb(name, free, dtype, leaf):
            t = es.enter_context(nc.sbuf_tensor(name, [128, free], dtype))
            return Buf(name, t, free, leaf)

        def psb(name, free, dtype, leaf):
            t = es.enter_context(nc.psum_tensor(name, [128, free], dtype))
            return Buf(name, t, free, leaf)

        NMAX = 512
        CT = sb("CT", NCT, F32, NCT)
        CTB = sb("CTB", 128 + 128, BF16, 256)
        xT = sb("xT", 8 * NMAX, F32, NMAX)
        hT = sb("hT", 8 * NMAX, BF16, NMAX)
        mergedT = sb("mergedT", 8 * NMAX, F32, NMAX)
        ykT = sb("ykT", 8 * NMAX, BF16, NMAX)
        NWB = 5
        WB = sb("WB", NWB * 4096, BF16, 4096)
        ssmT = [sb("ssmT%d" % l, 1024, F32, 256) for l in range(2)]
        hgS = [sb("hgS%d" % l, 1024, F32, 128) for l in range(2)]
        rgH = [sb("rgH%d" % l, 16, F32, 16) for l in range(2)]
        histM = [sb("histM%d" % l, 16 * 2 * 3, F32, 6) for l in range(2)]
        histR = [sb("histR%d" % l, 8 * 2 * 3, F32, 6) for l in range(2)]
        ssmB = sb("ssmB", 2 * 256, BF16, 256)
        hgB = sb("hgB", 2 * 128, BF16, 128)
        LC = []
        for l in range(2):
            LC.append(dict(
                nmix=sb("nmix%d" % l, 8, F32, 8), nffn=sb("nffn%d" % l, 8, F32, 8),
                mcw=sb("mcw%d" % l, 64, F32, 64), mcb=sb("mcb%d" % l, 16, F32, 16),
                rcw=sb("rcw%d" % l, 32, F32, 32), rcb=sb("rcb%d" % l, 8, F32, 8),
                rba=sb("rba%d" % l, 8, F32, 8), rbi=sb("rbi%d" % l, 8, F32, 8),
                rc1=sb("rc1%d" % l, 8, F32, 8), rc2=sb("rc2%d" % l, 8, F32, 8),
                lb=sb("lb%d" % l, 8, F32, 8), oml=sb("oml%d" % l, 8, F32, 8), noml=sb("noml%d" % l, 8, F32, 8),
                mnw=sb("mnw%d" % l, 1024, F32, 1024), hnw=sb("hnw%d" % l, 128, F32, 128),
                dtb=sb("dtb%d" % l, 16, F32, 16), aneg=sb("aneg%d" % l, 16, F32, 16), dd=sb("dd%d" % l, 16, F32, 16),
                bdA=sb("bdA%d" % l, 1024, BF16, 128), bdI=sb("bdI%d" % l, 1024, BF16, 128),
                wdt=sb("wdt%d" % l, 128, BF16, 128),
            ))
        nfin = sb("nfin", 8, F32, 8)
        CBT = sb("CBT", 4, F32, 4)
        ctmp = sb("ctmp", 64, F32, 64)
        NF = 10
        SF = sb("SF", NF * NMAX, F32, NMAX)
        NBH = 6
        SH = sb("SH", NBH * NMAX, BF16, NMAX)
        actT = sb("actT", 11 * NMAX, BF16, NMAX)
        PC = sb("PC", 2 * 520, F32, 520)
        SM = sb("SM", 512, F32, 16)
        TM = sb("TM", 8 * 256, F32, 256)
        TMH = sb("TMH", 8 * 256, BF16, 256)
        IO = sb("IO", 2 * 1024, F32, 1024)
        PA = [psb("PA%d" % j, 512, F32, 512) for j in range(3)]
        PS0 = psb("PS0", 512, F32, 512)
        PS1 = psb("PS1", 512, F32, 512)
        PS2 = psb("PS2", 512, F32, 512)
        PT = [psb("PT%d" % j, 1024, BF16, 1024) for j in range(2)]

        def cst(name, q=None):
            o, n = COFF[name if q is None else (q, name)]
            return o, n

        st = dict(pa=0, wb=0, sf=0, pt=0)

        def psA():
            st["pa"] = (st["pa"] + 1) % 3
            return PA[st["pa"]]

        def sf(N):
            st["sf"] = (st["sf"] + 1) % NF
            j = st["sf"]
            return SF.v(j * NMAX, j * NMAX + N)

        shc = [0]

        def sh(N):
            shc[0] = (shc[0] + 1) % NBH
            j = shc[0]
            return SH.v(j * NMAX, j * NMAX + N)

        ident_f = lambda p, n: CT.v(0, n, 0, p)
        ones_f = lambda p, n: CT.v(128, 128 + n, 0, p)
        ident_b = lambda p, n: CTB.v(0, n, 0, p)
        ones_b = lambda p, n: CTB.v(128, 128 + n, 0, p)

        def ctv(name, q, p0=0, p1=None):
            o, n = COFF[(q, name)]
            return CT.v(o, o + n, p0, 2 * q if p1 is None else p1)

        class WP:
            def __init__(self, b):
                self.b = b

            def t(self, kc, c0, n):
                return WB.v(self.b * 4096 + kc * 512 + c0, self.b * 4096 + kc * 512 + c0 + n)

        def wload(parts):
            st["wb"] = (st["wb"] + 1) % NWB
            b = st["wb"]
            for src, nk, dc in parts:
                n = src.shape[1]
                dst = WB.v(b * 4096, (b + 1) * 4096).r("p (kc n) -> p kc n", n=512)[:, 0:nk, dc:dc + n]
                R.dma("pool", dst, DV(src.rearrange("(kc p) n -> p kc n", p=128)))
            return WP(b)

        class WDT:
            def __init__(self, b):
                self.b = b

            def t(self, kc, c0, n):
                return self.b.v(kc * 16 + c0, kc * 16 + c0 + n)

        def win(l, c0, n):
            return (w_in[l, :, c0:c0 + n], 8, None)

        def wl(*parts):
            pl = []
            dc = 0
            for src, nk, _ in parts:
                pl.append((src, nk, dc))
                dc += src.shape[1]
            assert dc <= 512
            return wload(pl)

        R.memset(CBT.v(0, 1), 1.0)
        R.memset(CBT.v(1, 2), EPS)
        b_one = lambda p=128: CBT.v(0, 1, 0, p)
        b_eps = lambda p=128: CBT.v(1, 2, 0, p)
        R.dma("sp", CT.v(0, NCT), DV(ctab))
        R.dma("pool", CTB.v(0, 128), DV(ctab[:, 0:128]))
        R.dma("pool", CTB.v(128, 256), DV(ctab[:, 128:256]))

        def fm_load(dst, src1d, ncol):
            R.dma("sp", dst.v(0, ncol), DV(src1d.rearrange("(c p) -> p c", p=128)), slow=True)

        def bc_load(dst, src1d, n):
            R.dma("sp", dst.v(0, n), DV(src1d.partition_broadcast(128)), slow=True)

        for l in (range(nlayers) if stop >= 1 else ()):
            c = LC[l]
            fm_load(c["nmix"], norm_mix[l], 8)
            fm_load(c["nffn"], norm_ffn[l], 8)
            for k in range(4):
                R.dma("sp", c["mcw"].v(0, 64).r("p (c k) -> p c k", k=4)[:, :, k], DV(mb_conv_w[l, k].rearrange("(c p) -> p c", p=128)), slow=True)
                R.dma("sp", c["rcw"].v(0, 32).r("p (c k) -> p c k", k=4)[:, :, k], DV(rg_conv_w[l, k].rearrange("(c p) -> p c", p=128)), slow=True)
            fm_load(c["mcb"], mb_conv_b[l], 16)
            fm_load(c["rcb"], rg_conv_b[l], 8)
            fm_load(c["rba"], rg_b_a[l], 8)
            fm_load(c["rbi"], rg_b_i[l], 8)
            fm_load(c["rc1"], rg_lambda[l], 8)
            t0 = ctmp.v(0, 8)
            R.act(t0, c["rc1"].v(0, 8), AF.Exp, scale=-1.0)
            R.act(t0, t0, AF.Ln, bias=b_one())
            R.ts(c["rc1"].v(0, 8), t0, -8.0, None, ALU.mult)
            R.ts(c["rc2"].v(0, 8), t0, -16.0, None, ALU.mult)
            if l == 0:
                R.memset(c["lb"].v(0, 8), 0.0)
            else:
                t1 = ctmp.v(8, 16)
                t2 = ctmp.v(16, 24)
                fm_load_v = lambda dstv, src: R.dma("sp", dstv, DV(src.rearrange("(c p) -> p c", p=128)), slow=True)
                fm_load_v(t1, hg_lb_logits[0])
                fm_load_v(t2, hg_lb_logits[1])
                R.tt(t2, t2, t1, ALU.subtract)
                R.act(c["lb"].v(0, 8), t2, AF.Sigmoid)
            R.ts(c["oml"].v(0, 8), c["lb"].v(0, 8), -1.0, 1.0, ALU.mult, ALU.add)
            R.ts(c["noml"].v(0, 8), c["oml"].v(0, 8), -1.0, None, ALU.mult)
            bc_load(c["mnw"], mb_norm_w[l], 1024)
            bc_load(c["hnw"], hg_norm_w[l], 128)
            bc_load(c["dtb"], mb_dt_bias[l], 16)
            bc_load(c["dd"], mb_d[l], 16)
            bc_load(c["aneg"], mb_a_log[l], 16)
            R.act(c["aneg"].v(0, 16), c["aneg"].v(0, 16), AF.Exp)
            R.ts(c["aneg"].v(0, 16), c["aneg"].v(0, 16), -1.0, None, ALU.mult)
            for nm, wsrc in (("bdA", rg_w_a), ("bdI", rg_w_i)):
                R.memset(c[nm].v(0, 1024), 0.0, eng="pool")
                for par in range(2):
                    dst = c[nm].v(0, 1024, par * 64, par * 64 + 64).r("p (m j) -> p m j", j=128)[:, :, par * 64:par * 64 + 64]
                    src = wsrc[l].rearrange("(m two) i j -> two i m j", two=2)[par]
                    R.dma("pool", dst, DV(src))
            R.dma("pool", c["wdt"].v(0, 128).r("p (kc n) -> p kc n", n=16), DV(w_in[l, :, OFF_DT:OFF_DT + 16].rearrange("(kc p) n -> p kc n", p=128)))
        fm_load(nfin, norm_final, 8)

        def dense_fm(ps, wp, coff, src, N, nk=8, first=True, last=True, kbase=0):
            for k in range(nk):
                R.mm(ps, wp.t(k, coff, 128), src.v((kbase + k) * NMAX, (kbase + k) * NMAX + N),
                     start=(first and k == 0), stop=(last and k == nk - 1))

        def dense_tm(ps, wp, coff, n, src, tok0, P):
            for k in range(8):
                R.mm(ps, src.v(k * NMAX + tok0, k * NMAX + tok0 + P), wp.t(k, coff, n), start=(k == 0), stop=(k == 7))

        def rmsnorm_fm(nw, N, dst):
            ps = psA().v(0, N)
            for cch in range(8):
                sq = sh(N)
                R.act(sq, xT.v(cch * NMAX, cch * NMAX + N), AF.Square)
                R.mm(ps, ones_b(128, 128), sq, start=(cch == 0), stop=(cch == 7))
            rs = sf(N)
            R.act(rs, ps, AF.Sqrt, bias=b_eps(), scale=1.0 / D)
            R.recip(rs, rs)
            for cch in range(8):
                R.stt(dst.v(cch * NMAX, cch * NMAX + N), xT.v(cch * NMAX, cch * NMAX + N), nw.v(cch, cch + 1), rs, ALU.mult, ALU.mult)

        def conv_fm(ps, seg, hist, ci, cw, cb, out_f32):
            npc, n = seg.npc, seg.n_piece
            W = 3 + n
            pcb = PC.v((ci % 2) * 520, (ci % 2) * 520 + npc * W)
            pc3 = pcb.r("p (a w) -> p a w", w=W)
            R.copy(pc3[:, :, 3:W], ps.r("p (a n) -> p a n", n=n), eng="act")
            h3 = hist.v(ci * 6, ci * 6 + npc * 3).r("p (a k) -> p a k", k=3)
            R.copy(pc3[:, :, 0:3], h3)
            o3 = out_f32.r("p (a n) -> p a n", n=n)
            R.ts(o3, pc3[:, :, 0:n], cw.v(ci * 4, ci * 4 + 1), cb.v(ci, ci + 1), ALU.mult, ALU.add)
            for k in range(1, 4):
                R.stt(o3, pc3[:, :, k:k + n], cw.v(ci * 4 + k, ci * 4 + k + 1), o3, ALU.mult, ALU.add)
            R.copy(h3, pc3[:, :, n:n + 3])

        def layer_pass(seg, l):
            c = LC[l]
            N, P, q, NT = seg.N, seg.P, seg.q, seg.NT
            og = outs[seg.kind]
            if seg.first:
                for pc in range(seg.npc):
                    sq_ = seg.seqs[pc]
                    hm = histM[l].v(0, 96).r("p (c a k) -> p c a k", a=2, k=3)[:, :, pc, :]
                    hr = histR[l].v(0, 48).r("p (c a k) -> p c a k", a=2, k=3)[:, :, pc, :]
                    rh = rgH[l].v(0, 16).r("p (m a) -> p m a", a=2)[:, :, pc]
                    if seg.kind == "p":
                        R.memset(hm, 0.0)
                        R.memset(hr, 0.0)
                        R.memset(rh, 0.0)
                    else:
                        for k in range(3):
                            R.dma("sp", hm[:, :, k], DV(st_sc[l, sq_, k].rearrange("(c p) -> p c", p=128)), slow=True)
                            R.dma("sp", hr[:, :, k], DV(st_rc[l, sq_, k].rearrange("(c p) -> p c", p=128)), slow=True)
                        R.dma("sp", rh, DV(st_rg[l, sq_].rearrange("(m p) -> p m", p=128)), slow=True)
            rmsnorm_fm(c["nmix"], N, hT)
            if stop <= 3:
                return

            def merge(k):
                for half in range(2):
                    wg = wl(win(l, OFF_MG + k * 1024 + half * 512, 512))
                    wb_ = wl((w_branch[l, k, :, half * 512:(half + 1) * 512], 8, None))
                    for j in range(4):
                        dc = half * 4 + j
                        psg = psA().v(0, N)
                        dense_fm(psg, wg, j * 128, hT, N)
                        sg = sf(N)
                        R.act(sg, psg, AF.Sigmoid)
                        psp = psA().v(0, N)
                        dense_fm(psp, wb_, j * 128, ykT, N)
                        md = mergedT.v(dc * NMAX, dc * NMAX + N)
                        if k == 0:
                            R.tt(md, psp, sg, ALU.mult)
                        else:
                            R.tt(sg, psp, sg, ALU.mult)
                            R.tt(md, md, sg, ALU.add)

            def smv(tt, o, n, p1=None):
                return SM.v(tt * 64 + o, tt * 64 + o + n, 0, P if p1 is None else p1)
            for tt in (range(NT) if "B" in phases else ()):
                ps = PS2.v(320, 336, 0, P)
                dense_tm(ps, WDT(c["wdt"]), 0, 16, hT, tt * P, P)
                dt_ = smv(tt, 0, 16)
                R.tt(dt_, ps, c["dtb"].v(0, 16, 0, P), ALU.add)
                R.act(dt_, dt_, AF.Exp)
                R.act(dt_, dt_, AF.Ln, bias=b_one(P))
                dtA = smv(tt, 16, 16)
                R.tt(dtA, dt_, c["aneg"].v(0, 16, 0, P), ALU.mult)
                ps2 = PS2.v(336, 352, 0, P)
                R.mm(ps2, ctv("U2", q), dtA)
                acs = smv(tt, 32, 16)
                R.copy(acs, ps2)
                R.act(smv(tt, 48, 16), ps2, AF.Exp)
            for g in (range(4) if "B" in phases else ()):
                wa = wl(win(l, OFF_XS + g * 256, 256), win(l, OFF_B + g * 128, 128), win(l, OFF_C + g * 128, 128))
                wz = wl(win(l, OFF_Z + g * 256, 256))
                xa = []
                for j, ci in enumerate((2 * g, 2 * g + 1, 8 + g, 12 + g)):
                    ps = psA().v(0, N)
                    dense_fm(ps, wa, j * 128, hT, N)
                    cf = sf(N)
                    conv_fm(ps, seg, histM[l], ci, c["mcw"], c["mcb"], cf)
                    xb_ = sh(N)
                    R.act(xb_, cf, AF.Silu)
                    xa.append(xb_)
                BT, CTt = xa[2], xa[3]
                for tt in range(NT):
                    t0_ = tt * P
                    pt = PT[tt % 2]
                    for j in range(3):
                        R.tr(pt.v(j * 128, j * 128 + 128, 0, P), xa[j][:, t0_:t0_ + P], ident_b(128, 128))
                    xtm = TMH.v(0, 256, 0, P)
                    btm = TMH.v(256, 384, 0, P)
                    R.copy(TMH.v(0, 384, 0, P), pt.v(0, 384, 0, P))
                    psz = psA().v(0, 256, 0, P)
                    dense_tm(psz, wz, 0, 256, hT, t0_, P)
                    zs = TM.v(0, 256, 0, P)
                    R.act(zs, psz, AF.Silu)
                    dt_g = smv(tt, 4 * g, 4)
                    dtA_g = smv(tt, 16 + 4 * g, 4)
                    acs_g = smv(tt, 32 + 4 * g, 4)
                    eacs_g = smv(tt, 48 + 4 * g, 4)
                    rhsU = TM.v(256, 256 + 4 * q, 0, P)
                    R.tt(rhsU.r("p (h i) -> p h i", i=q), ctv("Uq", q).us(1).bc([P, 4, q]), dtA_g.us(2).bc([P, 4, q]), ALU.mult)
                    pab = PS0.v(0, 4 * q, 0, P)
                    R.mm(pab, ctv("BD1", q), rhsU)
                    segm = TM.v(512, 512 + 4 * q, 0, P)
                    R.tt(segm.r("p (h i) -> p h i", i=q), pab.r("p (h i) -> p h i", i=q), acs_g.us(2).bc([P, 4, q]), ALU.subtract)
                    R.tt(segm.r("p (h i) -> p h i", i=q), segm.r("p (h i) -> p h i", i=q), ctv("neg", q).us(1).bc([P, 4, q]), ALU.add)
                    R.act(segm, segm, AF.Exp)
                    wend = SM.v(256 + 0, 256 + 4, 0, P)
                    R.tt(wend, pab.r("p (h i) -> p h i", i=q)[:, :, q - 1], acs_g, ALU.subtract)
                    R.act(wend, wend, AF.Exp)
                    R.tt(wend, wend, dt_g, ALU.mult)
                    pcb = PS2.v(256, 256 + q, 0, P)
                    for cc in range(2):
                        R.mm(PS2.v(256, 256 + q, cc * q, cc * q + q), BT[:, t0_ + cc * q:t0_ + cc * q + q], CTt[:, t0_ + cc * q:t0_ + cc * q + q])
                    R.tt(segm.r("p (h i) -> p h i", i=q), segm.r("p (h i) -> p h i", i=q), pcb.us(1).bc([P, 4, q]), ALU.mult)
                    MT = TMH.v(512, 512 + 4 * q, 0, P)
                    R.tt(MT.r("p (h i) -> p h i", i=q), segm.r("p (h i) -> p h i", i=q), dt_g.us(2).bc([P, 4, q]), ALU.mult)
                    xw = TMH.v(768, 1024, 0, P)
                    R.tt(xw.r("p (h d) -> p h d", d=64), xtm.r("p (h d) -> p h d", d=64), wend.us(2).bc([P, 4, 64]), ALU.mult)
                    for cc in range(2):
                        for h in range(4):
                            R.mm(PS1.v(h * 64, h * 64 + 64, cc * q, cc * q + q),
                                 TMH.v(512 + h * q, 512 + h * q + q, cc * q, cc * q + q),
                                 TMH.v(h * 64, h * 64 + 64, cc * q, cc * q + q))
                    sT = ssmT[l].v(g * 256, g * 256 + 256)
                    for cc in range(2):
                        sq_, isf, isl = seg.chunk_info(tt, cc)
                        if isf:
                            if seg.kind == "p":
                                R.memset(sT, 0.0)
                            else:
                                stg = IO.v(0, 256)
                                R.dma("sp", stg.r("p (a n) -> p a n", n=128), DV(st_ssm[l, sq_, 4 * g:4 * g + 4].rearrange("(a two) d n -> (two d) a n", two=2)))
                                pf = psA()
                                for a in range(2):
                                    R.tr(pf.v(a * 128, a * 128 + 128), stg[:, a * 128:a * 128 + 128], ident_f(128, 128))
                                R.copy(sT, pf.v(0, 256))
                        sB = ssmB.v(((tt * 2 + cc) % 2) * 256, ((tt * 2 + cc) % 2) * 256 + 256)
                        R.copy(sB, sT, eng="act")
                        R.mm(PS1.v(256, 512, cc * q, cc * q + q), CTt[:, t0_ + cc * q:t0_ + cc * q + q], sB)
                        pst = psA().v(0, 256)
                        R.mm(pst, TMH.v(256, 384, cc * q, cc * q + q), TMH.v(768, 1024, cc * q, cc * q + q))
                        pdc = PS2.v(352, 356)
                        R.mm(pdc, ctv("sel%d" % cc, q), dtA_g)
                        dcy = SM.v(256 + 16, 256 + 20)
                        R.act(dcy, pdc, AF.Exp)
                        R.tt(sT.r("p (h d) -> p h d", d=64), sT.r("p (h d) -> p h d", d=64), dcy.us(2).bc([128, 4, 64]), ALU.mult)
                        R.tt(sT, sT, pst, ALU.add)
                        if isl:
                            pf = psA()
                            for a in range(2):
                                R.tr(pf.v(a * 128, a * 128 + 128), sT[:, a * 128:a * 128 + 128], ident_f(128, 128))
                            stg = IO.v(256, 512)
                            R.copy(stg, pf.v(0, 256))
                            R.dma("sp", DV(og["ssm"][l, sq_, 4 * g:4 * g + 4].rearrange("(a two) d n -> (two d) a n", two=2)), stg.r("p (a n) -> p a n", n=128))
                    y1 = TM.v(768, 1024, 0, P)
                    R.tt(y1.r("p (h d) -> p h d", d=64), PS1.v(256, 512, 0, P).r("p (h d) -> p h d", d=64), eacs_g.us(2).bc([P, 4, 64]), ALU.mult)
                    R.tt(y1, y1, PS1.v(0, 256, 0, P), ALU.add)
                    y2 = TM.v(1024, 1280, 0, P)
                    R.tt(y2.r("p (h d) -> p h d", d=64), xtm.r("p (h d) -> p h d", d=64), c["dd"].v(4 * g, 4 * g + 4, 0, P).us(2).bc([P, 4, 64]), ALU.mult)
                    R.tt(y1, y1, y2, ALU.add)
                    R.tt(y1, y1, zs, ALU.mult)
                    ssq = SM.v(256 + 32, 256 + 33, 0, P)
                    R.act(y2, y1, AF.Square, accum=ssq)
                    R.act(ssq, ssq, AF.Sqrt, bias=b_eps(P), scale=1.0 / 256)
                    R.recip(ssq, ssq)
                    yn = TMH.v(1024, 1280, 0, P)
                    R.stt(yn, y1, ssq, c["mnw"].v(g * 256, g * 256 + 256, 0, P), ALU.mult, ALU.mult)
                    pt2 = PT[(tt + 1) % 2]
                    for a in range(2):
                        R.tr(pt2.v(512 + a * 128, 512 + a * 128 + P), yn[:, a * 128:a * 128 + 128], ident_b(P, P))
                    for a in range(2):
                        R.copy(ykT.v((2 * g + a) * NMAX + t0_, (2 * g + a) * NMAX + t0_ + P), pt2.v(512 + a * 128, 512 + a * 128 + P), eng="act")
            if "B" in phases:
                merge(0)

            NCH = N // q
            for hh in (range(8) if "C" in phases else ()):
                wh = wl(win(l, OFF_HQ + hh * 128, 128), win(l, OFF_HF + hh * 128, 128), win(l, OFF_HI + hh * 128, 128), win(l, OFF_HG + hh * 128, 128))
                psq = psA().v(0, N)
                dense_fm(psq, wh, 0, hT, N)
                qs = sf(N)
                R.act(qs, psq, AF.Silu)
                psf = psA().v(0, N)
                dense_fm(psf, wh, 128, hT, N)
                sig = sf(N)
                R.act(sig, psf, AF.Sigmoid)
                gl = sf(N)
                R.act(gl, sig, AF.Ln, bias=c["lb"].v(hh, hh + 1), scale=c["oml"].v(hh, hh + 1))
                kk = sf(N)
                R.ts(kk, sig, c["noml"].v(hh, hh + 1), c["oml"].v(hh, hh + 1), ALU.mult, ALU.add)
                gxb = PC.v(0, 1 + N)
                R.memset(gxb[:, 0:1], 0.0)
                R.scan(gxb[:, 1:1 + N], ones_f(128, N), gl, 0.0, ALU.mult, ALU.add)
                gx3 = gxb[:, 1:1 + N].r("p (c i) -> p c i", i=q)
                ref = gxb[:, q // 2:q // 2 + (NCH - 1) * q + 1:q] if NCH > 1 else gxb[:, q // 2:q // 2 + 1]
                beg = gxb[:, 0:(NCH - 1) * q + 1:q] if NCH > 1 else gxb[:, 0:1]
                end = gxb[:, q:q + (NCH - 1) * q + 1:q] if NCH > 1 else gxb[:, q:q + 1]
                A1 = sf(N)
                R.tt(A1.r("p (c i) -> p c i", i=q), gx3, ref.us(2).bc([128, NCH, q]), ALU.subtract)
                E1 = sf(N)
                R.act(E1, A1, AF.Exp)
                E2 = sf(N)
                R.act(E2, A1, AF.Exp, scale=-1.0)
                qe = sh(N)
                R.tt(qe, qs, E1, ALU.mult)
                ke = sh(N)
                R.tt(ke, kk, E2, ALU.mult)
                sm0 = SM.v(320, 320 + NCH)
                sm1 = SM.v(336, 336 + NCH)
                sm2 = SM.v(352, 352 + NCH)
                R.tt(sm0, ref, beg, ALU.subtract)
                R.act(sm0, sm0, AF.Exp)
                R.tt(sm1, end, ref, ALU.subtract)
                R.act(sm1, sm1, AF.Exp)
                R.tt(sm2, end, beg, ALU.subtract)
                R.act(sm2, sm2, AF.Exp)
                R.tt(E1.r("p (c i) -> p c i", i=q), E1.r("p (c i) -> p c i", i=q), sm0.us(2).bc([128, NCH, q]), ALU.mult)
                qg = sh(N)
                R.tt(qg, qs, E1, ALU.mult)
                R.tt(E2.r("p (c i) -> p c i", i=q), E2.r("p (c i) -> p c i", i=q), sm1.us(2).bc([128, NCH, q]), ALU.mult)
                kl = sh(N)
                R.tt(kl, kk, E2, ALU.mult)
                S_ = hgS[l].v(hh * 128, hh * 128 + 128)
                for tt in range(NT):
                    t0_ = tt * P
                    psv = psA().v(0, 128, 0, P)
                    dense_tm(psv, wh, 256, 128, hT, t0_, P)
                    vtm = TMH.v(1280, 1408, 0, P)
                    R.copy(vtm, psv, eng="act")
                    psg = psA().v(0, 128, 0, P)
                    dense_tm(psg, wh, 384, 128, hT, t0_, P)
                    gsl = TM.v(1280, 1408, 0, P)
                    R.act(gsl, psg, AF.Silu)
                    pt = PT[tt % 2]
                    R.tr(pt.v(0, 128, 0, P), kl[:, t0_:t0_ + P], ident_b(128, 128))
                    kltm = TMH.v(1408, 1536, 0, P)
                    R.copy(kltm, pt.v(0, 128, 0, P))
                    for cc in range(2):
                        R.mm(PS0.v(384, 384 + q, cc * q, cc * q + q), ke[:, t0_ + cc * q:t0_ + cc * q + q], qe[:, t0_ + cc * q:t0_ + cc * q + q])
                    attm = TMH.v(1536, 1536 + q, 0, P)
                    R.tt(attm, PS0.v(384, 384 + q, 0, P), ctv("m01", q), ALU.mult)
                    for cc in range(2):
                        sq_, isf, isl = seg.chunk_info(tt, cc)
                        if isf:
                            if seg.kind == "p":
                                R.memset(S_, 0.0)
                            else:
                                R.dma("sp", S_, DV(st_hg[l, sq_, hh]))
                        sB = hgB.v(((tt * 2 + cc) % 2) * 128, ((tt * 2 + cc) % 2) * 128 + 128)
                        R.copy(sB, S_, eng="act")
                        po = PS1.v(0, 128, cc * q, cc * q + q)
                        R.mm(po, TMH.v(1536, 1536 + q, cc * q, cc * q + q), TMH.v(1280, 1408, cc * q, cc * q + q), start=True, stop=False)
                        R.mm(po, qg[:, t0_ + cc * q:t0_ + cc * q + q], sB, start=False, stop=True)
                        pss = psA().v(0, 128)
                        R.mm(pss, TMH.v(1408, 1536, cc * q, cc * q + q), TMH.v(1280, 1408, cc * q, cc * q + q))
                        cg = tt * 2 + cc
                        R.stt(S_, S_, sm2[:, cg:cg + 1], pss, ALU.mult, ALU.add)
                        if isl:
                            R.dma("sp", DV(og["hg"][l, sq_, hh]), S_)
                    po_ = PS1.v(0, 128, 0, P)
                    ssq = SM.v(256 + 40, 256 + 41, 0, P)
                    junk = TM.v(1408, 1536, 0, P)
                    R.act(junk, po_, AF.Square, accum=ssq)
                    R.act(ssq, ssq, AF.Sqrt, bias=b_eps(P), scale=1.0 / 128)
                    R.recip(ssq, ssq)
                    on = TM.v(1536, 1664, 0, P)
                    R.stt(on, po_, ssq, c["hnw"].v(0, 128, 0, P), ALU.mult, ALU.mult)
                    yh = TMH.v(1664, 1792, 0, P)
                    R.tt(yh, on, gsl, ALU.mult)
                    pt2 = PT[(tt + 1) % 2]
                    R.tr(pt2.v(512, 512 + P), yh, ident_b(P, P))
                    R.copy(ykT.v(hh * NMAX + t0_, hh * NMAX + t0_ + P), pt2.v(512, 512 + P), eng="act")
            if "C" in phases:
                merge(1)

            for half in (range(2) if "D" in phases else ()):
                wrg = wl(win(l, OFF_RGATE + half * 512, 512))
                wrx = wl(win(l, OFF_RX + half * 512, 512))
                for j in range(4):
                    m = half * 4 + j
                    psx = psA().v(0, N)
                    dense_fm(psx, wrx, j * 128, hT, N)
                    xc = sf(N)
                    conv_fm(psx, seg, histR[l], m, c["rcw"], c["rcb"], xc)
                    xcb = sh(N)
                    R.copy(xcb, xc, eng="act")
                    psa = psA().v(0, N)
                    R.mm(psa, c["bdA"].v(m * 128, m * 128 + 128), xcb)
                    psi = psA().v(0, N)
                    R.mm(psi, c["bdI"].v(m * 128, m * 128 + 128), xcb)
                    rr = sf(N)
                    R.act(rr, psa, AF.Sigmoid, bias=c["rba"].v(m, m + 1))
                    ig = sf(N)
                    R.act(ig, psi, AF.Sigmoid, bias=c["rbi"].v(m, m + 1))
                    aa = sf(N)
                    R.act(aa, rr, AF.Exp, scale=c["rc1"].v(m, m + 1))
                    R.act(rr, rr, AF.Exp, scale=c["rc2"].v(m, m + 1))
                    R.act(rr, rr, AF.Sqrt, bias=b_one(), scale=-1.0)
                    R.tt(ig, ig, xc, ALU.mult)
                    R.tt(ig, ig, rr, ALU.mult)
                    hs = sf(N)
                    for pc in range(seg.npc):
                        n = seg.n_piece
                        hst = rgH[l].v(m * 2 + pc, m * 2 + pc + 1)
                        R.scan(hs[:, pc * n:(pc + 1) * n], aa[:, pc * n:(pc + 1) * n], ig[:, pc * n:(pc + 1) * n], hst, ALU.mult, ALU.add)
                        R.copy(hst, hs[:, (pc + 1) * n - 1:(pc + 1) * n])
                    psr = psA().v(0, N)
                    dense_fm(psr, wrg, j * 128, hT, N)
                    ge = sf(N)
                    R.act(ge, psr, AF.Gelu_apprx_tanh)
                    R.tt(ykT.v(m * NMAX, m * NMAX + N), ge, hs, ALU.mult)
            if "D" in phases:
                merge(2)

            if seg.last:
                for pc in range(seg.npc):
                    sq_ = seg.seqs[pc]
                    hm = histM[l].v(0, 96).r("p (c a k) -> p c a k", a=2, k=3)[:, :, pc, :]
                    hr = histR[l].v(0, 48).r("p (c a k) -> p c a k", a=2, k=3)[:, :, pc, :]
                    rh = rgH[l].v(0, 16).r("p (m a) -> p m a", a=2)[:, :, pc]
                    for k in range(3):
                        R.dma("sp", DV(og["sc"][l, sq_, k].rearrange("(c p) -> p c", p=128)), hm[:, :, k], slow=True)
                        R.dma("sp", DV(og["rc"][l, sq_, k].rearrange("(c p) -> p c", p=128)), hr[:, :, k], slow=True)
                    R.dma("sp", DV(og["rg"][l, sq_].rearrange("(m p) -> p m", p=128)), rh, slow=True)

            for cch in range(8):
                R.copy(hT.v(cch * NMAX, cch * NMAX + N), mergedT.v(cch * NMAX, cch * NMAX + N), eng="act")
            for half in (range(2) if "E" in phases else ()):
                wo = wl((w_out[l, :, half * 512:(half + 1) * 512], 8, None))
                for j in range(4):
                    dc = half * 4 + j
                    ps = psA().v(0, N)
                    dense_fm(ps, wo, j * 128, hT, N)
                    xv = xT.v(dc * NMAX, dc * NMAX + N)
                    R.tt(xv, xv, ps, ALU.add)

            rmsnorm_fm(c["nffn"], N, hT)
            for fh in (range(2) if "F" in phases else ()):
                fb = fh * 11
                fc = 0
                while fc < 11:
                    nf = min(4, 11 - fc)
                    wg_ = wl((w_ffn_in[l, :, (fb + fc) * 128:(fb + fc + nf) * 128], 8, None))
                    wu_ = wl((w_ffn_in[l, :, FF + (fb + fc) * 128:FF + (fb + fc + nf) * 128], 8, None))
                    for j in range(nf):
                        psg = psA().v(0, N)
                        dense_fm(psg, wg_, j * 128, hT, N)
                        sg = sf(N)
                        R.act(sg, psg, AF.Silu)
                        psu = psA().v(0, N)
                        dense_fm(psu, wu_, j * 128, hT, N)
                        R.tt(actT.v((fc + j) * NMAX, (fc + j) * NMAX + N), sg, psu, ALU.mult)
                    fc += nf
                for half in range(2):
                    pss = [PA[0].v(0, N), PA[1].v(0, N), PA[2].v(0, N), PS2.v(0, N)]
                    f0 = 0
                    while f0 < 11:
                        nf = min(8, 11 - f0)
                        wo = wl((w_ffn_out[l, (fb + f0) * 128:(fb + f0 + nf) * 128, half * 512:(half + 1) * 512], nf, None))
                        for j in range(4):
                            dense_fm(pss[j], wo, j * 128, actT, N, nk=nf, first=(f0 == 0), last=(f0 + nf == 11), kbase=f0)
                        f0 += nf
                    for j in range(4):
                        dc = half * 4 + j
                        xv = xT.v(dc * NMAX, dc * NMAX + N)
                        R.tt(xv, xv, pss[j], ALU.add)

        def run_segment(seg):
            N, P, NT = seg.N, seg.P, seg.NT
            for tt in range(NT):
                pc = (tt * P) // seg.n_piece
                tk = seg.tok0[pc] + (tt * P) % seg.n_piece
                io = IO.v((tt % 2) * 1024, (tt % 2) * 1024 + 1024, 0, P)
                if seg.kind == "p":
                    R.dma("sp", io, DV(xp[seg.seqs[0], tk:tk + P, :]))
                else:
                    for a in range(seg.npc):
                        R.dma("sp", IO.v((tt % 2) * 1024, (tt % 2) * 1024 + 1024, a * 32, a * 32 + 32), DV(xs[seg.seqs[a], :, :]))
                for half in (range(2) if "x" not in phases else ()):
                    pf = psA()
                    for j in range(4):
                        cch = half * 4 + j
                        R.tr(pf.v(j * 128, j * 128 + P), io[:, cch * 128:cch * 128 + 128], ident_f(P, P))
                    for j in (range(4) if "y" not in phases else ()):
                        cch = half * 4 + j
                        R.copy(xT.v(cch * NMAX + tt * P, cch * NMAX + tt * P + P), pf.v(j * 128, j * 128 + P), eng=("act" if (j % 2 and "v" not in phases) else "dve"))
            if stop <= 2:
                return
            for l in range(nlayers):
                layer_pass(seg, l)
            if stop <= 3:
                return
            rmsnorm_fm(nfin, N, mergedT)
            for tt in range(NT):
                io = IO.v((tt % 2) * 1024, (tt % 2) * 1024 + 1024, 0, P)
                for half in range(2):
                    pf = psA()
                    for j in range(4):
                        cch = half * 4 + j
                        R.tr(pf.v(j * 128, j * 128 + 128, 0, P), mergedT.v(cch * NMAX + tt * P, cch * NMAX + tt * P + P), ident_f(128, 128))
                    R.copy(io[:, half * 512:half * 512 + 512], pf.v(0, 512, 0, P), eng=("act" if half else "dve"))
                if seg.kind == "p":
                    tk = seg.tok0[0] + tt * P
                    R.dma("sp", DV(yp[seg.seqs[0], tk:tk + P, :]), io)
                else:
                    for a in range(seg.npc):
                        R.dma("sp", DV(ys[seg.seqs[a], :, :]), IO.v((tt % 2) * 1024, (tt % 2) * 1024 + 1024, a * 32, a * 32 + 32))

        segs = []
        for s in range(nseq_p):
            for j in range(nseg_p):
                segs.append(Seg("p", [s], [j * 512], 512, 64, j == 0, j == nseg_p - 1))
        if with_sample:
            segs.append(Seg("s", [0, 1], [0, 0], 32, 32, True, True))
        for sg_ in (segs if stop >= 2 else ()):
            run_segment(sg_)
        if dbg_out is not None:
            pass
        R.finalize(nc, es)
    return nc, R


_WNAMES = ["norm_mix", "w_in", "mb_conv_w", "mb_conv_b", "mb_dt_bias", "mb_a_log", "mb_d", "mb_norm_w",
           "hg_lb_logits", "hg_norm_w", "rg_conv_w", "rg_conv_b", "rg_w_a", "rg_b_a", "rg_w_i", "rg_b_i",
           "rg_lambda", "w_branch", "w_out", "norm_ffn", "w_ffn_in", "w_ffn_out", "norm_final"]


def make_in_maps(inputs, ncores=8):
    f = lambda a: np.ascontiguousarray(np.asarray(a, dtype=np.float32))
    wts = {k: f(inputs[k]) for k in _WNAMES}
    maps = []
    for c in range(ncores):
        s = slice(2 * c, 2 * c + 2)
        m = dict(wts)
        m["xp"] = f(inputs["x_prompt"][s])
        m["xs"] = f(inputs["x_sample"][s])
        m["st_ssm"] = f(inputs["state_ssm"][:, s])
        m["st_sc"] = f(inputs["state_ssm_conv"][:, s])
        m["st_hg"] = f(inputs["state_hgrn"][:, s])
        m["st_rg"] = f(inputs["state_rglru"][:, s])
        m["st_rc"] = f(inputs["state_rglru_conv"][:, s])
        m["ctab"] = CTAB
        maps.append(m)
    return maps


def gather(results):
    cat = lambda k, ax: np.concatenate([np.asarray(r[k], dtype=np.float32) for r in results], axis=ax)
    out = [cat("yp", 0), cat("ys", 0)]
    for g in ("p", "s"):
        for nm in ("ssm", "sc", "hg", "rg", "rc"):
            out.append(cat("o_%s_%s" % (nm, g), 1))
    return tuple(out)


def kernel(**inputs):
    nc, _ = build()
    maps = make_in_maps(inputs)
    res = run_bass_kernel_spmd(nc, maps, core_ids=list(range(8)))
    return gather(res.results)
```

```python
import contextlib
import numpy as np
import concourse.bass as bass
import concourse.mybir as mybir
from concourse.bass_utils import run_bass_kernel_spmd

F32 = mybir.dt.float32
BF16 = mybir.dt.bfloat16
AF = mybir.ActivationFunctionType
ALU = mybir.AluOpType

D = 1024
SEQ = 2048
DEC_SEQ = 32
EPS = 1e-6
FF = 2816
INW = 12304
OFF_Z, OFF_XS, OFF_B, OFF_C, OFF_DT = 0, 1024, 2048, 2560, 3072
OFF_HQ, OFF_HF, OFF_HI, OFF_HG, OFF_RGATE, OFF_RX, OFF_MG = 3088, 4112, 5136, 6160, 7184, 8208, 9232
NEG = -30000.0


class V:
    __slots__ = ("ap", "keys")

    def __init__(self, ap, keys):
        self.ap = ap
        self.keys = keys

    def r(self, pat, **kw):
        return V(self.ap.rearrange(pat, **kw), self.keys)

    def __getitem__(self, idx):
        return V(self.ap[idx], self.keys)

    def bc(self, shape):
        return V(self.ap.to_broadcast(list(shape)), self.keys)

    def us(self, axis):
        return V(self.ap.unsqueeze(axis), self.keys)


class Buf:
    def __init__(self, name, t, free, leaf):
        self.name, self.t, self.free, self.leaf = name, t, free, leaf

    def v(self, f0, f1, p0=0, p1=128):
        assert 0 <= f0 < f1 <= self.free, (self.name, f0, f1, self.free)
        keys = tuple((self.name, j) for j in range(f0 // self.leaf, (f1 - 1) // self.leaf + 1))
        return V(self.t[p0:p1, f0:f1], keys)


class Op:
    __slots__ = ("eng", "fn", "reads", "writes", "dma", "deps", "needed", "sig", "waits", "grp")

    def __init__(self, eng, fn, reads, writes, dma):
        self.eng, self.fn, self.reads, self.writes, self.dma = eng, fn, reads, writes, dma
        self.deps = ()
        self.needed = False
        self.sig = None
        self.waits = ()


NS_DMA = 8


class Rec:
    def __init__(self):
        self.ops = []
        self.gctr = 0
        self.gcur = None

    @contextlib.contextmanager
    def atomic(self):
        if self.gcur is not None:
            yield
            return
        self.gctr += 1
        self.gcur = self.gctr
        try:
            yield
        finally:
            self.gcur = None

    def add(self, eng, fn, reads=(), writes=(), dma=False):
        rk = []
        for x in reads:
            if x is not None and not isinstance(x, (int, float)):
                rk.extend(x.keys)
        wk = []
        for x in writes:
            if x is not None:
                wk.extend(x.keys)
        op = Op(eng, fn, tuple(rk), tuple(wk), dma)
        if self.gcur is not None:
            op.grp = self.gcur
        else:
            self.gctr += 1
            op.grp = self.gctr
        self.ops.append(op)

    def mm(self, out, lhsT, rhs, start=True, stop=True):
        rd = [lhsT, rhs] + ([] if start else [out])
        self.add("pe", lambda e: e.matmul(out.ap, lhsT.ap, rhs.ap, start=start, stop=stop), rd, [out])

    def tr(self, out, in_, ident):
        if in_.ap.dtype == F32:
            self.add("pe", lambda e: e.matmul(out.ap, in_.ap, ident.ap, start=True, stop=True), [in_, ident], [out])
        else:
            self.add("pe", lambda e: e.transpose(out.ap, in_.ap, ident.ap), [in_, ident], [out])

    def act(self, out, in_, func, bias=None, scale=None, accum=None):
        kw = {}
        rd = [in_]
        if bias is not None:
            if isinstance(bias, V):
                kw["bias"] = bias.ap
                rd.append(bias)
            else:
                kw["bias"] = float(bias)
        if scale is not None:
            if isinstance(scale, V):
                kw["scale"] = scale.ap
                rd.append(scale)
            else:
                kw["scale"] = float(scale)
        wr = [out]
        if accum is not None:
            kw["accum_out"] = accum.ap
            wr.append(accum)
        self.add("act", lambda e: e.activation(out=out.ap, in_=in_.ap, func=func, **kw), rd, wr)

    def tt(self, out, in0, in1, op, eng="dve"):
        self.add(eng, lambda e: e.tensor_tensor(out=out.ap, in0=in0.ap, in1=in1.ap, op=op), [in0, in1], [out])

    def ts(self, out, in0, s1, s2, op0, op1=None, eng="dve"):
        rd = [in0]
        a1 = s1
        a2 = s2
        if isinstance(s1, V):
            rd.append(s1)
            a1 = s1.ap
        if isinstance(s2, V):
            rd.append(s2)
            a2 = s2.ap
        if op1 is None:
            self.add(eng, lambda e: e.tensor_scalar(out=out.ap, in0=in0.ap, scalar1=a1, scalar2=None, op0=op0), rd, [out])
        else:
            self.add(eng, lambda e: e.tensor_scalar(out=out.ap, in0=in0.ap, scalar1=a1, scalar2=a2, op0=op0, op1=op1), rd, [out])

    def stt(self, out, in0, scalar, in1, op0, op1):
        rd = [in0, in1]
        sc = scalar
        if isinstance(scalar, V):
            rd.append(scalar)
            sc = scalar.ap
        self.add("dve", lambda e: e.scalar_tensor_tensor(out=out.ap, in0=in0.ap, scalar=sc, in1=in1.ap, op0=op0, op1=op1), rd, [out])

    def scan(self, out, d0, d1, init, op0, op1):
        rd = [d0, d1]
        iv = init
        if isinstance(init, V):
            rd.append(init)
            iv = init.ap
        self.add("dve", lambda e: e.tensor_tensor_scan(out=out.ap, data0=d0.ap, data1=d1.ap, initial=iv, op0=op0, op1=op1), rd, [out])

    def copy(self, out, in_, eng="dve"):
        if eng == "act":
            self.add("act", lambda e: e.copy(out.ap, in_.ap), [in_], [out])
        else:
            self.add(eng, lambda e: e.tensor_copy(out=out.ap, in_=in_.ap), [in_], [out])

    def memset(self, out, val, eng="dve"):
        self.add(eng, lambda e: e.memset(out.ap, val), [], [out])

    def recip(self, out, in_):
        self.add("dve", lambda e: e.reciprocal(out.ap, in_.ap), [in_], [out])

    def dma(self, q, out, in_, slow=False):
        if slow:
            self.add(q, lambda e: e.dma_start(out=out.ap, in_=in_.ap, allow_slow_non_contiguous=True), [in_], [out], dma=True)
        else:
            self.add(q, lambda e: e.dma_start(out=out.ap, in_=in_.ap), [in_], [out], dma=True)

    def finalize(self, nc, es):
        ops = self.ops
        last_w = {}
        readers = {}
        for i, op in enumerate(ops):
            deps = set()
            for k in op.reads:
                w = last_w.get(k)
                if w is not None:
                    deps.add(w)
                if k[0][0] == "P" and k[0] != "PC":
                    for r_ in readers.get(k, ()):
                        if ops[r_].eng != op.eng:
                            deps.add(r_)
            for k in op.writes:
                w = last_w.get(k)
                if w is not None:
                    deps.add(w)
                rs = readers.get(k)
                if rs:
                    deps.update(rs)
            deps.discard(i)
            if op.eng == "pe":
                deps = {d for d in deps if ops[d].eng != "pe" or ops[d].dma}
            op.deps = sorted(deps)
            for d in op.deps:
                ops[d].needed = True
            for k in op.writes:
                last_w[k] = i
                readers[k] = []
            for k in op.reads:
                if k not in op.writes:
                    readers.setdefault(k, []).append(i)
        engs = ["sp", "act", "pe", "dve", "pool"]
        csem = {e: es.enter_context(nc.semaphore("c_" + e)) for e in engs}
        dsem = {e: [es.enter_context(nc.semaphore("d_%s%d" % (e, j))) for j in range(NS_DMA)] for e in ("sp", "pool")}
        ccount = {e: 0 for e in engs}
        dcount = {e: 0 for e in dsem}
        waited = {e: {} for e in engs}
        for op in ops:
            w = []
            wd = waited[op.eng]
            need = {}
            for d in op.deps:
                sem, val = ops[d].sig
                if need.get(id(sem), (None, 0))[1] < val:
                    need[id(sem)] = (sem, val)
            for sid, (sem, val) in need.items():
                if wd.get(sid, 0) < val:
                    wd[sid] = val
                    w.append((sem, val))
            if op.dma:
                n = dcount[op.eng]
                dcount[op.eng] = n + 1
                sem = dsem[op.eng][n % NS_DMA]
                rnd = n // NS_DMA + 1
                if rnd > 1 and wd.get(id(sem), 0) < 16 * (rnd - 1):
                    wd[id(sem)] = 16 * (rnd - 1)
                    w.append((sem, 16 * (rnd - 1)))
                op.sig = (sem, 16 * rnd)
            elif op.needed:
                ccount[op.eng] += 1
                op.sig = (csem[op.eng], ccount[op.eng])
            op.waits = w
        per = {e: [op for op in ops if op.eng == e] for e in engs}
        self.stats = {e: len(per[e]) for e in engs}

        def emit(e, name):
            for op in per[name]:
                for sem, val in op.waits:
                    e.wait_ge(sem, val)
                ins = op.fn(e)
                if op.dma:
                    ins.then_inc(op.sig[0], 16)
                elif op.sig is not None:
                    ins.then_inc(op.sig[0], 1)
            if name in dsem:
                n = dcount[name]
                for j in range(NS_DMA):
                    cnt = (n - j + NS_DMA - 1) // NS_DMA if n > j else 0
                    if cnt > 0:
                        e.wait_ge(dsem[name][j], 16 * cnt)

        with nc.Block() as block:
            @block.sync
            def _(e):
                emit(e, "sp")

            @block.scalar
            def _(e):
                emit(e, "act")

            @block.tensor
            def _(e):
                emit(e, "pe")

            @block.vector
            def _(e):
                emit(e, "dve")

            @block.gpsimd
            def _(e):
                emit(e, "pool")


def _tables(q):
    P = 2 * q
    t = np.arange(P)
    U2 = ((t[:, None] // q == t[None, :] // q) & (t[:, None] <= t[None, :])).astype(np.float32)
    Uq = ((t[:, None] % q) <= np.arange(q)[None, :]).astype(np.float32)
    BD1 = (t[:, None] // q == t[None, :] // q).astype(np.float32)
    m01 = (np.arange(q)[None, :] >= (t[:, None] % q)).astype(np.float32)
    neg = np.where(m01 > 0, 0.0, NEG).astype(np.float32)
    sel0 = np.repeat((t[:, None] // q == 0).astype(np.float32), 128, axis=1)
    sel1 = np.repeat((t[:, None] // q == 1).astype(np.float32), 128, axis=1)
    cols = [U2, Uq, BD1, neg, m01, sel0, sel1]
    out = np.zeros((128, sum(c.shape[1] for c in cols)), np.float32)
    o = 0
    offs = {}
    for nm, c in zip(["U2", "Uq", "BD1", "neg", "m01", "sel0", "sel1"], cols):
        out[:P, o:o + c.shape[1]] = c
        offs[nm] = (o, c.shape[1])
        o += c.shape[1]
    return out, offs


def make_ctab():
    ident = np.eye(128, dtype=np.float32)
    ones = np.ones((128, 512), np.float32)
    t64, o64 = _tables(64)
    t32, o32 = _tables(32)
    tab = np.concatenate([ident, ones, t64, t32], axis=1)
    base64 = 128 + 512
    base32 = base64 + t64.shape[1]
    offs = {"ident": (0, 128), "ones": (128, 512)}
    for k, (o, n) in o64.items():
        offs[(64, k)] = (base64 + o, n)
    for k, (o, n) in o32.items():
        offs[(32, k)] = (base32 + o, n)
    return np.ascontiguousarray(tab), offs


CTAB, COFF = make_ctab()
NCT = CTAB.shape[1]


class Seg:
    def __init__(self, kind, seqs, tok0, n_piece, q, first, last):
        self.kind = kind
        self.seqs = seqs
        self.tok0 = tok0
        self.n_piece = n_piece
        self.npc = len(seqs)
        self.N = n_piece * len(seqs)
        self.q = q
        self.P = 2 * q
        self.NT = self.N // self.P
        self.first = first
        self.last = last

    def chunk_info(self, tt, c):
        if self.kind == "p":
            return self.seqs[0], (self.first and tt == 0 and c == 0), (self.last and tt == self.NT - 1 and c == 1)
        return self.seqs[c], True, True


def build(nseg_p=4, nseq_p=2, with_sample=True, nlayers=2, dbg=None, phases="BCDEF", stop=99):
    nc = bass.Bass("TRN2", target_bir_lowering=False)
    R = Rec()
    es = contextlib.ExitStack()
    with es:
        def din(name, shape):
            return nc.dram_tensor(name, list(shape), F32, kind="ExternalInput").ap()

        def dout(name, shape):
            return nc.dram_tensor(name, list(shape), F32, kind="ExternalOutput").ap()

        def DV(ap):
            return V(ap, ())

        xp = din("xp", (2, SEQ, D))
        xs = din("xs", (2, DEC_SEQ, D))
        st_ssm = din("st_ssm", (2, 2, 16, 64, 128))
        st_sc = din("st_sc", (2, 2, 3, 2048))
        st_hg = din("st_hg", (2, 2, 8, 128, 128))
        st_rg = din("st_rg", (2, 2, 1024))
        st_rc = din("st_rc", (2, 2, 3, 1024))
        norm_mix = din("norm_mix", (2, D))
        w_in = din("w_in", (2, D, INW))
        mb_conv_w = din("mb_conv_w", (2, 4, 2048))
        mb_conv_b = din("mb_conv_b", (2, 2048))
        mb_dt_bias = din("mb_dt_bias", (2, 16))
        mb_a_log = din("mb_a_log", (2, 16))
        mb_d = din("mb_d", (2, 16))
        mb_norm_w = din("mb_norm_w", (2, 1024))
        hg_lb_logits = din("hg_lb_logits", (2, 1024))
        hg_norm_w = din("hg_norm_w", (2, 128))
        rg_conv_w = din("rg_conv_w", (2, 4, 1024))
        rg_conv_b = din("rg_conv_b", (2, 1024))
        rg_w_a = din("rg_w_a", (2, 16, 64, 64))
        rg_b_a = din("rg_b_a", (2, 1024))
        rg_w_i = din("rg_w_i", (2, 16, 64, 64))
        rg_b_i = din("rg_b_i", (2, 1024))
        rg_lambda = din("rg_lambda", (2, 1024))
        w_branch = din("w_branch", (2, 3, D, D))
        w_out = din("w_out", (2, D, D))
        norm_ffn = din("norm_ffn", (2, D))
        w_ffn_in = din("w_ffn_in", (2, D, 2 * FF))
        w_ffn_out = din("w_ffn_out", (2, FF, D))
        norm_final = din("norm_final", (D,))
        ctab = din("ctab", (128, NCT))

        yp = dout("yp", (2, SEQ, D))
        ys = dout("ys", (2, DEC_SEQ, D))
        outs = {}
        for g in ("p", "s"):
            outs[g] = dict(
                ssm=dout("o_ssm_" + g, (2, 2, 16, 64, 128)),
                sc=dout("o_sc_" + g, (2, 2, 3, 2048)),
                hg=dout("o_hg_" + g, (2, 2, 8, 128, 128)),
                rg=dout("o_rg_" + g, (2, 2, 1024)),
                rc=dout("o_rc_" + g, (2, 2, 3, 1024)),
            )
        dbg_out = None
        if dbg:
            dbg_out = dout("dbg", (128, dbg))

        def sb(name, free, dtype, leaf):
            t = es.enter_context(nc.sbuf_tensor(name, [128, free], dtype))
            return Buf(name, t, free, leaf)

        def psb(name, free, dtype, leaf):
            t = es.enter_context(nc.psum_tensor(name, [128, free], dtype))
            return Buf(name, t, free, leaf)

        NMAX = 512
        CT = sb("CT", NCT, F32, NCT)
        CTB = sb("CTB", 128 + 128, BF16, 256)
        xT = sb("xT", 8 * NMAX, F32, NMAX)
        hT = sb("hT", 8 * NMAX, BF16, NMAX)
        mergedT = sb("mergedT", 8 * NMAX, F32, NMAX)
        ykT = sb("ykT", 8 * NMAX, BF16, NMAX)
        NWB = 4
        WB = sb("WB", NWB * 4096, BF16, 4096)
        ssmT = [sb("ssmT%d" % l, 1024, F32, 256) for l in range(2)]
        hgS = [sb("hgS%d" % l, 1024, F32, 128) for l in range(2)]
        rgH = [sb("rgH%d" % l, 16, F32, 2) for l in range(2)]
        histM = [sb("histM%d" % l, 16 * 2 * 3, F32, 6) for l in range(2)]
        histR = [sb("histR%d" % l, 8 * 2 * 3, F32, 6) for l in range(2)]
        ssmB = sb("ssmB", 4 * 256, BF16, 256)
        hgB = sb("hgB", 4 * 128, BF16, 128)
        LC = []
        for l in range(2):
            LC.append(dict(
                nmix=sb("nmix%d" % l, 8, F32, 8), nffn=sb("nffn%d" % l, 8, F32, 8),
                mcw=sb("mcw%d" % l, 64, F32, 64), mcb=sb("mcb%d" % l, 16, F32, 16),
                rcw=sb("rcw%d" % l, 32, F32, 32), rcb=sb("rcb%d" % l, 8, F32, 8),
                rba=sb("rba%d" % l, 8, F32, 8), rbi=sb("rbi%d" % l, 8, F32, 8),
                rc1=sb("rc1%d" % l, 8, F32, 8), rc2=sb("rc2%d" % l, 8, F32, 8),
                lb=sb("lb%d" % l, 8, F32, 8), oml=sb("oml%d" % l, 8, F32, 8), noml=sb("noml%d" % l, 8, F32, 8),
                hnw=sb("hnw%d" % l, 128, F32, 128),
                dtb=sb("dtb%d" % l, 16, F32, 16), aneg=sb("aneg%d" % l, 16, F32, 16), dd=sb("dd%d" % l, 16, F32, 16),
                bdA=sb("bdA%d" % l, 1024, BF16, 128), bdI=sb("bdI%d" % l, 1024, BF16, 128),
                wdt=sb("wdt%d" % l, 128, BF16, 128),
            ))
        nfin = sb("nfin", 8, F32, 8)
        MNW = sb("MNW", 1024, F32, 1024)
        CBT = sb("CBT", 4, F32, 4)
        ctmp = sb("ctmp", 64, F32, 64)
        actT = sb("actT", 11 * NMAX, BF16, NMAX)
        IO = sb("IO", 1024, F32, 1024)
        SMG = sb("SMG", 256, F32, 64)

        class Ctx:
            def __init__(self, i):
                self.i = i
                self.NF, self.NH = 5, 4
                self.SF = sb("SF%d" % i, self.NF * NMAX, F32, NMAX)
                self.SH = sb("SH%d" % i, self.NH * NMAX, BF16, NMAX)
                self.PC = sb("PC%d" % i, 2 * 520, F32, 520)
                self.SM = sb("SM%d" % i, 256, F32, 16)
                self.TM = sb("TM%d" % i, 8 * 256, F32, 256)
                self.TMH = sb("TMH%d" % i, 8 * 256, BF16, 256)
                self.IOS = sb("IOS%d" % i, 512, F32, 256)
                self.f = [psb("PF%d_%d" % (i, j), 512, F32, 512) for j in range(3)]
                self.t = psb("PTT%d" % i, 1024, BF16, 1024)
                self.ops = []
                self.sfc = 0
                self.shc = 0

            def sfb(self, j, N):
                return self.SF.v(j * NMAX, j * NMAX + N)

            def shb(self, j, N):
                return self.SH.v(j * NMAX, j * NMAX + N)

            def sf(self, N):
                self.sfc = (self.sfc + 1) % self.NF
                return self.sfb(self.sfc, N)

            def sh(self, N):
                self.shc = (self.shc + 1) % self.NH
                return self.shb(self.shc, N)

        CX = [Ctx(0), Ctx(1)]
        PALL = [CX[0].f[0], CX[0].f[1], CX[0].f[2], CX[1].f[0], CX[1].f[1], CX[1].f[2]]

        st = dict(pa=0, wb=0)

        def psA():
            st["pa"] = (st["pa"] + 1) % 6
            return PALL[st["pa"]]

        def sf(N):
            return CX[0].sf(N)

        def sh(N):
            return CX[0].sh(N)

        def run_streams(bodies):
            k = 0
            while k < len(bodies):
                pair = bodies[k:k + 2]
                lists = []
                for j, body in enumerate(pair):
                    main = R.ops
                    R.ops = []
                    body(CX[j])
                    lists.append(R.ops)
                    R.ops = main
                glists = []
                for x in lists:
                    gl_ = []
                    for op in x:
                        if gl_ and gl_[-1][0].grp == op.grp:
                            gl_[-1].append(op)
                        else:
                            gl_.append([op])
                    glists.append(gl_)
                m = max(len(x) for x in glists)
                for idx in range(m):
                    for x in glists:
                        if idx < len(x):
                            R.ops.extend(x[idx])
                k += 2

        ident_f = lambda p, n: CT.v(0, n, 0, p)
        ones_f = lambda p, n: CT.v(128, 128 + n, 0, p)
        ident_b = lambda p, n: CTB.v(0, n, 0, p)
        ones_b = lambda p, n: CTB.v(128, 128 + n, 0, p)

        def ctv(name, q, p0=0, p1=None):
            o, n = COFF[(q, name)]
            return CT.v(o, o + n, p0, 2 * q if p1 is None else p1)

        class WP:
            def __init__(self, b):
                self.b = b

            def t(self, kc, c0, n):
                return WB.v(self.b * 4096 + kc * 512 + c0, self.b * 4096 + kc * 512 + c0 + n)

        def wload(parts):
            st["wb"] = (st["wb"] + 1) % NWB
            b = st["wb"]
            for src, nk, dc in parts:
                n = src.shape[1]
                dst = WB.v(b * 4096, (b + 1) * 4096).r("p (kc n) -> p kc n", n=512)[:, 0:nk, dc:dc + n]
                R.dma("pool", dst, DV(src.rearrange("(kc p) n -> p kc n", p=128)))
            return WP(b)

        class WDT:
            def __init__(self, b):
                self.b = b

            def t(self, kc, c0, n):
                return self.b.v(kc * 16 + c0, kc * 16 + c0 + n)

        def win(l, c0, n):
            return (w_in[l, :, c0:c0 + n], 8, None)

        def wl(*parts):
            pl = []
            dc = 0
            for src, nk, _ in parts:
                pl.append((src, nk, dc))
                dc += src.shape[1]
            assert dc <= 512
            return wload(pl)

        R.memset(CBT.v(0, 1), 1.0)
        R.memset(CBT.v(1, 2), EPS)
        b_one = lambda p=128: CBT.v(0, 1, 0, p)
        b_eps = lambda p=128: CBT.v(1, 2, 0, p)
        R.dma("sp", CT.v(0, NCT), DV(ctab))
        R.dma("pool", CTB.v(0, 128), DV(ctab[:, 0:128]))
        R.dma("pool", CTB.v(128, 256), DV(ctab[:, 128:256]))

        def fm_load(dst, src1d, ncol):
            R.dma("sp", dst.v(0, ncol), DV(src1d.rearrange("(c p) -> p c", p=128)), slow=True)

        def bc_load(dst, src1d, n):
            R.dma("sp", dst.v(0, n), DV(src1d.partition_broadcast(128)), slow=True)

        for l in (range(nlayers) if stop >= 1 else ()):
            c = LC[l]
            fm_load(c["nmix"], norm_mix[l], 8)
            fm_load(c["nffn"], norm_ffn[l], 8)
            for k in range(4):
                R.dma("sp", c["mcw"].v(0, 64).r("p (c k) -> p c k", k=4)[:, :, k], DV(mb_conv_w[l, k].rearrange("(c p) -> p c", p=128)), slow=True)
                R.dma("sp", c["rcw"].v(0, 32).r("p (c k) -> p c k", k=4)[:, :, k], DV(rg_conv_w[l, k].rearrange("(c p) -> p c", p=128)), slow=True)
            fm_load(c["mcb"], mb_conv_b[l], 16)
            fm_load(c["rcb"], rg_conv_b[l], 8)
            fm_load(c["rba"], rg_b_a[l], 8)
            fm_load(c["rbi"], rg_b_i[l], 8)
            fm_load(c["rc1"], rg_lambda[l], 8)
            t0 = ctmp.v(0, 8)
            R.act(t0, c["rc1"].v(0, 8), AF.Exp, scale=-1.0)
            R.act(t0, t0, AF.Ln, bias=b_one())
            R.ts(c["rc1"].v(0, 8), t0, -8.0, None, ALU.mult)
            R.ts(c["rc2"].v(0, 8), t0, -16.0, None, ALU.mult)
            if l == 0:
                R.memset(c["lb"].v(0, 8), 0.0)
            else:
                t1 = ctmp.v(8, 16)
                t2 = ctmp.v(16, 24)
                fm_load_v = lambda dstv, src: R.dma("sp", dstv, DV(src.rearrange("(c p) -> p c", p=128)), slow=True)
                fm_load_v(t1, hg_lb_logits[0])
                fm_load_v(t2, hg_lb_logits[1])
                R.tt(t2, t2, t1, ALU.subtract)
                R.act(c["lb"].v(0, 8), t2, AF.Sigmoid)
            R.ts(c["oml"].v(0, 8), c["lb"].v(0, 8), -1.0, 1.0, ALU.mult, ALU.add)
            R.ts(c["noml"].v(0, 8), c["oml"].v(0, 8), -1.0, None, ALU.mult)
            bc_load(c["hnw"], hg_norm_w[l], 128)
            bc_load(c["dtb"], mb_dt_bias[l], 16)
            bc_load(c["dd"], mb_d[l], 16)
            bc_load(c["aneg"], mb_a_log[l], 16)
            R.act(c["aneg"].v(0, 16), c["aneg"].v(0, 16), AF.Exp)
            R.ts(c["aneg"].v(0, 16), c["aneg"].v(0, 16), -1.0, None, ALU.mult)
            for nm, wsrc in (("bdA", rg_w_a), ("bdI", rg_w_i)):
                R.memset(c[nm].v(0, 1024), 0.0, eng="pool")
                for par in range(2):
                    dst = c[nm].v(0, 1024, par * 64, par * 64 + 64).r("p (m j) -> p m j", j=128)[:, :, par * 64:par * 64 + 64]
                    src = wsrc[l].rearrange("(m two) i j -> two i m j", two=2)[par]
                    R.dma("pool", dst, DV(src))
            R.dma("pool", c["wdt"].v(0, 128).r("p (kc n) -> p kc n", n=16), DV(w_in[l, :, OFF_DT:OFF_DT + 16].rearrange("(kc p) n -> p kc n", p=128)))
        fm_load(nfin, norm_final, 8)

        def dense_fm(ps, wp, coff, src, N, nk=8, first=True, last=True, kbase=0):
            with R.atomic():
                for k in range(nk):
                    R.mm(ps, wp.t(k, coff, 128), src.v((kbase + k) * NMAX, (kbase + k) * NMAX + N),
                         start=(first and k == 0), stop=(last and k == nk - 1))

        def dense_tm(ps, wp, coff, n, src, tok0, P):
            with R.atomic():
                for k in range(8):
                    R.mm(ps, src.v(k * NMAX + tok0, k * NMAX + tok0 + P), wp.t(k, coff, n), start=(k == 0), stop=(k == 7))

        def rmsnorm_fm(nw, N, dst):
            ps = psA().v(0, N)
            for cch in range(8):
                sq = sh(N)
                R.act(sq, xT.v(cch * NMAX, cch * NMAX + N), AF.Square)
                R.mm(ps, ones_b(128, 128), sq, start=(cch == 0), stop=(cch == 7))
            rs = sf(N)
            R.act(rs, ps, AF.Sqrt, bias=b_eps(), scale=1.0 / D)
            R.recip(rs, rs)
            for cch in range(8):
                R.stt(dst.v(cch * NMAX, cch * NMAX + N), xT.v(cch * NMAX, cch * NMAX + N), nw.v(cch, cch + 1), rs, ALU.mult, ALU.mult)

        def conv_fm(cx, ps, seg, hist, ci, cw, cb, out_f32):
            npc, n = seg.npc, seg.n_piece
            W = 3 + n
            pcb = cx.PC.v((ci % 2) * 520, (ci % 2) * 520 + npc * W)
            pc3 = pcb.r("p (a w) -> p a w", w=W)
            R.copy(pc3[:, :, 3:W], ps.r("p (a n) -> p a n", n=n), eng="act")
            h3 = hist.v(ci * 6, ci * 6 + npc * 3).r("p (a k) -> p a k", k=3)
            R.copy(pc3[:, :, 0:3], h3)
            o3 = out_f32.r("p (a n) -> p a n", n=n)
            R.ts(o3, pc3[:, :, 0:n], cw.v(ci * 4, ci * 4 + 1), cb.v(ci, ci + 1), ALU.mult, ALU.add)
            for k in range(1, 4):
                R.stt(o3, pc3[:, :, k:k + n], cw.v(ci * 4 + k, ci * 4 + k + 1), o3, ALU.mult, ALU.add)
            R.copy(h3, pc3[:, :, n:n + 3])

        def layer_pass(seg, l):
            c = LC[l]
            N, P, q, NT = seg.N, seg.P, seg.q, seg.NT
            og = outs[seg.kind]
            if seg.first:
                for pc in range(seg.npc):
                    sq_ = seg.seqs[pc]
                    hm = histM[l].v(0, 96).r("p (c a k) -> p c a k", a=2, k=3)[:, :, pc, :]
                    hr = histR[l].v(0, 48).r("p (c a k) -> p c a k", a=2, k=3)[:, :, pc, :]
                    rh = rgH[l].v(0, 16).r("p (m a) -> p m a", a=2)[:, :, pc]
                    if seg.kind == "p":
                        R.memset(hm, 0.0)
                        R.memset(hr, 0.0)
                        R.memset(rh, 0.0)
                    else:
                        for k in range(3):
                            R.dma("sp", hm[:, :, k], DV(st_sc[l, sq_, k].rearrange("(c p) -> p c", p=128)), slow=True)
                            R.dma("sp", hr[:, :, k], DV(st_rc[l, sq_, k].rearrange("(c p) -> p c", p=128)), slow=True)
                        R.dma("sp", rh, DV(st_rg[l, sq_].rearrange("(m p) -> p m", p=128)), slow=True)
            bc_load(MNW, mb_norm_w[l], 1024)
            rmsnorm_fm(c["nmix"], N, hT)
            if stop <= 3:
                return

            def merge(k):
                for half in range(2):
                    wg = wl(win(l, OFF_MG + k * 1024 + half * 512, 512))
                    wb_ = wl((w_branch[l, k, :, half * 512:(half + 1) * 512], 8, None))
                    for j in range(4):
                        dc = half * 4 + j
                        psg = psA().v(0, N)
                        dense_fm(psg, wg, j * 128, hT, N)
                        sg = sf(N)
                        R.act(sg, psg, AF.Sigmoid)
                        psp = psA().v(0, N)
                        dense_fm(psp, wb_, j * 128, ykT, N)
                        md = mergedT.v(dc * NMAX, dc * NMAX + N)
                        if k == 0:
                            R.tt(md, psp, sg, ALU.mult)
                        else:
                            R.tt(sg, psp, sg, ALU.mult)
                            R.tt(md, md, sg, ALU.add)

            def smv(tt, o, n, p1=None):
                return SMG.v(tt * 64 + o, tt * 64 + o + n, 0, P if p1 is None else p1)
            for tt in (range(NT) if "B" in phases else ()):
                ps = psA().v(0, 16, 0, P)
                dense_tm(ps, WDT(c["wdt"]), 0, 16, hT, tt * P, P)
                dt_ = smv(tt, 0, 16)
                R.tt(dt_, ps, c["dtb"].v(0, 16, 0, P), ALU.add)
                R.act(dt_, dt_, AF.Exp)
                R.act(dt_, dt_, AF.Ln, bias=b_one(P))
                dtA = smv(tt, 16, 16)
                R.tt(dtA, dt_, c["aneg"].v(0, 16, 0, P), ALU.mult)
                ps2 = psA().v(0, 16, 0, P)
                R.mm(ps2, ctv("U2", q), dtA)
                acs = smv(tt, 32, 16)
                R.copy(acs, ps2)
                R.act(smv(tt, 48, 16), acs, AF.Exp)

            def mamba_group(cx, g):
                TM, TMH, SM = cx.TM, cx.TMH, cx.SM
                F0, F1, F2, PTb = cx.f[0], cx.f[1], cx.f[2], cx.t
                wa = wl(win(l, OFF_XS + g * 256, 256), win(l, OFF_B + g * 128, 128), win(l, OFF_C + g * 128, 128))
                wz = wl(win(l, OFF_Z + g * 256, 256))
                xa = []
                for j, ci in enumerate((2 * g, 2 * g + 1, 8 + g, 12 + g)):
                    ps = (F1 if j % 2 == 0 else F2).v(0, N)
                    dense_fm(ps, wa, j * 128, hT, N)
                    cf = cx.sfb(j % 2, N)
                    conv_fm(cx, ps, seg, histM[l], ci, c["mcw"], c["mcb"], cf)
                    xb_ = cx.shb(j, N)
                    R.act(xb_, cf, AF.Silu)
                    xa.append(xb_)
                BT, CTt = xa[2], xa[3]
                for tt in range(NT):
                    t0_ = tt * P
                    for j in range(3):
                        R.tr(PTb.v(j * 128, j * 128 + 128, 0, P), xa[j][:, t0_:t0_ + P], ident_b(128, 128))
                    xtm = TMH.v(0, 256, 0, P)
                    R.copy(TMH.v(0, 384, 0, P), PTb.v(0, 384, 0, P))
                    psz = F2.v(0, 256, 0, P)
                    dense_tm(psz, wz, 0, 256, hT, t0_, P)
                    zs = TM.v(0, 256, 0, P)
                    R.act(zs, psz, AF.Silu)
                    dt_g = smv(tt, 4 * g, 4)
                    dtA_g = smv(tt, 16 + 4 * g, 4)
                    acs_g = smv(tt, 32 + 4 * g, 4)
                    eacs_g = smv(tt, 48 + 4 * g, 4)
                    rhsU = TM.v(256, 256 + 4 * q, 0, P)
                    R.tt(rhsU.r("p (h i) -> p h i", i=q), ctv("Uq", q).us(1).bc([P, 4, q]), dtA_g.us(2).bc([P, 4, q]), ALU.mult)
                    pab = F1.v(0, 4 * q, 0, P)
                    R.mm(pab, ctv("BD1", q), rhsU)
                    segm = TM.v(512, 512 + 4 * q, 0, P)
                    seg3 = segm.r("p (h i) -> p h i", i=q)
                    R.tt(seg3, pab.r("p (h i) -> p h i", i=q), acs_g.us(2).bc([P, 4, q]), ALU.subtract)
                    wend = SM.v(0, 4, 0, P)
                    R.tt(wend, pab.r("p (h i) -> p h i", i=q)[:, :, q - 1], acs_g, ALU.subtract)
                    R.tt(seg3, seg3, ctv("neg", q).us(1).bc([P, 4, q]), ALU.add)
                    R.act(segm, segm, AF.Exp)
                    R.act(wend, wend, AF.Exp)
                    R.tt(wend, wend, dt_g, ALU.mult)
                    pcb = F2.v(256, 256 + q, 0, P)
                    for cc in range(2):
                        R.mm(F2.v(256, 256 + q, cc * q, cc * q + q), BT[:, t0_ + cc * q:t0_ + cc * q + q], CTt[:, t0_ + cc * q:t0_ + cc * q + q])
                    R.tt(seg3, seg3, pcb.us(1).bc([P, 4, q]), ALU.mult)
                    MT = TMH.v(512, 512 + 4 * q, 0, P)
                    R.tt(MT.r("p (h i) -> p h i", i=q), seg3, dt_g.us(2).bc([P, 4, q]), ALU.mult)
                    xw = TMH.v(768, 1024, 0, P)
                    R.tt(xw.r("p (h d) -> p h d", d=64), xtm.r("p (h d) -> p h d", d=64), wend.us(2).bc([P, 4, 64]), ALU.mult)
                    for cc in range(2):
                        for h in range(4):
                            R.mm(F0.v(h * 64, h * 64 + 64, cc * q, cc * q + q),
                                 TMH.v(512 + h * q, 512 + h * q + q, cc * q, cc * q + q),
                                 TMH.v(h * 64, h * 64 + 64, cc * q, cc * q + q))
                    sT = ssmT[l].v(g * 256, g * 256 + 256)
                    for cc in range(2):
                        sq_, isf, isl = seg.chunk_info(tt, cc)
                        if isf:
                            if seg.kind == "p":
                                R.memset(sT, 0.0)
                            else:
                                stg = cx.IOS.v(0, 256)
                                R.dma("sp", stg.r("p (a n) -> p a n", n=128), DV(st_ssm[l, sq_, 4 * g:4 * g + 4].rearrange("(a two) d n -> (two d) a n", two=2)))
                                for a in range(2):
                                    R.tr(F1.v(a * 128, a * 128 + 128), stg[:, a * 128:a * 128 + 128], ident_f(128, 128))
                                R.copy(sT, F1.v(0, 256))
                        sB = ssmB.v((2 * cx.i + cc) * 256, (2 * cx.i + cc) * 256 + 256)
                        R.copy(sB, sT, eng="act")
                        R.mm(F0.v(256, 512, cc * q, cc * q + q), CTt[:, t0_ + cc * q:t0_ + cc * q + q], sB)
                        pst = F1.v(0, 256)
                        R.mm(pst, TMH.v(256, 384, cc * q, cc * q + q), TMH.v(768, 1024, cc * q, cc * q + q))
                        pdc = F2.v(384, 388)
                        R.mm(pdc, ctv("sel%d" % cc, q), dtA_g)
                        dcy = SM.v(16, 20)
                        R.act(dcy, pdc, AF.Exp)
                        R.tt(sT.r("p (h d) -> p h d", d=64), sT.r("p (h d) -> p h d", d=64), dcy.us(2).bc([128, 4, 64]), ALU.mult)
                        R.tt(sT, sT, pst, ALU.add)
                        if isl:
                            for a in range(2):
                                R.tr(F1.v(a * 128, a * 128 + 128), sT[:, a * 128:a * 128 + 128], ident_f(128, 128))
                            stg = cx.IOS.v(256, 512)
                            R.copy(stg, F1.v(0, 256))
                            R.dma("sp", DV(og["ssm"][l, sq_, 4 * g:4 * g + 4].rearrange("(a two) d n -> (two d) a n", two=2)), stg.r("p (a n) -> p a n", n=128))
                    y1 = TM.v(768, 1024, 0, P)
                    R.tt(y1.r("p (h d) -> p h d", d=64), F0.v(256, 512, 0, P).r("p (h d) -> p h d", d=64), eacs_g.us(2).bc([P, 4, 64]), ALU.mult)
                    R.tt(y1, y1, F0.v(0, 256, 0, P), ALU.add)
                    y2 = TM.v(1024, 1280, 0, P)
                    R.tt(y2.r("p (h d) -> p h d", d=64), xtm.r("p (h d) -> p h d", d=64), c["dd"].v(4 * g, 4 * g + 4, 0, P).us(2).bc([P, 4, 64]), ALU.mult)
                    R.tt(y1, y1, y2, ALU.add)
                    R.tt(y1, y1, zs, ALU.mult)
                    ssq = SM.v(32, 33, 0, P)
                    R.act(y2, y1, AF.Square, accum=ssq)
                    R.act(ssq, ssq, AF.Sqrt, bias=b_eps(P), scale=1.0 / 256)
                    R.recip(ssq, ssq)
                    yn = TMH.v(1024, 1280, 0, P)
                    R.stt(yn, y1, ssq, MNW.v(g * 256, g * 256 + 256, 0, P), ALU.mult, ALU.mult)
                    for a in range(2):
                        R.tr(PTb.v(512 + a * 128, 512 + a * 128 + P), yn[:, a * 128:a * 128 + 128], ident_b(P, P))
                    for a in range(2):
                        R.copy(ykT.v((2 * g + a) * NMAX + t0_, (2 * g + a) * NMAX + t0_ + P), PTb.v(512 + a * 128, 512 + a * 128 + P), eng="act")

            if "B" in phases:
                run_streams([(lambda cx, g=g: mamba_group(cx, g)) for g in range(4)])
                merge(0)

            NCH = N // q

            def hgrn_head(cx, hh):
                TM, TMH, SM = cx.TM, cx.TMH, cx.SM
                F0, F1, F2, PTb = cx.f[0], cx.f[1], cx.f[2], cx.t
                wh = wl(win(l, OFF_HQ + hh * 128, 128), win(l, OFF_HF + hh * 128, 128), win(l, OFF_HI + hh * 128, 128), win(l, OFF_HG + hh * 128, 128))
                psq = F1.v(0, N)
                dense_fm(psq, wh, 0, hT, N)
                qs = cx.sfb(0, N)
                R.act(qs, psq, AF.Silu)
                psf = F2.v(0, N)
                dense_fm(psf, wh, 128, hT, N)
                sig = cx.sfb(1, N)
                R.act(sig, psf, AF.Sigmoid)
                gl = cx.sfb(2, N)
                R.act(gl, sig, AF.Ln, bias=c["lb"].v(hh, hh + 1), scale=c["oml"].v(hh, hh + 1))
                kk = cx.sfb(3, N)
                R.ts(kk, sig, c["noml"].v(hh, hh + 1), c["oml"].v(hh, hh + 1), ALU.mult, ALU.add)
                gxb = cx.PC.v(0, 1 + N)
                R.memset(gxb[:, 0:1], 0.0)
                R.scan(gxb[:, 1:1 + N], ones_f(128, N), gl, 0.0, ALU.mult, ALU.add)
                gx3 = gxb[:, 1:1 + N].r("p (c i) -> p c i", i=q)
                ref = gxb[:, q // 2:q // 2 + (NCH - 1) * q + 1:q] if NCH > 1 else gxb[:, q // 2:q // 2 + 1]
                beg = gxb[:, 0:(NCH - 1) * q + 1:q] if NCH > 1 else gxb[:, 0:1]
                end = gxb[:, q:q + (NCH - 1) * q + 1:q] if NCH > 1 else gxb[:, q:q + 1]
                A1 = cx.sfb(4, N)
                R.tt(A1.r("p (c i) -> p c i", i=q), gx3, ref.us(2).bc([128, NCH, q]), ALU.subtract)
                E1 = cx.sfb(1, N)
                R.act(E1, A1, AF.Exp)
                E2 = cx.sfb(2, N)
                R.act(E2, A1, AF.Exp, scale=-1.0)
                qe = cx.shb(0, N)
                R.tt(qe, qs, E1, ALU.mult)
                ke = cx.shb(1, N)
                R.tt(ke, kk, E2, ALU.mult)
                sm0 = SM.v(64, 64 + NCH)
                sm1 = SM.v(80, 80 + NCH)
                sm2 = SM.v(96, 96 + NCH)
                R.tt(sm0, ref, beg, ALU.subtract)
                R.act(sm0, sm0, AF.Exp)
                R.tt(sm1, end, ref, ALU.subtract)
                R.act(sm1, sm1, AF.Exp)
                R.tt(sm2, end, beg, ALU.subtract)
                R.act(sm2, sm2, AF.Exp)
                R.tt(E1.r("p (c i) -> p c i", i=q), E1.r("p (c i) -> p c i", i=q), sm0.us(2).bc([128, NCH, q]), ALU.mult)
                qg = cx.shb(2, N)
                R.tt(qg, qs, E1, ALU.mult)
                R.tt(E2.r("p (c i) -> p c i", i=q), E2.r("p (c i) -> p c i", i=q), sm1.us(2).bc([128, NCH, q]), ALU.mult)
                kl = cx.shb(3, N)
                R.tt(kl, kk, E2, ALU.mult)
                S_ = hgS[l].v(hh * 128, hh * 128 + 128)
                for tt in range(NT):
                    t0_ = tt * P
                    psv = F1.v(0, 128, 0, P)
                    dense_tm(psv, wh, 256, 128, hT, t0_, P)
                    vtm = TMH.v(1280, 1408, 0, P)
                    R.copy(vtm, psv, eng="act")
                    psg = F2.v(0, 128, 0, P)
                    dense_tm(psg, wh, 384, 128, hT, t0_, P)
                    gsl = TM.v(1280, 1408, 0, P)
                    R.act(gsl, psg, AF.Silu)
                    R.tr(PTb.v(0, 128, 0, P), kl[:, t0_:t0_ + P], ident_b(128, 128))
                    kltm = TMH.v(1536, 1664, 0, P)
                    R.copy(kltm, PTb.v(0, 128, 0, P))
                    for cc in range(2):
                        R.mm(F1.v(0, q, cc * q, cc * q + q), ke[:, t0_ + cc * q:t0_ + cc * q + q], qe[:, t0_ + cc * q:t0_ + cc * q + q])
                    attm = TMH.v(1792, 1792 + q, 0, P)
                    R.tt(attm, F1.v(0, q, 0, P), ctv("m01", q), ALU.mult)
                    for cc in range(2):
                        sq_, isf, isl = seg.chunk_info(tt, cc)
                        if isf:
                            if seg.kind == "p":
                                R.memset(S_, 0.0)
                            else:
                                R.dma("sp", S_, DV(st_hg[l, sq_, hh]))
                        sB = hgB.v((2 * cx.i + cc) * 128, (2 * cx.i + cc) * 128 + 128)
                        R.copy(sB, S_, eng="act")
                        po = F0.v(0, 128, cc * q, cc * q + q)
                        with R.atomic():
                            R.mm(po, TMH.v(1792, 1792 + q, cc * q, cc * q + q), TMH.v(1280, 1408, cc * q, cc * q + q), start=True, stop=False)
                            R.mm(po, qg[:, t0_ + cc * q:t0_ + cc * q + q], sB, start=False, stop=True)
                        pss = F2.v(0, 128)
                        R.mm(pss, TMH.v(1536, 1664, cc * q, cc * q + q), TMH.v(1280, 1408, cc * q, cc * q + q))
                        cg = tt * 2 + cc
                        R.stt(S_, S_, sm2[:, cg:cg + 1], pss, ALU.mult, ALU.add)
                        if isl:
                            R.dma("sp", DV(og["hg"][l, sq_, hh]), S_)
                    po_ = F0.v(0, 128, 0, P)
                    ssq = SM.v(40, 41, 0, P)
                    junk = TM.v(1536, 1664, 0, P)
                    R.act(junk, po_, AF.Square, accum=ssq)
                    R.act(ssq, ssq, AF.Sqrt, bias=b_eps(P), scale=1.0 / 128)
                    R.recip(ssq, ssq)
                    on = TM.v(1792, 1920, 0, P)
                    R.stt(on, po_, ssq, c["hnw"].v(0, 128, 0, P), ALU.mult, ALU.mult)
                    yh = TMH.v(1024, 1152, 0, P)
                    R.tt(yh, on, gsl, ALU.mult)
                    R.tr(PTb.v(512, 512 + P), yh, ident_b(P, P))
                    R.copy(ykT.v(hh * NMAX + t0_, hh * NMAX + t0_ + P), PTb.v(512, 512 + P), eng="act")

            if "C" in phases:
                run_streams([(lambda cx, hh=hh: hgrn_head(cx, hh)) for hh in range(8)])
                merge(1)

            def rg_tile(cx, wrg, wrx, j, m):
                F0, F1, F2 = cx.f[0], cx.f[1], cx.f[2]
                psx = F1.v(0, N)
                dense_fm(psx, wrx, j * 128, hT, N)
                xc = cx.sfb(0, N)
                conv_fm(cx, psx, seg, histR[l], m, c["rcw"], c["rcb"], xc)
                xcb = cx.shb(0, N)
                R.copy(xcb, xc, eng="act")
                psa = F2.v(0, N)
                R.mm(psa, c["bdA"].v(m * 128, m * 128 + 128), xcb)
                psi = F0.v(0, N)
                R.mm(psi, c["bdI"].v(m * 128, m * 128 + 128), xcb)
                rr = cx.sfb(1, N)
                R.act(rr, psa, AF.Sigmoid, bias=c["rba"].v(m, m + 1))
                ig = cx.sfb(2, N)
                R.act(ig, psi, AF.Sigmoid, bias=c["rbi"].v(m, m + 1))
                aa = cx.sfb(3, N)
                R.act(aa, rr, AF.Exp, scale=c["rc1"].v(m, m + 1))
                R.act(rr, rr, AF.Exp, scale=c["rc2"].v(m, m + 1))
                R.act(rr, rr, AF.Sqrt, bias=b_one(), scale=-1.0)
                R.tt(ig, ig, xc, ALU.mult)
                R.tt(ig, ig, rr, ALU.mult)
                hs = cx.sfb(4, N)
                for pc in range(seg.npc):
                    n = seg.n_piece
                    hst = rgH[l].v(m * 2 + pc, m * 2 + pc + 1)
                    R.scan(hs[:, pc * n:(pc + 1) * n], aa[:, pc * n:(pc + 1) * n], ig[:, pc * n:(pc + 1) * n], hst, ALU.mult, ALU.add)
                    R.copy(hst, hs[:, (pc + 1) * n - 1:(pc + 1) * n])
                psr = F1.v(0, N)
                dense_fm(psr, wrg, j * 128, hT, N)
                ge = cx.sfb(1, N)
                R.act(ge, psr, AF.Gelu_apprx_tanh)
                R.tt(ykT.v(m * NMAX, m * NMAX + N), ge, hs, ALU.mult)

            for half in (range(2) if "D" in phases else ()):
                wrg = wl(win(l, OFF_RGATE + half * 512, 512))
                wrx = wl(win(l, OFF_RX + half * 512, 512))
                run_streams([(lambda cx, j=j: rg_tile(cx, wrg, wrx, j, half * 4 + j)) for j in range(4)])
            if "D" in phases:
                merge(2)

            if seg.last:
                for pc in range(seg.npc):
                    sq_ = seg.seqs[pc]
                    hm = histM[l].v(0, 96).r("p (c a k) -> p c a k", a=2, k=3)[:, :, pc, :]
                    hr = histR[l].v(0, 48).r("p (c a k) -> p c a k", a=2, k=3)[:, :, pc, :]
                    rh = rgH[l].v(0, 16).r("p (m a) -> p m a", a=2)[:, :, pc]
                    for k in range(3):
                        R.dma("sp", DV(og["sc"][l, sq_, k].rearrange("(c p) -> p c", p=128)), hm[:, :, k], slow=True)
                        R.dma("sp", DV(og["rc"][l, sq_, k].rearrange("(c p) -> p c", p=128)), hr[:, :, k], slow=True)
                    R.dma("sp", DV(og["rg"][l, sq_].rearrange("(m p) -> p m", p=128)), rh, slow=True)

            for cch in range(8):
                R.copy(hT.v(cch * NMAX, cch * NMAX + N), mergedT.v(cch * NMAX, cch * NMAX + N), eng="act")
            for half in (range(2) if "E" in phases else ()):
                wo = wl((w_out[l, :, half * 512:(half + 1) * 512], 8, None))
                for j in range(4):
                    dc = half * 4 + j
                    ps = psA().v(0, N)
                    dense_fm(ps, wo, j * 128, hT, N)
                    xv = xT.v(dc * NMAX, dc * NMAX + N)
                    R.tt(xv, xv, ps, ALU.add)

            rmsnorm_fm(c["nffn"], N, hT)
            for fh in (range(2) if "F" in phases else ()):
                fb = fh * 11
                fc = 0
                while fc < 11:
                    nf = min(4, 11 - fc)
                    wg_ = wl((w_ffn_in[l, :, (fb + fc) * 128:(fb + fc + nf) * 128], 8, None))
                    wu_ = wl((w_ffn_in[l, :, FF + (fb + fc) * 128:FF + (fb + fc + nf) * 128], 8, None))
                    for j in range(nf):
                        psg = psA().v(0, N)
                        dense_fm(psg, wg_, j * 128, hT, N)
                        sg = sf(N)
                        R.act(sg, psg, AF.Silu)
                        psu = psA().v(0, N)
                        dense_fm(psu, wu_, j * 128, hT, N)
                        R.tt(actT.v((fc + j) * NMAX, (fc + j) * NMAX + N), sg, psu, ALU.mult)
                    fc += nf
                for half in range(2):
                    pss = [PALL[j].v(0, N) for j in range(4)]
                    f0 = 0
                    while f0 < 11:
                        nf = min(8, 11 - f0)
                        wo = wl((w_ffn_out[l, (fb + f0) * 128:(fb + f0 + nf) * 128, half * 512:(half + 1) * 512], nf, None))
                        for j in range(4):
                            dense_fm(pss[j], wo, j * 128, actT, N, nk=nf, first=(f0 == 0), last=(f0 + nf == 11), kbase=f0)
                        f0 += nf
                    for j in range(4):
                        dc = half * 4 + j
                        xv = xT.v(dc * NMAX, dc * NMAX + N)
                        R.tt(xv, xv, pss[j], ALU.add)

        def run_segment(seg):
            N, P, NT = seg.N, seg.P, seg.NT
            for tt in range(NT):
                pc = (tt * P) // seg.n_piece
                tk = seg.tok0[pc] + (tt * P) % seg.n_piece
                io = IO.v(0, 1024, 0, P)
                if seg.kind == "p":
                    R.dma("sp", io, DV(xp[seg.seqs[0], tk:tk + P, :]))
                else:
                    for a in range(seg.npc):
                        R.dma("sp", IO.v(0, 1024, a * 32, a * 32 + 32), DV(xs[seg.seqs[a], :, :]))
                for half in (range(2) if "x" not in phases else ()):
                    pf = psA()
                    for j in range(4):
                        cch = half * 4 + j
                        R.tr(pf.v(j * 128, j * 128 + P), io[:, cch * 128:cch * 128 + 128], ident_f(P, P))
                    for j in (range(4) if "y" not in phases else ()):
                        cch = half * 4 + j
                        R.copy(xT.v(cch * NMAX + tt * P, cch * NMAX + tt * P + P), pf.v(j * 128, j * 128 + P), eng=("act" if (j % 2 and "v" not in phases) else "dve"))
            if stop <= 2:
                return
            for l in range(nlayers):
                layer_pass(seg, l)
            if stop <= 3:
                return
            rmsnorm_fm(nfin, N, mergedT)
            for tt in range(NT):
                io = IO.v(0, 1024, 0, P)
                for half in range(2):
                    pf = psA()
                    for j in range(4):
                        cch = half * 4 + j
                        R.tr(pf.v(j * 128, j * 128 + 128, 0, P), mergedT.v(cch * NMAX + tt * P, cch * NMAX + tt * P + P), ident_f(128, 128))
                    R.copy(io[:, half * 512:half * 512 + 512], pf.v(0, 512, 0, P), eng=("act" if half else "dve"))
                if seg.kind == "p":
                    tk = seg.tok0[0] + tt * P
                    R.dma("sp", DV(yp[seg.seqs[0], tk:tk + P, :]), io)
                else:
                    for a in range(seg.npc):
                        R.dma("sp", DV(ys[seg.seqs[a], :, :]), IO.v(0, 1024, a * 32, a * 32 + 32))

        segs = []
        for s in range(nseq_p):
            for j in range(nseg_p):
                segs.append(Seg("p", [s], [j * 512], 512, 64, j == 0, j == nseg_p - 1))
        if with_sample:
            segs.append(Seg("s", [0, 1], [0, 0], 32, 32, True, True))
        for sg_ in (segs if stop >= 2 else ()):
            run_segment(sg_)
        if dbg_out is not None:
            pass
        R.finalize(nc, es)
    return nc, R


_WNAMES = ["norm_mix", "w_in", "mb_conv_w", "mb_conv_b", "mb_dt_bias", "mb_a_log", "mb_d", "mb_norm_w",
           "hg_lb_logits", "hg_norm_w", "rg_conv_w", "rg_conv_b", "rg_w_a", "rg_b_a", "rg_w_i", "rg_b_i",
           "rg_lambda", "w_branch", "w_out", "norm_ffn", "w_ffn_in", "w_ffn_out", "norm_final"]


def make_in_maps(inputs, ncores=8):
    f = lambda a: np.ascontiguousarray(np.asarray(a, dtype=np.float32))
    wts = {k: f(inputs[k]) for k in _WNAMES}
    maps = []
    for c in range(ncores):
        s = slice(2 * c, 2 * c + 2)
        m = dict(wts)
        m["xp"] = f(inputs["x_prompt"][s])
        m["xs"] = f(inputs["x_sample"][s])
        m["st_ssm"] = f(inputs["state_ssm"][:, s])
        m["st_sc"] = f(inputs["state_ssm_conv"][:, s])
        m["st_hg"] = f(inputs["state_hgrn"][:, s])
        m["st_rg"] = f(inputs["state_rglru"][:, s])
        m["st_rc"] = f(inputs["state_rglru_conv"][:, s])
        m["ctab"] = CTAB
        maps.append(m)
    return maps


def gather(results):
    cat = lambda k, ax: np.concatenate([np.asarray(r[k], dtype=np.float32) for r in results], axis=ax)
    out = [cat("yp", 0), cat("ys", 0)]
    for g in ("p", "s"):
        for nm in ("ssm", "sc", "hg", "rg", "rc"):
            out.append(cat("o_%s_%s" % (nm, g), 1))
    return tuple(out)


def kernel(**inputs):
    nc, _ = build()
    maps = make_in_maps(inputs)
    res = run_bass_kernel_spmd(nc, maps, core_ids=list(range(8)))
    return gather(res.results)
```

```python
import contextlib
import numpy as np
import concourse.bass as bass
import concourse.mybir as mybir
from concourse.bass_utils import run_bass_kernel_spmd

F32 = mybir.dt.float32
BF16 = mybir.dt.bfloat16
AF = mybir.ActivationFunctionType
ALU = mybir.AluOpType

D = 1024
SEQ = 2048
DEC_SEQ = 32
EPS = 1e-6
FF = 2816
INW = 12304
OFF_Z, OFF_XS, OFF_B, OFF_C, OFF_DT = 0, 1024, 2048, 2560, 3072
OFF_HQ, OFF_HF, OFF_HI, OFF_HG, OFF_RGATE, OFF_RX, OFF_MG = 3088, 4112, 5136, 6160, 7184, 8208, 9232
NEG = -30000.0


class V:
    __slots__ = ("ap", "keys")

    def __init__(self, ap, keys):
        self.ap = ap
        self.keys = keys

    def r(self, pat, **kw):
        return V(self.ap.rearrange(pat, **kw), self.keys)

    def __getitem__(self, idx):
        return V(self.ap[idx], self.keys)

    def bc(self, shape):
        return V(self.ap.to_broadcast(list(shape)), self.keys)

    def us(self, axis):
        return V(self.ap.unsqueeze(axis), self.keys)


class Buf:
    def __init__(self, name, t, free, leaf):
        self.name, self.t, self.free, self.leaf = name, t, free, leaf

    def v(self, f0, f1, p0=0, p1=128):
        assert 0 <= f0 < f1 <= self.free, (self.name, f0, f1, self.free)
        keys = tuple((self.name, j) for j in range(f0 // self.leaf, (f1 - 1) // self.leaf + 1))
        return V(self.t[p0:p1, f0:f1], keys)


class Op:
    __slots__ = ("eng", "fn", "reads", "writes", "dma", "deps", "needed", "sig", "waits", "grp")

    def __init__(self, eng, fn, reads, writes, dma):
        self.eng, self.fn, self.reads, self.writes, self.dma = eng, fn, reads, writes, dma
        self.deps = ()
        self.needed = False
        self.sig = None
        self.waits = ()


NS_DMA = 8


class Rec:
    def __init__(self):
        self.ops = []
        self.gctr = 0
        self.gcur = None

    @contextlib.contextmanager
    def atomic(self):
        if self.gcur is not None:
            yield
            return
        self.gctr += 1
        self.gcur = self.gctr
        try:
            yield
        finally:
            self.gcur = None

    def add(self, eng, fn, reads=(), writes=(), dma=False):
        rk = []
        for x in reads:
            if x is not None and not isinstance(x, (int, float)):
                rk.extend(x.keys)
        wk = []
        for x in writes:
            if x is not None:
                wk.extend(x.keys)
        op = Op(eng, fn, tuple(rk), tuple(wk), dma)
        if self.gcur is not None:
            op.grp = self.gcur
        else:
            self.gctr += 1
            op.grp = self.gctr
        self.ops.append(op)

    def mm(self, out, lhsT, rhs, start=True, stop=True):
        rd = [lhsT, rhs] + ([] if start else [out])
        self.add("pe", lambda e: e.matmul(out.ap, lhsT.ap, rhs.ap, start=start, stop=stop), rd, [out])

    def tr(self, out, in_, ident):
        if in_.ap.dtype == F32:
            self.add("pe", lambda e: e.matmul(out.ap, in_.ap, ident.ap, start=True, stop=True), [in_, ident], [out])
        else:
            self.add("pe", lambda e: e.transpose(out.ap, in_.ap, ident.ap), [in_, ident], [out])

    def act(self, out, in_, func, bias=None, scale=None, accum=None):
        kw = {}
        rd = [in_]
        if bias is not None:
            if isinstance(bias, V):
                kw["bias"] = bias.ap
                rd.append(bias)
            else:
                kw["bias"] = float(bias)
        if scale is not None:
            if isinstance(scale, V):
                kw["scale"] = scale.ap
                rd.append(scale)
            else:
                kw["scale"] = float(scale)
        wr = [out]
        if accum is not None:
            kw["accum_out"] = accum.ap
            wr.append(accum)
        self.add("act", lambda e: e.activation(out=out.ap, in_=in_.ap, func=func, **kw), rd, wr)

    def tt(self, out, in0, in1, op, eng="dve"):
        self.add(eng, lambda e: e.tensor_tensor(out=out.ap, in0=in0.ap, in1=in1.ap, op=op), [in0, in1], [out])

    def ts(self, out, in0, s1, s2, op0, op1=None, eng="dve"):
        rd = [in0]
        a1 = s1
        a2 = s2
        if isinstance(s1, V):
            rd.append(s1)
            a1 = s1.ap
        if isinstance(s2, V):
            rd.append(s2)
            a2 = s2.ap
        if op1 is None:
            self.add(eng, lambda e: e.tensor_scalar(out=out.ap, in0=in0.ap, scalar1=a1, scalar2=None, op0=op0), rd, [out])
        else:
            self.add(eng, lambda e: e.tensor_scalar(out=out.ap, in0=in0.ap, scalar1=a1, scalar2=a2, op0=op0, op1=op1), rd, [out])

    def stt(self, out, in0, scalar, in1, op0, op1):
        rd = [in0, in1]
        sc = scalar
        if isinstance(scalar, V):
            rd.append(scalar)
            sc = scalar.ap
        self.add("dve", lambda e: e.scalar_tensor_tensor(out=out.ap, in0=in0.ap, scalar=sc, in1=in1.ap, op0=op0, op1=op1), rd, [out])

    def scan(self, out, d0, d1, init, op0, op1):
        rd = [d0, d1]
        iv = init
        if isinstance(init, V):
            rd.append(init)
            iv = init.ap
        self.add("dve", lambda e: e.tensor_tensor_scan(out=out.ap, data0=d0.ap, data1=d1.ap, initial=iv, op0=op0, op1=op1), rd, [out])

    def copy(self, out, in_, eng="dve"):
        if eng == "act":
            self.add("act", lambda e: e.copy(out.ap, in_.ap), [in_], [out])
        else:
            self.add(eng, lambda e: e.tensor_copy(out=out.ap, in_=in_.ap), [in_], [out])

    def memset(self, out, val, eng="dve"):
        self.add(eng, lambda e: e.memset(out.ap, val), [], [out])

    def recip(self, out, in_):
        self.add("dve", lambda e: e.reciprocal(out.ap, in_.ap), [in_], [out])

    def dma(self, q, out, in_, slow=False):
        if slow:
            self.add(q, lambda e: e.dma_start(out=out.ap, in_=in_.ap, allow_slow_non_contiguous=True), [in_], [out], dma=True)
        else:
            self.add(q, lambda e: e.dma_start(out=out.ap, in_=in_.ap), [in_], [out], dma=True)

    def finalize(self, nc, es):
        ops = self.ops
        last_w = {}
        readers = {}
        for i, op in enumerate(ops):
            deps = set()
            for k in op.reads:
                w = last_w.get(k)
                if w is not None:
                    deps.add(w)
                if k[0][0] == "P" and k[0] != "PC":
                    for r_ in readers.get(k, ()):
                        if ops[r_].eng != op.eng:
                            deps.add(r_)
            for k in op.writes:
                w = last_w.get(k)
                if w is not None:
                    deps.add(w)
                rs = readers.get(k)
                if rs:
                    deps.update(rs)
            deps.discard(i)
            if op.eng == "pe":
                deps = {d for d in deps if ops[d].eng != "pe" or ops[d].dma}
            op.deps = sorted(deps)
            for d in op.deps:
                ops[d].needed = True
            for k in op.writes:
                last_w[k] = i
                readers[k] = []
            for k in op.reads:
                if k not in op.writes:
                    readers.setdefault(k, []).append(i)
        engs = ["sp", "act", "pe", "dve", "pool"]
        csem = {e: es.enter_context(nc.semaphore("c_" + e)) for e in engs}
        dsem = {e: [es.enter_context(nc.semaphore("d_%s%d" % (e, j))) for j in range(NS_DMA)] for e in ("sp", "pool")}
        ccount = {e: 0 for e in engs}
        dcount = {e: 0 for e in dsem}
        waited = {e: {} for e in engs}
        for op in ops:
            w = []
            wd = waited[op.eng]
            need = {}
            for d in op.deps:
                sem, val = ops[d].sig
                if need.get(id(sem), (None, 0))[1] < val:
                    need[id(sem)] = (sem, val)
            for sid, (sem, val) in need.items():
                if wd.get(sid, 0) < val:
                    wd[sid] = val
                    w.append((sem, val))
            if op.dma:
                n = dcount[op.eng]
                dcount[op.eng] = n + 1
                sem = dsem[op.eng][n % NS_DMA]
                rnd = n // NS_DMA + 1
                if rnd > 1 and wd.get(id(sem), 0) < 16 * (rnd - 1):
                    wd[id(sem)] = 16 * (rnd - 1)
                    w.append((sem, 16 * (rnd - 1)))
                op.sig = (sem, 16 * rnd)
            elif op.needed:
                ccount[op.eng] += 1
                op.sig = (csem[op.eng], ccount[op.eng])
            op.waits = w
        per = {e: [op for op in ops if op.eng == e] for e in engs}
        self.stats = {e: len(per[e]) for e in engs}

        def emit(e, name):
            for op in per[name]:
                for sem, val in op.waits:
                    e.wait_ge(sem, val)
                ins = op.fn(e)
                if op.dma:
                    ins.then_inc(op.sig[0], 16)
                elif op.sig is not None:
                    ins.then_inc(op.sig[0], 1)
            if name in dsem:
                n = dcount[name]
                for j in range(NS_DMA):
                    cnt = (n - j + NS_DMA - 1) // NS_DMA if n > j else 0
                    if cnt > 0:
                        e.wait_ge(dsem[name][j], 16 * cnt)

        with nc.Block() as block:
            @block.sync
            def _(e):
                emit(e, "sp")

            @block.scalar
            def _(e):
                emit(e, "act")

            @block.tensor
            def _(e):
                emit(e, "pe")

            @block.vector
            def _(e):
                emit(e, "dve")

            @block.gpsimd
            def _(e):
                emit(e, "pool")


def _tables(q):
    P = 2 * q
    t = np.arange(P)
    U2 = ((t[:, None] // q == t[None, :] // q) & (t[:, None] <= t[None, :])).astype(np.float32)
    Uq = ((t[:, None] % q) <= np.arange(q)[None, :]).astype(np.float32)
    BD1 = (t[:, None] // q == t[None, :] // q).astype(np.float32)
    m01 = (np.arange(q)[None, :] >= (t[:, None] % q)).astype(np.float32)
    neg = np.where(m01 > 0, 0.0, NEG).astype(np.float32)
    sel0 = np.repeat((t[:, None] // q == 0).astype(np.float32), 128, axis=1)
    sel1 = np.repeat((t[:, None] // q == 1).astype(np.float32), 128, axis=1)
    cols = [U2, Uq, BD1, neg, m01, sel0, sel1]
    out = np.zeros((128, sum(c.shape[1] for c in cols)), np.float32)
    o = 0
    offs = {}
    for nm, c in zip(["U2", "Uq", "BD1", "neg", "m01", "sel0", "sel1"], cols):
        out[:P, o:o + c.shape[1]] = c
        offs[nm] = (o, c.shape[1])
        o += c.shape[1]
    return out, offs


def make_ctab():
    ident = np.eye(128, dtype=np.float32)
    ones = np.ones((128, 512), np.float32)
    t64, o64 = _tables(64)
    t32, o32 = _tables(32)
    tab = np.concatenate([ident, ones, t64, t32], axis=1)
    base64 = 128 + 512
    base32 = base64 + t64.shape[1]
    offs = {"ident": (0, 128), "ones": (128, 512)}
    for k, (o, n) in o64.items():
        offs[(64, k)] = (base64 + o, n)
    for k, (o, n) in o32.items():
        offs[(32, k)] = (base32 + o, n)
    return np.ascontiguousarray(tab), offs


CTAB, COFF = make_ctab()
NCT = CTAB.shape[1]


class Seg:
    def __init__(self, kind, seqs, tok0, n_piece, q, first, last):
        self.kind = kind
        self.seqs = seqs
        self.tok0 = tok0
        self.n_piece = n_piece
        self.npc = len(seqs)
        self.N = n_piece * len(seqs)
        self.q = q
        self.P = 2 * q
        self.NT = self.N // self.P
        self.first = first
        self.last = last

    def chunk_info(self, tt, c):
        if self.kind == "p":
            return self.seqs[0], (self.first and tt == 0 and c == 0), (self.last and tt == self.NT - 1 and c == 1)
        return self.seqs[c], True, True


def build(nseg_p=4, nseq_p=2, with_sample=True, nlayers=2, dbg=None, phases="BCDEF", stop=99):
    nc = bass.Bass("TRN2", target_bir_lowering=False)
    R = Rec()
    es = contextlib.ExitStack()
    with es:
        def din(name, shape):
            return nc.dram_tensor(name, list(shape), F32, kind="ExternalInput").ap()

        def dout(name, shape):
            return nc.dram_tensor(name, list(shape), F32, kind="ExternalOutput").ap()

        def DV(ap):
            return V(ap, ())

        xp = din("xp", (2, SEQ, D))
        xs = din("xs", (2, DEC_SEQ, D))
        st_ssm = din("st_ssm", (2, 2, 16, 64, 128))
        st_sc = din("st_sc", (2, 2, 3, 2048))
        st_hg = din("st_hg", (2, 2, 8, 128, 128))
        st_rg = din("st_rg", (2, 2, 1024))
        st_rc = din("st_rc", (2, 2, 3, 1024))
        norm_mix = din("norm_mix", (2, D))
        w_in = din("w_in", (2, D, INW))
        mb_conv_w = din("mb_conv_w", (2, 4, 2048))
        mb_conv_b = din("mb_conv_b", (2, 2048))
        mb_dt_bias = din("mb_dt_bias", (2, 16))
        mb_a_log = din("mb_a_log", (2, 16))
        mb_d = din("mb_d", (2, 16))
        mb_norm_w = din("mb_norm_w", (2, 1024))
        hg_lb_logits = din("hg_lb_logits", (2, 1024))
        hg_norm_w = din("hg_norm_w", (2, 128))
        rg_conv_w = din("rg_conv_w", (2, 4, 1024))
        rg_conv_b = din("rg_conv_b", (2, 1024))
        rg_w_a = din("rg_w_a", (2, 16, 64, 64))
        rg_b_a = din("rg_b_a", (2, 1024))
        rg_w_i = din("rg_w_i", (2, 16, 64, 64))
        rg_b_i = din("rg_b_i", (2, 1024))
        rg_lambda = din("rg_lambda", (2, 1024))
        w_branch = din("w_branch", (2, 3, D, D))
        w_out = din("w_out", (2, D, D))
        norm_ffn = din("norm_ffn", (2, D))
        w_ffn_in = din("w_ffn_in", (2, D, 2 * FF))
        w_ffn_out = din("w_ffn_out", (2, FF, D))
        norm_final = din("norm_final", (D,))
        ctab = din("ctab", (128, NCT))

        yp = dout("yp", (2, SEQ, D))
        ys = dout("ys", (2, DEC_SEQ, D))
        outs = {}
        for g in ("p", "s"):
            outs[g] = dict(
                ssm=dout("o_ssm_" + g, (2, 2, 16, 64, 128)),
                sc=dout("o_sc_" + g, (2, 2, 3, 2048)),
                hg=dout("o_hg_" + g, (2, 2, 8, 128, 128)),
                rg=dout("o_rg_" + g, (2, 2, 1024)),
                rc=dout("o_rc_" + g, (2, 2, 3, 1024)),
            )
        dbg_out = None
        if dbg:
            dbg_out = dout("dbg", (128, dbg))

        def sb(name, free, dtype, leaf):
            t = es.enter_context(nc.sbuf_tensor(name, [128, free], dtype))
            return Buf(name, t, free, leaf)

        def psb(name, free, dtype, leaf):
            t = es.enter_context(nc.psum_tensor(name, [128, free], dtype))
            return Buf(name, t, free, leaf)

        NMAX = 512
        CT = sb("CT", NCT, F32, NCT)
        CTB = sb("CTB", 128 + 128, BF16, 256)
        xT = sb("xT", 8 * NMAX, F32, NMAX)
        hT = sb("hT", 8 * NMAX, BF16, NMAX)
        mergedT = sb("mergedT", 8 * NMAX, F32, NMAX)
        ykT = sb("ykT", 8 * NMAX, BF16, NMAX)
        NWB = 4
        WB = sb("WB", NWB * 4096, BF16, 4096)
        ssmT = [sb("ssmT%d" % l, 1024, F32, 256) for l in range(2)]
        hgS = [sb("hgS%d" % l, 1024, F32, 128) for l in range(2)]
        rgH = [sb("rgH%d" % l, 16, F32, 2) for l in range(2)]
        histM = [sb("histM%d" % l, 16 * 2 * 3, F32, 6) for l in range(2)]
        histR = [sb("histR%d" % l, 8 * 2 * 3, F32, 6) for l in range(2)]
        ssmB = sb("ssmB", 4 * 256, BF16, 256)
        hgB = sb("hgB", 4 * 128, BF16, 128)
        LC = []
        for l in range(2):
            LC.append(dict(
                nmix=sb("nmix%d" % l, 8, F32, 8), nffn=sb("nffn%d" % l, 8, F32, 8),
                mcw=sb("mcw%d" % l, 64, F32, 64), mcb=sb("mcb%d" % l, 16, F32, 16),
                rcw=sb("rcw%d" % l, 32, F32, 32), rcb=sb("rcb%d" % l, 8, F32, 8),
                rba=sb("rba%d" % l, 8, F32, 8), rbi=sb("rbi%d" % l, 8, F32, 8),
                rc1=sb("rc1%d" % l, 8, F32, 8), rc2=sb("rc2%d" % l, 8, F32, 8),
                lb=sb("lb%d" % l, 8, F32, 8), oml=sb("oml%d" % l, 8, F32, 8), noml=sb("noml%d" % l, 8, F32, 8),
                hnw=sb("hnw%d" % l, 128, F32, 128),
                dtb=sb("dtb%d" % l, 16, F32, 16), aneg=sb("aneg%d" % l, 16, F32, 16), dd=sb("dd%d" % l, 16, F32, 16),
                bdA=sb("bdA%d" % l, 1024, BF16, 128), bdI=sb("bdI%d" % l, 1024, BF16, 128),
                wdt=sb("wdt%d" % l, 128, BF16, 128),
            ))
        nfin = sb("nfin", 8, F32, 8)
        MNW = sb("MNW", 1024, F32, 1024)
        CBT = sb("CBT", 4, F32, 4)
        ctmp = sb("ctmp", 64, F32, 64)
        actT = sb("actT", 11 * NMAX, BF16, NMAX)
        IO = sb("IO", 1024, F32, 1024)
        SMG = sb("SMG", 256, F32, 64)

        class Ctx:
            def __init__(self, i):
                self.i = i
                self.NF, self.NH = 5, 4
                self.SF = sb("SF%d" % i, self.NF * NMAX, F32, NMAX)
                self.SH = sb("SH%d" % i, self.NH * NMAX, BF16, NMAX)
                self.PC = sb("PC%d" % i, 2 * 520, F32, 520)
                self.SM = sb("SM%d" % i, 256, F32, 16)
                self.TM = sb("TM%d" % i, 8 * 256, F32, 256)
                self.TMH = sb("TMH%d" % i, 8 * 256, BF16, 256)
                self.IOS = sb("IOS%d" % i, 512, F32, 256)
                self.f = [psb("PF%d_%d" % (i, j), 512, F32, 512) for j in range(3)]
                self.t = psb("PTT%d" % i, 1024, BF16, 1024)
                self.ops = []
                self.sfc = 0
                self.shc = 0

            def sfb(self, j, N):
                return self.SF.v(j * NMAX, j * NMAX + N)

            def shb(self, j, N):
                return self.SH.v(j * NMAX, j * NMAX + N)

            def sf(self, N):
                self.sfc = (self.sfc + 1) % self.NF
                return self.sfb(self.sfc, N)

            def sh(self, N):
                self.shc = (self.shc + 1) % self.NH
                return self.shb(self.shc, N)

        CX = [Ctx(0), Ctx(1)]
        PALL = [CX[0].f[0], CX[0].f[1], CX[0].f[2], CX[1].f[0], CX[1].f[1], CX[1].f[2]]

        st = dict(pa=0, wb=0)

        def psA():
            st["pa"] = (st["pa"] + 1) % 6
            return PALL[st["pa"]]

        def sf(N):
            return CX[0].sf(N)

        def sh(N):
            return CX[0].sh(N)

        def run_streams(bodies):
            k = 0
            while k < len(bodies):
                pair = bodies[k:k + 2]
                lists = []
                for j, body in enumerate(pair):
                    main = R.ops
                    R.ops = []
                    body(CX[j])
                    lists.append(R.ops)
                    R.ops = main
                glists = []
                for x in lists:
                    gl_ = []
                    for op in x:
                        if gl_ and gl_[-1][0].grp == op.grp:
                            gl_[-1].append(op)
                        else:
                            gl_.append([op])
                    glists.append(gl_)
                m = max(len(x) for x in glists)
                for idx in range(m):
                    for x in glists:
                        if idx < len(x):
                            R.ops.extend(x[idx])
                k += 2

        ident_f = lambda p, n: CT.v(0, n, 0, p)
        ones_f = lambda p, n: CT.v(128, 128 + n, 0, p)
        ident_b = lambda p, n: CTB.v(0, n, 0, p)
        ones_b = lambda p, n: CTB.v(128, 128 + n, 0, p)

        def ctv(name, q, p0=0, p1=None):
            o, n = COFF[(q, name)]
            return CT.v(o, o + n, p0, 2 * q if p1 is None else p1)

        class WP:
            def __init__(self, b):
                self.b = b

            def t(self, kc, c0, n):
                return WB.v(self.b * 4096 + kc * 512 + c0, self.b * 4096 + kc * 512 + c0 + n)

        def wload(parts):
            st["wb"] = (st["wb"] + 1) % NWB
            b = st["wb"]
            for src, nk, dc in parts:
                n = src.shape[1]
                dst = WB.v(b * 4096, (b + 1) * 4096).r("p (kc n) -> p kc n", n=512)[:, 0:nk, dc:dc + n]
                R.dma("pool", dst, DV(src.rearrange("(kc p) n -> p kc n", p=128)))
            return WP(b)

        class WDT:
            def __init__(self, b):
                self.b = b

            def t(self, kc, c0, n):
                return self.b.v(kc * 16 + c0, kc * 16 + c0 + n)

        def win(l, c0, n):
            return (w_in[l, :, c0:c0 + n], 8, None)

        def wl(*parts):
            pl = []
            dc = 0
            for src, nk, _ in parts:
                pl.append((src, nk, dc))
                dc += src.shape[1]
            assert dc <= 512
            return wload(pl)

        R.memset(CBT.v(0, 1), 1.0)
        R.memset(CBT.v(1, 2), EPS)
        b_one = lambda p=128: CBT.v(0, 1, 0, p)
        b_eps = lambda p=128: CBT.v(1, 2, 0, p)
        R.dma("sp", CT.v(0, NCT), DV(ctab))
        R.dma("pool", CTB.v(0, 128), DV(ctab[:, 0:128]))
        R.dma("pool", CTB.v(128, 256), DV(ctab[:, 128:256]))

        def fm_load(dst, src1d, ncol):
            R.dma("sp", dst.v(0, ncol), DV(src1d.rearrange("(c p) -> p c", p=128)), slow=True)

        def bc_load(dst, src1d, n):
            R.dma("sp", dst.v(0, n), DV(src1d.partition_broadcast(128)), slow=True)

        for l in (range(nlayers) if stop >= 1 else ()):
            c = LC[l]
            fm_load(c["nmix"], norm_mix[l], 8)
            fm_load(c["nffn"], norm_ffn[l], 8)
            for k in range(4):
                R.dma("sp", c["mcw"].v(0, 64).r("p (c k) -> p c k", k=4)[:, :, k], DV(mb_conv_w[l, k].rearrange("(c p) -> p c", p=128)), slow=True)
                R.dma("sp", c["rcw"].v(0, 32).r("p (c k) -> p c k", k=4)[:, :, k], DV(rg_conv_w[l, k].rearrange("(c p) -> p c", p=128)), slow=True)
            fm_load(c["mcb"], mb_conv_b[l], 16)
            fm_load(c["rcb"], rg_conv_b[l], 8)
            fm_load(c["rba"], rg_b_a[l], 8)
            fm_load(c["rbi"], rg_b_i[l], 8)
            fm_load(c["rc1"], rg_lambda[l], 8)
            t0 = ctmp.v(0, 8)
            R.act(t0, c["rc1"].v(0, 8), AF.Exp, scale=-1.0)
            R.act(t0, t0, AF.Ln, bias=b_one())
            R.ts(c["rc1"].v(0, 8), t0, -8.0, None, ALU.mult)
            R.ts(c["rc2"].v(0, 8), t0, -16.0, None, ALU.mult)
            if l == 0:
                R.memset(c["lb"].v(0, 8), 0.0)
            else:
                t1 = ctmp.v(8, 16)
                t2 = ctmp.v(16, 24)
                fm_load_v = lambda dstv, src: R.dma("sp", dstv, DV(src.rearrange("(c p) -> p c", p=128)), slow=True)
                fm_load_v(t1, hg_lb_logits[0])
                fm_load_v(t2, hg_lb_logits[1])
                R.tt(t2, t2, t1, ALU.subtract)
                R.act(c["lb"].v(0, 8), t2, AF.Sigmoid)
            R.ts(c["oml"].v(0, 8), c["lb"].v(0, 8), -1.0, 1.0, ALU.mult, ALU.add)
            R.ts(c["noml"].v(0, 8), c["oml"].v(0, 8), -1.0, None, ALU.mult)
            bc_load(c["hnw"], hg_norm_w[l], 128)
            bc_load(c["dtb"], mb_dt_bias[l], 16)
            bc_load(c["dd"], mb_d[l], 16)
            bc_load(c["aneg"], mb_a_log[l], 16)
            R.act(c["aneg"].v(0, 16), c["aneg"].v(0, 16), AF.Exp)
            R.ts(c["aneg"].v(0, 16), c["aneg"].v(0, 16), -1.0, None, ALU.mult)
            for nm, wsrc in (("bdA", rg_w_a), ("bdI", rg_w_i)):
                R.memset(c[nm].v(0, 1024), 0.0, eng="pool")
                for par in range(2):
                    dst = c[nm].v(0, 1024, par * 64, par * 64 + 64).r("p (m j) -> p m j", j=128)[:, :, par * 64:par * 64 + 64]
                    src = wsrc[l].rearrange("(m two) i j -> two i m j", two=2)[par]
                    R.dma("pool", dst, DV(src))
            R.dma("pool", c["wdt"].v(0, 128).r("p (kc n) -> p kc n", n=16), DV(w_in[l, :, OFF_DT:OFF_DT + 16].rearrange("(kc p) n -> p kc n", p=128)))
        fm_load(nfin, norm_final, 8)

        def dense_fm(ps, wp, coff, src, N, nk=8, first=True, last=True, kbase=0):
            with R.atomic():
                for k in range(nk):
                    R.mm(ps, wp.t(k, coff, 128), src.v((kbase + k) * NMAX, (kbase + k) * NMAX + N),
                         start=(first and k == 0), stop=(last and k == nk - 1))

        def dense_tm(ps, wp, coff, n, src, tok0, P):
            with R.atomic():
                for k in range(8):
                    R.mm(ps, src.v(k * NMAX + tok0, k * NMAX + tok0 + P), wp.t(k, coff, n), start=(k == 0), stop=(k == 7))

        def rmsnorm_fm(nw, N, dst):
            ps = psA().v(0, N)
            for cch in range(8):
                sq = sh(N)
                R.act(sq, xT.v(cch * NMAX, cch * NMAX + N), AF.Square)
                R.mm(ps, ones_b(128, 128), sq, start=(cch == 0), stop=(cch == 7))
            rs = sf(N)
            R.act(rs, ps, AF.Sqrt, bias=b_eps(), scale=1.0 / D)
            R.recip(rs, rs)
            for cch in range(8):
                R.stt(dst.v(cch * NMAX, cch * NMAX + N), xT.v(cch * NMAX, cch * NMAX + N), nw.v(cch, cch + 1), rs, ALU.mult, ALU.mult)

        def conv_fm(cx, ps, seg, hist, ci, cw, cb, out_f32):
            npc, n = seg.npc, seg.n_piece
            W = 3 + n
            pcb = cx.PC.v((ci % 2) * 520, (ci % 2) * 520 + npc * W)
            pc3 = pcb.r("p (a w) -> p a w", w=W)
            R.copy(pc3[:, :, 3:W], ps.r("p (a n) -> p a n", n=n), eng="act")
            h3 = hist.v(ci * 6, ci * 6 + npc * 3).r("p (a k) -> p a k", k=3)
            R.copy(pc3[:, :, 0:3], h3)
            o3 = out_f32.r("p (a n) -> p a n", n=n)
            R.ts(o3, pc3[:, :, 0:n], cw.v(ci * 4, ci * 4 + 1), cb.v(ci, ci + 1), ALU.mult, ALU.add)
            for k in range(1, 4):
                R.stt(o3, pc3[:, :, k:k + n], cw.v(ci * 4 + k, ci * 4 + k + 1), o3, ALU.mult, ALU.add)
            R.copy(h3, pc3[:, :, n:n + 3])

        def layer_pass(seg, l):
            c = LC[l]
            N, P, q, NT = seg.N, seg.P, seg.q, seg.NT
            og = outs[seg.kind]
            if seg.first:
                for pc in range(seg.npc):
                    sq_ = seg.seqs[pc]
                    hm = histM[l].v(0, 96).r("p (c a k) -> p c a k", a=2, k=3)[:, :, pc, :]
                    hr = histR[l].v(0, 48).r("p (c a k) -> p c a k", a=2, k=3)[:, :, pc, :]
                    rh = rgH[l].v(0, 16).r("p (m a) -> p m a", a=2)[:, :, pc]
                    if seg.kind == "p":
                        R.memset(hm, 0.0)
                        R.memset(hr, 0.0)
                        R.memset(rh, 0.0)
                    else:
                        for k in range(3):
                            R.dma("sp", hm[:, :, k], DV(st_sc[l, sq_, k].rearrange("(c p) -> p c", p=128)), slow=True)
                            R.dma("sp", hr[:, :, k], DV(st_rc[l, sq_, k].rearrange("(c p) -> p c", p=128)), slow=True)
                        R.dma("sp", rh, DV(st_rg[l, sq_].rearrange("(m p) -> p m", p=128)), slow=True)
            bc_load(MNW, mb_norm_w[l], 1024)
            rmsnorm_fm(c["nmix"], N, hT)
            if stop <= 3:
                return

            def merge(k):
                for half in range(2):
                    wg = wl(win(l, OFF_MG + k * 1024 + half * 512, 512))
                    wb_ = wl((w_branch[l, k, :, half * 512:(half + 1) * 512], 8, None))
                    for j in range(4):
                        dc = half * 4 + j
                        psg = psA().v(0, N)
                        dense_fm(psg, wg, j * 128, hT, N)
                        sg = sf(N)
                        R.act(sg, psg, AF.Sigmoid)
                        psp = psA().v(0, N)
                        dense_fm(psp, wb_, j * 128, ykT, N)
                        md = mergedT.v(dc * NMAX, dc * NMAX + N)
                        if k == 0:
                            R.tt(md, psp, sg, ALU.mult)
                        else:
                            R.tt(sg, psp, sg, ALU.mult)
                            R.tt(md, md, sg, ALU.add)

            def smv(tt, o, n, p1=None):
                return SMG.v(tt * 64 + o, tt * 64 + o + n, 0, P if p1 is None else p1)
            for tt in (range(NT) if "B" in phases else ()):
                ps = psA().v(0, 16, 0, P)
                dense_tm(ps, WDT(c["wdt"]), 0, 16, hT, tt * P, P)
                dt_ = smv(tt, 0, 16)
                R.tt(dt_, ps, c["dtb"].v(0, 16, 0, P), ALU.add)
                R.act(dt_, dt_, AF.Exp)
                R.act(dt_, dt_, AF.Ln, bias=b_one(P))
                dtA = smv(tt, 16, 16)
                R.tt(dtA, dt_, c["aneg"].v(0, 16, 0, P), ALU.mult)
                ps2 = psA().v(0, 16, 0, P)
                R.mm(ps2, ctv("U2", q), dtA)
                acs = smv(tt, 32, 16)
                R.copy(acs, ps2)
                R.act(smv(tt, 48, 16), acs, AF.Exp)

            def mamba_group(cx, g):
                TM, TMH, SM = cx.TM, cx.TMH, cx.SM
                F0, F1, F2, PTb = cx.f[0], cx.f[1], cx.f[2], cx.t
                wa = wl(win(l, OFF_XS + g * 256, 256), win(l, OFF_B + g * 128, 128), win(l, OFF_C + g * 128, 128))
                wz = wl(win(l, OFF_Z + g * 256, 256))
                xa = []
                for j, ci in enumerate((2 * g, 2 * g + 1, 8 + g, 12 + g)):
                    ps = (F1 if j % 2 == 0 else F2).v(0, N)
                    dense_fm(ps, wa, j * 128, hT, N)
                    cf = cx.sfb(j % 2, N)
                    conv_fm(cx, ps, seg, histM[l], ci, c["mcw"], c["mcb"], cf)
                    xb_ = cx.shb(j, N)
                    R.act(xb_, cf, AF.Silu)
                    xa.append(xb_)
                BT, CTt = xa[2], xa[3]
                for tt in range(NT):
                    t0_ = tt * P
                    for j in range(3):
                        R.tr(PTb.v(j * 128, j * 128 + 128, 0, P), xa[j][:, t0_:t0_ + P], ident_b(128, 128))
                    xtm = TMH.v(0, 256, 0, P)
                    R.copy(TMH.v(0, 384, 0, P), PTb.v(0, 384, 0, P))
                    psz = F2.v(0, 256, 0, P)
                    dense_tm(psz, wz, 0, 256, hT, t0_, P)
                    zs = TM.v(0, 256, 0, P)
                    R.act(zs, psz, AF.Silu)
                    dt_g = smv(tt, 4 * g, 4)
                    dtA_g = smv(tt, 16 + 4 * g, 4)
                    acs_g = smv(tt, 32 + 4 * g, 4)
                    eacs_g = smv(tt, 48 + 4 * g, 4)
                    rhsU = TM.v(256, 256 + 4 * q, 0, P)
                    R.tt(rhsU.r("p (h i) -> p h i", i=q), ctv("Uq", q).us(1).bc([P, 4, q]), dtA_g.us(2).bc([P, 4, q]), ALU.mult)
                    pab = F1.v(0, 4 * q, 0, P)
                    R.mm(pab, ctv("BD1", q), rhsU)
                    segm = TM.v(512, 512 + 4 * q, 0, P)
                    seg3 = segm.r("p (h i) -> p h i", i=q)
                    R.tt(seg3, pab.r("p (h i) -> p h i", i=q), acs_g.us(2).bc([P, 4, q]), ALU.subtract)
                    wend = SM.v(0, 4, 0, P)
                    R.tt(wend, pab.r("p (h i) -> p h i", i=q)[:, :, q - 1], acs_g, ALU.subtract)
                    R.tt(seg3, seg3, ctv("neg", q).us(1).bc([P, 4, q]), ALU.add)
                    R.act(segm, segm, AF.Exp)
                    R.act(wend, wend, AF.Exp)
                    R.tt(wend, wend, dt_g, ALU.mult)
                    pcb = F2.v(256, 256 + q, 0, P)
                    for cc in range(2):
                        R.mm(F2.v(256, 256 + q, cc * q, cc * q + q), BT[:, t0_ + cc * q:t0_ + cc * q + q], CTt[:, t0_ + cc * q:t0_ + cc * q + q])
                    R.tt(seg3, seg3, pcb.us(1).bc([P, 4, q]), ALU.mult)
                    MT = TMH.v(512, 512 + 4 * q, 0, P)
                    R.tt(MT.r("p (h i) -> p h i", i=q), seg3, dt_g.us(2).bc([P, 4, q]), ALU.mult)
                    xw = TMH.v(768, 1024, 0, P)
                    R.tt(xw.r("p (h d) -> p h d", d=64), xtm.r("p (h d) -> p h d", d=64), wend.us(2).bc([P, 4, 64]), ALU.mult)
                    for cc in range(2):
                        for h in range(4):
                            R.mm(F0.v(h * 64, h * 64 + 64, cc * q, cc * q + q),
                                 TMH.v(512 + h * q, 512 + h * q + q, cc * q, cc * q + q),
                                 TMH.v(h * 64, h * 64 + 64, cc * q, cc * q + q))
                    sT = ssmT[l].v(g * 256, g * 256 + 256)
                    for cc in range(2):
                        sq_, isf, isl = seg.chunk_info(tt, cc)
                        if isf:
                            if seg.kind == "p":
                                R.memset(sT, 0.0)
                            else:
                                stg = cx.IOS.v(0, 256)
                                R.dma("sp", stg.r("p (a n) -> p a n", n=128), DV(st_ssm[l, sq_, 4 * g:4 * g + 4].rearrange("(a two) d n -> (two d) a n", two=2)))
                                for a in range(2):
                                    R.tr(F1.v(a * 128, a * 128 + 128), stg[:, a * 128:a * 128 + 128], ident_f(128, 128))
                                R.copy(sT, F1.v(0, 256))
                        sB = ssmB.v((2 * cx.i + cc) * 256, (2 * cx.i + cc) * 256 + 256)
                        R.copy(sB, sT, eng="act")
                        R.mm(F0.v(256, 512, cc * q, cc * q + q), CTt[:, t0_ + cc * q:t0_ + cc * q + q], sB)
                        pst = F1.v(0, 256)
                        R.mm(pst, TMH.v(256, 384, cc * q, cc * q + q), TMH.v(768, 1024, cc * q, cc * q + q))
                        pdc = F2.v(384, 388)
                        R.mm(pdc, ctv("sel%d" % cc, q), dtA_g)
                        dcy = SM.v(16, 20)
                        R.act(dcy, pdc, AF.Exp)
                        R.tt(sT.r("p (h d) -> p h d", d=64), sT.r("p (h d) -> p h d", d=64), dcy.us(2).bc([128, 4, 64]), ALU.mult)
                        R.tt(sT, sT, pst, ALU.add)
                        if isl:
                            for a in range(2):
                                R.tr(F1.v(a * 128, a * 128 + 128), sT[:, a * 128:a * 128 + 128], ident_f(128, 128))
                            stg = cx.IOS.v(256, 512)
                            R.copy(stg, F1.v(0, 256))
                            R.dma("sp", DV(og["ssm"][l, sq_, 4 * g:4 * g + 4].rearrange("(a two) d n -> (two d) a n", two=2)), stg.r("p (a n) -> p a n", n=128))
                    y1 = TM.v(768, 1024, 0, P)
                    R.tt(y1.r("p (h d) -> p h d", d=64), F0.v(256, 512, 0, P).r("p (h d) -> p h d", d=64), eacs_g.us(2).bc([P, 4, 64]), ALU.mult)
                    R.tt(y1, y1, F0.v(0, 256, 0, P), ALU.add)
                    y2 = TM.v(1024, 1280, 0, P)
                    R.tt(y2.r("p (h d) -> p h d", d=64), xtm.r("p (h d) -> p h d", d=64), c["dd"].v(4 * g, 4 * g + 4, 0, P).us(2).bc([P, 4, 64]), ALU.mult)
                    R.tt(y1, y1, y2, ALU.add)
                    R.tt(y1, y1, zs, ALU.mult)
                    ssq = SM.v(32, 33, 0, P)
                    R.act(y2, y1, AF.Square, accum=ssq)
                    R.act(ssq, ssq, AF.Sqrt, bias=b_eps(P), scale=1.0 / 256)
                    R.recip(ssq, ssq)
                    yn = TMH.v(1024, 1280, 0, P)
                    R.stt(yn, y1, ssq, MNW.v(g * 256, g * 256 + 256, 0, P), ALU.mult, ALU.mult)
                    for a in range(2):
                        R.tr(PTb.v(512 + a * 128, 512 + a * 128 + P), yn[:, a * 128:a * 128 + 128], ident_b(P, P))
                    for a in range(2):
                        R.copy(ykT.v((2 * g + a) * NMAX + t0_, (2 * g + a) * NMAX + t0_ + P), PTb.v(512 + a * 128, 512 + a * 128 + P), eng="act")

            if "B" in phases:
                run_streams([(lambda cx, g=g: mamba_group(cx, g)) for g in range(4)])
                merge(0)

            NCH = N // q

            def hgrn_head(cx, hh):
                TM, TMH, SM = cx.TM, cx.TMH, cx.SM
                F0, F1, F2, PTb = cx.f[0], cx.f[1], cx.f[2], cx.t
                wh = wl(win(l, OFF_HQ + hh * 128, 128), win(l, OFF_HF + hh * 128, 128), win(l, OFF_HI + hh * 128, 128), win(l, OFF_HG + hh * 128, 128))
                psq = F1.v(0, N)
                dense_fm(psq, wh, 0, hT, N)
                qs = cx.sfb(0, N)
                R.act(qs, psq, AF.Silu)
                psf = F2.v(0, N)
                dense_fm(psf, wh, 128, hT, N)
                sig = cx.sfb(1, N)
                R.act(sig, psf, AF.Sigmoid)
                gl = cx.sfb(2, N)
                R.act(gl, sig, AF.Ln, bias=c["lb"].v(hh, hh + 1), scale=c["oml"].v(hh, hh + 1))
                kk = cx.sfb(3, N)
                R.ts(kk, sig, c["noml"].v(hh, hh + 1), c["oml"].v(hh, hh + 1), ALU.mult, ALU.add)
                gxb = cx.PC.v(0, 1 + N)
                R.memset(gxb[:, 0:1], 0.0)
                R.scan(gxb[:, 1:1 + N], ones_f(128, N), gl, 0.0, ALU.mult, ALU.add)
                gx3 = gxb[:, 1:1 + N].r("p (c i) -> p c i", i=q)
                ref = gxb[:, q // 2:q // 2 + (NCH - 1) * q + 1:q] if NCH > 1 else gxb[:, q // 2:q // 2 + 1]
                beg = gxb[:, 0:(NCH - 1) * q + 1:q] if NCH > 1 else gxb[:, 0:1]
                end = gxb[:, q:q + (NCH - 1) * q + 1:q] if NCH > 1 else gxb[:, q:q + 1]
                A1 = cx.sfb(4, N)
                R.tt(A1.r("p (c i) -> p c i", i=q), gx3, ref.us(2).bc([128, NCH, q]), ALU.subtract)
                E1 = cx.sfb(1, N)
                R.act(E1, A1, AF.Exp)
                E2 = cx.sfb(2, N)
                R.act(E2, A1, AF.Exp, scale=-1.0)
                qe = cx.shb(0, N)
                R.tt(qe, qs, E1, ALU.mult)
                ke = cx.shb(1, N)
                R.tt(ke, kk, E2, ALU.mult)
                sm0 = SM.v(64, 64 + NCH)
                sm1 = SM.v(80, 80 + NCH)
                sm2 = SM.v(96, 96 + NCH)
                R.tt(sm0, ref, beg, ALU.subtract)
                R.act(sm0, sm0, AF.Exp)
                R.tt(sm1, end, ref, ALU.subtract)
                R.act(sm1, sm1, AF.Exp)
                R.tt(sm2, end, beg, ALU.subtract)
                R.act(sm2, sm2, AF.Exp)
                R.tt(E1.r("p (c i) -> p c i", i=q), E1.r("p (c i) -> p c i", i=q), sm0.us(2).bc([128, NCH, q]), ALU.mult)
                qg = cx.shb(2, N)
                R.tt(qg, qs, E1, ALU.mult)
                R.tt(E2.r("p (c i) -> p c i", i=q), E2.r("p (c i) -> p c i", i=q), sm1.us(2).bc([128, NCH, q]), ALU.mult)
                kl = cx.shb(3, N)
                R.tt(kl, kk, E2, ALU.mult)
                S_ = hgS[l].v(hh * 128, hh * 128 + 128)
                for tt in range(NT):
                    psg = (F2 if tt % 2 == 0 else F1).v(0, 128, 0, P)
                    dense_tm(psg, wh, 384, 128, hT, tt * P, P)
                    R.act(TM.v(tt * 128, tt * 128 + 128, 0, P), psg, AF.Silu)
                for tt in range(NT):
                    psv = (F2 if tt % 2 == 0 else F1).v(0, 128, 0, P)
                    dense_tm(psv, wh, 256, 128, hT, tt * P, P)
                    R.copy(TMH.v(tt * 128, tt * 128 + 128, 0, P), psv, eng="act")
                for tt in range(NT):
                    t0_ = tt * P
                    vtm = TMH.v(tt * 128, tt * 128 + 128, 0, P)
                    gsl = TM.v(tt * 128, tt * 128 + 128, 0, P)
                    R.tr(PTb.v(0, 128, 0, P), kl[:, t0_:t0_ + P], ident_b(128, 128))
                    kltm = TMH.v(1536, 1664, 0, P)
                    R.copy(kltm, PTb.v(0, 128, 0, P))
                    for cc in range(2):
                        R.mm(F1.v(0, q, cc * q, cc * q + q), ke[:, t0_ + cc * q:t0_ + cc * q + q], qe[:, t0_ + cc * q:t0_ + cc * q + q])
                    attm = TMH.v(1792, 1792 + q, 0, P)
                    R.tt(attm, F1.v(0, q, 0, P), ctv("m01", q), ALU.mult)
                    for cc in range(2):
                        sq_, isf, isl = seg.chunk_info(tt, cc)
                        if isf:
                            if seg.kind == "p":
                                R.memset(S_, 0.0)
                            else:
                                R.dma("sp", S_, DV(st_hg[l, sq_, hh]))
                        sB = hgB.v((2 * cx.i + cc) * 128, (2 * cx.i + cc) * 128 + 128)
                        R.copy(sB, S_, eng="act")
                        po = F0.v(0, 128, cc * q, cc * q + q)
                        with R.atomic():
                            R.mm(po, TMH.v(1792, 1792 + q, cc * q, cc * q + q), TMH.v(tt * 128, tt * 128 + 128, cc * q, cc * q + q), start=True, stop=False)
                            R.mm(po, qg[:, t0_ + cc * q:t0_ + cc * q + q], sB, start=False, stop=True)
                        pss = F2.v(0, 128)
                        R.mm(pss, TMH.v(1536, 1664, cc * q, cc * q + q), TMH.v(tt * 128, tt * 128 + 128, cc * q, cc * q + q))
                        cg = tt * 2 + cc
                        R.stt(S_, S_, sm2[:, cg:cg + 1], pss, ALU.mult, ALU.add)
                        if isl:
                            R.dma("sp", DV(og["hg"][l, sq_, hh]), S_)
                    po_ = F0.v(0, 128, 0, P)
                    ssq = SM.v(40, 41, 0, P)
                    junk = TM.v(1536, 1664, 0, P)
                    R.act(junk, po_, AF.Square, accum=ssq)
                    R.act(ssq, ssq, AF.Sqrt, bias=b_eps(P), scale=1.0 / 128)
                    R.recip(ssq, ssq)
                    on = TM.v(1792, 1920, 0, P)
                    R.stt(on, po_, ssq, c["hnw"].v(0, 128, 0, P), ALU.mult, ALU.mult)
                    yh = TMH.v(1024, 1152, 0, P)
                    R.tt(yh, on, gsl, ALU.mult)
                    R.tr(PTb.v(512, 512 + P), yh, ident_b(P, P))
                    R.copy(ykT.v(hh * NMAX + t0_, hh * NMAX + t0_ + P), PTb.v(512, 512 + P), eng="act")

            if "C" in phases:
                run_streams([(lambda cx, hh=hh: hgrn_head(cx, hh)) for hh in range(8)])
                merge(1)

            def rg_tile(cx, wrg, wrx, j, m):
                F0, F1, F2 = cx.f[0], cx.f[1], cx.f[2]
                psx = F1.v(0, N)
                dense_fm(psx, wrx, j * 128, hT, N)
                xc = cx.sfb(0, N)
                conv_fm(cx, psx, seg, histR[l], m, c["rcw"], c["rcb"], xc)
                xcb = cx.shb(0, N)
                R.copy(xcb, xc, eng="act")
                psa = F2.v(0, N)
                R.mm(psa, c["bdA"].v(m * 128, m * 128 + 128), xcb)
                psi = F0.v(0, N)
                R.mm(psi, c["bdI"].v(m * 128, m * 128 + 128), xcb)
                rr = cx.sfb(1, N)
                R.act(rr, psa, AF.Sigmoid, bias=c["rba"].v(m, m + 1))
                ig = cx.sfb(2, N)
                R.act(ig, psi, AF.Sigmoid, bias=c["rbi"].v(m, m + 1))
                aa = cx.sfb(3, N)
                R.act(aa, rr, AF.Exp, scale=c["rc1"].v(m, m + 1))
                R.act(rr, rr, AF.Exp, scale=c["rc2"].v(m, m + 1))
                R.act(rr, rr, AF.Sqrt, bias=b_one(), scale=-1.0)
                R.tt(ig, ig, xc, ALU.mult)
                R.tt(ig, ig, rr, ALU.mult)
                hs = cx.sfb(4, N)
                for pc in range(seg.npc):
                    n = seg.n_piece
                    hst = rgH[l].v(m * 2 + pc, m * 2 + pc + 1)
                    R.scan(hs[:, pc * n:(pc + 1) * n], aa[:, pc * n:(pc + 1) * n], ig[:, pc * n:(pc + 1) * n], hst, ALU.mult, ALU.add)
                    R.copy(hst, hs[:, (pc + 1) * n - 1:(pc + 1) * n])
                psr = F1.v(0, N)
                dense_fm(psr, wrg, j * 128, hT, N)
                ge = cx.sfb(1, N)
                R.act(ge, psr, AF.Gelu_apprx_tanh)
                R.tt(ykT.v(m * NMAX, m * NMAX + N), ge, hs, ALU.mult)

            for half in (range(2) if "D" in phases else ()):
                wrg = wl(win(l, OFF_RGATE + half * 512, 512))
                wrx = wl(win(l, OFF_RX + half * 512, 512))
                run_streams([(lambda cx, j=j: rg_tile(cx, wrg, wrx, j, half * 4 + j)) for j in range(4)])
            if "D" in phases:
                merge(2)

            if seg.last:
                for pc in range(seg.npc):
                    sq_ = seg.seqs[pc]
                    hm = histM[l].v(0, 96).r("p (c a k) -> p c a k", a=2, k=3)[:, :, pc, :]
                    hr = histR[l].v(0, 48).r("p (c a k) -> p c a k", a=2, k=3)[:, :, pc, :]
                    rh = rgH[l].v(0, 16).r("p (m a) -> p m a", a=2)[:, :, pc]
                    for k in range(3):
                        R.dma("sp", DV(og["sc"][l, sq_, k].rearrange("(c p) -> p c", p=128)), hm[:, :, k], slow=True)
                        R.dma("sp", DV(og["rc"][l, sq_, k].rearrange("(c p) -> p c", p=128)), hr[:, :, k], slow=True)
                    R.dma("sp", DV(og["rg"][l, sq_].rearrange("(m p) -> p m", p=128)), rh, slow=True)

            for cch in range(8):
                R.copy(hT.v(cch * NMAX, cch * NMAX + N), mergedT.v(cch * NMAX, cch * NMAX + N), eng="act")
            for half in (range(2) if "E" in phases else ()):
                wo = wl((w_out[l, :, half * 512:(half + 1) * 512], 8, None))
                for j in range(4):
                    dc = half * 4 + j
                    ps = psA().v(0, N)
                    dense_fm(ps, wo, j * 128, hT, N)
                    xv = xT.v(dc * NMAX, dc * NMAX + N)
                    R.tt(xv, xv, ps, ALU.add)

            rmsnorm_fm(c["nffn"], N, hT)
            for fh in (range(2) if "F" in phases else ()):
                fb = fh * 11
                fc = 0
                while fc < 11:
                    nf = min(4, 11 - fc)
                    wg_ = wl((w_ffn_in[l, :, (fb + fc) * 128:(fb + fc + nf) * 128], 8, None))
                    wu_ = wl((w_ffn_in[l, :, FF + (fb + fc) * 128:FF + (fb + fc + nf) * 128], 8, None))
                    for j in range(nf):
                        psg = psA().v(0, N)
                        dense_fm(psg, wg_, j * 128, hT, N)
                        sg = sf(N)
                        R.act(sg, psg, AF.Silu)
                        psu = psA().v(0, N)
                        dense_fm(psu, wu_, j * 128, hT, N)
                        R.tt(actT.v((fc + j) * NMAX, (fc + j) * NMAX + N), sg, psu, ALU.mult)
                    fc += nf
                for half in range(2):
                    pss = [PALL[j].v(0, N) for j in range(4)]
                    f0 = 0
                    while f0 < 11:
                        nf = min(8, 11 - f0)
                        wo = wl((w_ffn_out[l, (fb + f0) * 128:(fb + f0 + nf) * 128, half * 512:(half + 1) * 512], nf, None))
                        for j in range(4):
                            dense_fm(pss[j], wo, j * 128, actT, N, nk=nf, first=(f0 == 0), last=(f0 + nf == 11), kbase=f0)
                        f0 += nf
                    for j in range(4):
                        dc = half * 4 + j
                        xv = xT.v(dc * NMAX, dc * NMAX + N)
                        R.tt(xv, xv, pss[j], ALU.add)

        def run_segment(seg):
            N, P, NT = seg.N, seg.P, seg.NT
            for tt in range(NT):
                pc = (tt * P) // seg.n_piece
                tk = seg.tok0[pc] + (tt * P) % seg.n_piece
                io = IO.v(0, 1024, 0, P)
                if seg.kind == "p":
                    R.dma("sp", io, DV(xp[seg.seqs[0], tk:tk + P, :]))
                else:
                    for a in range(seg.npc):
                        R.dma("sp", IO.v(0, 1024, a * 32, a * 32 + 32), DV(xs[seg.seqs[a], :, :]))
                for half in (range(2) if "x" not in phases else ()):
                    pf = psA()
                    for j in range(4):
                        cch = half * 4 + j
                        R.tr(pf.v(j * 128, j * 128 + P), io[:, cch * 128:cch * 128 + 128], ident_f(P, P))
                    for j in (range(4) if "y" not in phases else ()):
                        cch = half * 4 + j
                        R.copy(xT.v(cch * NMAX + tt * P, cch * NMAX + tt * P + P), pf.v(j * 128, j * 128 + P), eng=("act" if (j % 2 and "v" not in phases) else "dve"))
            if stop <= 2:
                return
            for l in range(nlayers):
                layer_pass(seg, l)
            if stop <= 3:
                return
            rmsnorm_fm(nfin, N, mergedT)
            for tt in range(NT):
                io = IO.v(0, 1024, 0, P)
                for half in range(2):
                    pf = psA()
                    for j in range(4):
                        cch = half * 4 + j
                        R.tr(pf.v(j * 128, j * 128 + 128, 0, P), mergedT.v(cch * NMAX + tt * P, cch * NMAX + tt * P + P), ident_f(128, 128))
                    R.copy(io[:, half * 512:half * 512 + 512], pf.v(0, 512, 0, P), eng=("act" if half else "dve"))
                if seg.kind == "p":
                    tk = seg.tok0[0] + tt * P
                    R.dma("sp", DV(yp[seg.seqs[0], tk:tk + P, :]), io)
                else:
                    for a in range(seg.npc):
                        R.dma("sp", DV(ys[seg.seqs[a], :, :]), IO.v(0, 1024, a * 32, a * 32 + 32))

        segs = []
        for s in range(nseq_p):
            for j in range(nseg_p):
                segs.append(Seg("p", [s], [j * 512], 512, 64, j == 0, j == nseg_p - 1))
        if with_sample:
            segs.append(Seg("s", [0, 1], [0, 0], 32, 32, True, True))
        for sg_ in (segs if stop >= 2 else ()):
            run_segment(sg_)
        if dbg_out is not None:
            pass
        R.finalize(nc, es)
    return nc, R


_WNAMES = ["norm_mix", "w_in", "mb_conv_w", "mb_conv_b", "mb_dt_bias", "mb_a_log", "mb_d", "mb_norm_w",
           "hg_lb_logits", "hg_norm_w", "rg_conv_w", "rg_conv_b", "rg_w_a", "rg_b_a", "rg_w_i", "rg_b_i",
           "rg_lambda", "w_branch", "w_out", "norm_ffn", "w_ffn_in", "w_ffn_out", "norm_final"]


def make_in_maps(inputs, ncores=8):
    f = lambda a: np.ascontiguousarray(np.asarray(a, dtype=np.float32))
    wts = {k: f(inputs[k]) for k in _WNAMES}
    maps = []
    for c in range(ncores):
        s = slice(2 * c, 2 * c + 2)
        m = dict(wts)
        m["xp"] = f(inputs["x_prompt"][s])
        m["xs"] = f(inputs["x_sample"][s])
        m["st_ssm"] = f(inputs["state_ssm"][:, s])
        m["st_sc"] = f(inputs["state_ssm_conv"][:, s])
        m["st_hg"] = f(inputs["state_hgrn"][:, s])
        m["st_rg"] = f(inputs["state_rglru"][:, s])
        m["st_rc"] = f(inputs["state_rglru_conv"][:, s])
        m["ctab"] = CTAB
        maps.append(m)
    return maps


def gather(results):
    cat = lambda k, ax: np.concatenate([np.asarray(r[k], dtype=np.float32) for r in results], axis=ax)
    out = [cat("yp", 0), cat("ys", 0)]
    for g in ("p", "s"):
        for nm in ("ssm", "sc", "hg", "rg", "rc"):
            out.append(cat("o_%s_%s" % (nm, g), 1))
    return tuple(out)


def kernel(**inputs):
    nc, _ = build()
    maps = make_in_maps(inputs)
    res = run_bass_kernel_spmd(nc, maps, core_ids=list(range(8)))
    return gather(res.results)
```

```python
import contextlib
import numpy as np
import concourse.bass as bass
import concourse.mybir as mybir
from concourse.bass_utils import run_bass_kernel_spmd

F32 = mybir.dt.float32
BF16 = mybir.dt.bfloat16
AF = mybir.ActivationFunctionType
ALU = mybir.AluOpType

D = 1024
SEQ = 2048
DEC_SEQ = 32
EPS = 1e-6
FF = 2816
INW = 12304
OFF_Z, OFF_XS, OFF_B, OFF_C, OFF_DT = 0, 1024, 2048, 2560, 3072
OFF_HQ, OFF_HF, OFF_HI, OFF_HG, OFF_RGATE, OFF_RX, OFF_MG = 3088, 4112, 5136, 6160, 7184, 8208, 9232
NEG = -30000.0


class V:
    __slots__ = ("ap", "keys")

    def __init__(self, ap, keys):
        self.ap = ap
        self.keys = keys

    def r(self, pat, **kw):
        return V(self.ap.rearrange(pat, **kw), self.keys)

    def __getitem__(self, idx):
        return V(self.ap[idx], self.keys)

    def bc(self, shape):
        return V(self.ap.to_broadcast(list(shape)), self.keys)

    def us(self, axis):
        return V(self.ap.unsqueeze(axis), self.keys)


class Buf:
    def __init__(self, name, t, free, leaf):
        self.name, self.t, self.free, self.leaf = name, t, free, leaf

    def v(self, f0, f1, p0=0, p1=128):
        assert 0 <= f0 < f1 <= self.free, (self.name, f0, f1, self.free)
        keys = tuple((self.name, j) for j in range(f0 // self.leaf, (f1 - 1) // self.leaf + 1))
        return V(self.t[p0:p1, f0:f1], keys)


class Op:
    __slots__ = ("eng", "fn", "reads", "writes", "dma", "deps", "needed", "sig", "waits", "grp")

    def __init__(self, eng, fn, reads, writes, dma):
        self.eng, self.fn, self.reads, self.writes, self.dma = eng, fn, reads, writes, dma
        self.deps = ()
        self.needed = False
        self.sig = None
        self.waits = ()


NS_DMA = 8


class Rec:
    def __init__(self):
        self.ops = []
        self.gctr = 0
        self.gcur = None

    @contextlib.contextmanager
    def atomic(self):
        if self.gcur is not None:
            yield
            return
        self.gctr += 1
        self.gcur = self.gctr
        try:
            yield
        finally:
            self.gcur = None

    def add(self, eng, fn, reads=(), writes=(), dma=False):
        rk = []
        for x in reads:
            if x is not None and not isinstance(x, (int, float)):
                rk.extend(x.keys)
        wk = []
        for x in writes:
            if x is not None:
                wk.extend(x.keys)
        op = Op(eng, fn, tuple(rk), tuple(wk), dma)
        if self.gcur is not None:
            op.grp = self.gcur
        else:
            self.gctr += 1
            op.grp = self.gctr
        self.ops.append(op)

    def mm(self, out, lhsT, rhs, start=True, stop=True):
        rd = [lhsT, rhs] + ([] if start else [out])
        self.add("pe", lambda e: e.matmul(out.ap, lhsT.ap, rhs.ap, start=start, stop=stop), rd, [out])

    def tr(self, out, in_, ident):
        if in_.ap.dtype == F32:
            self.add("pe", lambda e: e.matmul(out.ap, in_.ap, ident.ap, start=True, stop=True), [in_, ident], [out])
        else:
            self.add("pe", lambda e: e.transpose(out.ap, in_.ap, ident.ap), [in_, ident], [out])

    def act(self, out, in_, func, bias=None, scale=None, accum=None):
        kw = {}
        rd = [in_]
        if bias is not None:
            if isinstance(bias, V):
                kw["bias"] = bias.ap
                rd.append(bias)
            else:
                kw["bias"] = float(bias)
        if scale is not None:
            if isinstance(scale, V):
                kw["scale"] = scale.ap
                rd.append(scale)
            else:
                kw["scale"] = float(scale)
        wr = [out]
        if accum is not None:
            kw["accum_out"] = accum.ap
            wr.append(accum)
        self.add("act", lambda e: e.activation(out=out.ap, in_=in_.ap, func=func, **kw), rd, wr)

    def tt(self, out, in0, in1, op, eng="dve"):
        self.add(eng, lambda e: e.tensor_tensor(out=out.ap, in0=in0.ap, in1=in1.ap, op=op), [in0, in1], [out])

    def ts(self, out, in0, s1, s2, op0, op1=None, eng="dve"):
        rd = [in0]
        a1 = s1
        a2 = s2
        if isinstance(s1, V):
            rd.append(s1)
            a1 = s1.ap
        if isinstance(s2, V):
            rd.append(s2)
            a2 = s2.ap
        if op1 is None:
            self.add(eng, lambda e: e.tensor_scalar(out=out.ap, in0=in0.ap, scalar1=a1, scalar2=None, op0=op0), rd, [out])
        else:
            self.add(eng, lambda e: e.tensor_scalar(out=out.ap, in0=in0.ap, scalar1=a1, scalar2=a2, op0=op0, op1=op1), rd, [out])

    def stt(self, out, in0, scalar, in1, op0, op1):
        rd = [in0, in1]
        sc = scalar
        if isinstance(scalar, V):
            rd.append(scalar)
            sc = scalar.ap
        self.add("dve", lambda e: e.scalar_tensor_tensor(out=out.ap, in0=in0.ap, scalar=sc, in1=in1.ap, op0=op0, op1=op1), rd, [out])

    def scan(self, out, d0, d1, init, op0, op1):
        rd = [d0, d1]
        iv = init
        if isinstance(init, V):
            rd.append(init)
            iv = init.ap
        self.add("dve", lambda e: e.tensor_tensor_scan(out=out.ap, data0=d0.ap, data1=d1.ap, initial=iv, op0=op0, op1=op1), rd, [out])

    def copy(self, out, in_, eng="dve"):
        if eng == "act":
            self.add("act", lambda e: e.copy(out.ap, in_.ap), [in_], [out])
        else:
            self.add(eng, lambda e: e.tensor_copy(out=out.ap, in_=in_.ap), [in_], [out])

    def memset(self, out, val, eng="dve"):
        self.add(eng, lambda e: e.memset(out.ap, val), [], [out])

    def recip(self, out, in_):
        self.add("dve", lambda e: e.reciprocal(out.ap, in_.ap), [in_], [out])

    def dma(self, q, out, in_, slow=False):
        if slow:
            self.add(q, lambda e: e.dma_start(out=out.ap, in_=in_.ap, allow_slow_non_contiguous=True), [in_], [out], dma=True)
        else:
            self.add(q, lambda e: e.dma_start(out=out.ap, in_=in_.ap), [in_], [out], dma=True)

    def finalize(self, nc, es):
        ops = self.ops
        last_w = {}
        readers = {}
        for i, op in enumerate(ops):
            deps = set()
            for k in op.reads:
                w = last_w.get(k)
                if w is not None:
                    deps.add(w)
                if k[0][0] == "P" and k[0] != "PC":
                    for r_ in readers.get(k, ()):
                        if ops[r_].eng != op.eng:
                            deps.add(r_)
            for k in op.writes:
                w = last_w.get(k)
                if w is not None:
                    deps.add(w)
                rs = readers.get(k)
                if rs:
                    deps.update(rs)
            deps.discard(i)
            if op.eng == "pe":
                deps = {d for d in deps if ops[d].eng != "pe" or ops[d].dma}
            op.deps = sorted(deps)
            for d in op.deps:
                ops[d].needed = True
            for k in op.writes:
                last_w[k] = i
                readers[k] = []
            for k in op.reads:
                if k not in op.writes:
                    readers.setdefault(k, []).append(i)
        engs = ["sp", "act", "pe", "dve", "pool"]
        csem = {e: es.enter_context(nc.semaphore("c_" + e)) for e in engs}
        dsem = {e: [es.enter_context(nc.semaphore("d_%s%d" % (e, j))) for j in range(NS_DMA)] for e in ("sp", "pool")}
        ccount = {e: 0 for e in engs}
        dcount = {e: 0 for e in dsem}
        waited = {e: {} for e in engs}
        for op in ops:
            w = []
            wd = waited[op.eng]
            need = {}
            for d in op.deps:
                sem, val = ops[d].sig
                if need.get(id(sem), (None, 0))[1] < val:
                    need[id(sem)] = (sem, val)
            for sid, (sem, val) in need.items():
                if wd.get(sid, 0) < val:
                    wd[sid] = val
                    w.append((sem, val))
            if op.dma:
                n = dcount[op.eng]
                dcount[op.eng] = n + 1
                sem = dsem[op.eng][n % NS_DMA]
                rnd = n // NS_DMA + 1
                if rnd > 1 and wd.get(id(sem), 0) < 16 * (rnd - 1):
                    wd[id(sem)] = 16 * (rnd - 1)
                    w.append((sem, 16 * (rnd - 1)))
                op.sig = (sem, 16 * rnd)
            elif op.needed:
                ccount[op.eng] += 1
                op.sig = (csem[op.eng], ccount[op.eng])
            op.waits = w
        per = {e: [op for op in ops if op.eng == e] for e in engs}
        self.stats = {e: len(per[e]) for e in engs}

        def emit(e, name):
            for op in per[name]:
                for sem, val in op.waits:
                    e.wait_ge(sem, val)
                ins = op.fn(e)
                if op.dma:
                    ins.then_inc(op.sig[0], 16)
                elif op.sig is not None:
                    ins.then_inc(op.sig[0], 1)
            if name in dsem:
                n = dcount[name]
                for j in range(NS_DMA):
                    cnt = (n - j + NS_DMA - 1) // NS_DMA if n > j else 0
                    if cnt > 0:
                        e.wait_ge(dsem[name][j], 16 * cnt)

        with nc.Block() as block:
            @block.sync
            def _(e):
                emit(e, "sp")

            @block.scalar
            def _(e):
                emit(e, "act")

            @block.tensor
            def _(e):
                emit(e, "pe")

            @block.vector
            def _(e):
                emit(e, "dve")

            @block.gpsimd
            def _(e):
                emit(e, "pool")


def _tables(q):
    P = 2 * q
    t = np.arange(P)
    U2 = ((t[:, None] // q == t[None, :] // q) & (t[:, None] <= t[None, :])).astype(np.float32)
    Uq = ((t[:, None] % q) <= np.arange(q)[None, :]).astype(np.float32)
    BD1 = (t[:, None] // q == t[None, :] // q).astype(np.float32)
    m01 = (np.arange(q)[None, :] >= (t[:, None] % q)).astype(np.float32)
    neg = np.where(m01 > 0, 0.0, NEG).astype(np.float32)
    sel0 = np.repeat((t[:, None] // q == 0).astype(np.float32), 128, axis=1)
    sel1 = np.repeat((t[:, None] // q == 1).astype(np.float32), 128, axis=1)
    cols = [U2, Uq, BD1, neg, m01, sel0, sel1]
    out = np.zeros((128, sum(c.shape[1] for c in cols)), np.float32)
    o = 0
    offs = {}
    for nm, c in zip(["U2", "Uq", "BD1", "neg", "m01", "sel0", "sel1"], cols):
        out[:P, o:o + c.shape[1]] = c
        offs[nm] = (o, c.shape[1])
        o += c.shape[1]
    return out, offs


def make_ctab():
    ident = np.eye(128, dtype=np.float32)
    ones = np.ones((128, 512), np.float32)
    t64, o64 = _tables(64)
    t32, o32 = _tables(32)
    tab = np.concatenate([ident, ones, t64, t32], axis=1)
    base64 = 128 + 512
    base32 = base64 + t64.shape[1]
    offs = {"ident": (0, 128), "ones": (128, 512)}
    for k, (o, n) in o64.items():
        offs[(64, k)] = (base64 + o, n)
    for k, (o, n) in o32.items():
        offs[(32, k)] = (base32 + o, n)
    return np.ascontiguousarray(tab), offs


CTAB, COFF = make_ctab()
NCT = CTAB.shape[1]


class Seg:
    def __init__(self, kind, seqs, tok0, n_piece, q, first, last):
        self.kind = kind
        self.seqs = seqs
        self.tok0 = tok0
        self.n_piece = n_piece
        self.npc = len(seqs)
        self.N = n_piece * len(seqs)
        self.q = q
        self.P = 2 * q
        self.NT = self.N // self.P
        self.first = first
        self.last = last

    def chunk_info(self, tt, c):
        if self.kind == "p":
            return self.seqs[0], (self.first and tt == 0 and c == 0), (self.last and tt == self.NT - 1 and c == 1)
        return self.seqs[c], True, True


def build(nseg_p=4, nseq_p=2, with_sample=True, nlayers=2, dbg=None, phases="BCDEF", stop=99):
    nc = bass.Bass("TRN2", target_bir_lowering=False)
    R = Rec()
    es = contextlib.ExitStack()
    with es:
        def din(name, shape):
            return nc.dram_tensor(name, list(shape), F32, kind="ExternalInput").ap()

        def dout(name, shape):
            return nc.dram_tensor(name, list(shape), F32, kind="ExternalOutput").ap()

        def DV(ap):
            return V(ap, ())

        xp = din("xp", (2, SEQ, D))
        xs = din("xs", (2, DEC_SEQ, D))
        st_ssm = din("st_ssm", (2, 2, 16, 64, 128))
        st_sc = din("st_sc", (2, 2, 3, 2048))
        st_hg = din("st_hg", (2, 2, 8, 128, 128))
        st_rg = din("st_rg", (2, 2, 1024))
        st_rc = din("st_rc", (2, 2, 3, 1024))
        norm_mix = din("norm_mix", (2, D))
        w_in = din("w_in", (2, D, INW))
        mb_conv_w = din("mb_conv_w", (2, 4, 2048))
        mb_conv_b = din("mb_conv_b", (2, 2048))
        mb_dt_bias = din("mb_dt_bias", (2, 16))
        mb_a_log = din("mb_a_log", (2, 16))
        mb_d = din("mb_d", (2, 16))
        mb_norm_w = din("mb_norm_w", (2, 1024))
        hg_lb_logits = din("hg_lb_logits", (2, 1024))
        hg_norm_w = din("hg_norm_w", (2, 128))
        rg_conv_w = din("rg_conv_w", (2, 4, 1024))
        rg_conv_b = din("rg_conv_b", (2, 1024))
        rg_w_a = din("rg_w_a", (2, 16, 64, 64))
        rg_b_a = din("rg_b_a", (2, 1024))
        rg_w_i = din("rg_w_i", (2, 16, 64, 64))
        rg_b_i = din("rg_b_i", (2, 1024))
        rg_lambda = din("rg_lambda", (2, 1024))
        w_branch = din("w_branch", (2, 3, D, D))
        w_out = din("w_out", (2, D, D))
        norm_ffn = din("norm_ffn", (2, D))
        w_ffn_in = din("w_ffn_in", (2, D, 2 * FF))
        w_ffn_out = din("w_ffn_out", (2, FF, D))
        norm_final = din("norm_final", (D,))
        ctab = din("ctab", (128, NCT))

        yp = dout("yp", (2, SEQ, D))
        ys = dout("ys", (2, DEC_SEQ, D))
        outs = {}
        for g in ("p", "s"):
            outs[g] = dict(
                ssm=dout("o_ssm_" + g, (2, 2, 16, 64, 128)),
                sc=dout("o_sc_" + g, (2, 2, 3, 2048)),
                hg=dout("o_hg_" + g, (2, 2, 8, 128, 128)),
                rg=dout("o_rg_" + g, (2, 2, 1024)),
                rc=dout("o_rc_" + g, (2, 2, 3, 1024)),
            )
        dbg_out = None
        if dbg:
            dbg_out = dout("dbg", (128, dbg))

        def sb(name, free, dtype, leaf):
            t = es.enter_context(nc.sbuf_tensor(name, [128, free], dtype))
            return Buf(name, t, free, leaf)

        def psb(name, free, dtype, leaf):
            t = es.enter_context(nc.psum_tensor(name, [128, free], dtype))
            return Buf(name, t, free, leaf)

        NMAX = 512
        CT = sb("CT", NCT, F32, NCT)
        CTB = sb("CTB", 128 + 128, BF16, 256)
        xT = sb("xT", 8 * NMAX, F32, NMAX)
        hT = sb("hT", 8 * NMAX, BF16, NMAX)
        mergedT = sb("mergedT", 8 * NMAX, F32, NMAX)
        ykT = sb("ykT", 8 * NMAX, BF16, NMAX)
        NWB = 4
        WB = sb("WB", NWB * 4096, BF16, 4096)
        ssmT = [sb("ssmT%d" % l, 1024, F32, 256) for l in range(2)]
        hgS = [sb("hgS%d" % l, 1024, F32, 128) for l in range(2)]
        rgH = [sb("rgH%d" % l, 16, F32, 2) for l in range(2)]
        histM = [sb("histM%d" % l, 16 * 2 * 3, F32, 6) for l in range(2)]
        histR = [sb("histR%d" % l, 8 * 2 * 3, F32, 6) for l in range(2)]
        ssmB = sb("ssmB", 4 * 256, BF16, 256)
        hgB = sb("hgB", 4 * 128, BF16, 128)
        LC = []
        for l in range(2):
            LC.append(dict(
                nmix=sb("nmix%d" % l, 8, F32, 8), nffn=sb("nffn%d" % l, 8, F32, 8),
                mcw=sb("mcw%d" % l, 64, F32, 64), mcb=sb("mcb%d" % l, 16, F32, 16),
                rcw=sb("rcw%d" % l, 32, F32, 32), rcb=sb("rcb%d" % l, 8, F32, 8),
                rba=sb("rba%d" % l, 8, F32, 8), rbi=sb("rbi%d" % l, 8, F32, 8),
                rc1=sb("rc1%d" % l, 8, F32, 8), rc2=sb("rc2%d" % l, 8, F32, 8),
                lb=sb("lb%d" % l, 8, F32, 8), oml=sb("oml%d" % l, 8, F32, 8), noml=sb("noml%d" % l, 8, F32, 8),
                hnw=sb("hnw%d" % l, 128, F32, 128),
                dtb=sb("dtb%d" % l, 16, F32, 16), aneg=sb("aneg%d" % l, 16, F32, 16), dd=sb("dd%d" % l, 16, F32, 16),
                bdA=sb("bdA%d" % l, 1024, BF16, 128), bdI=sb("bdI%d" % l, 1024, BF16, 128),
                wdt=sb("wdt%d" % l, 128, BF16, 128),
            ))
        nfin = sb("nfin", 8, F32, 8)
        MNW = sb("MNW", 1024, F32, 1024)
        CBT = sb("CBT", 4, F32, 4)
        ctmp = sb("ctmp", 64, F32, 64)
        actT = sb("actT", 11 * NMAX, BF16, NMAX)
        IO = sb("IO", 1024, F32, 1024)
        SMG = sb("SMG", 256, F32, 64)

        class Ctx:
            def __init__(self, i):
                self.i = i
                self.NF, self.NH = 5, 4
                self.SF = sb("SF%d" % i, self.NF * NMAX, F32, NMAX)
                self.SH = sb("SH%d" % i, self.NH * NMAX, BF16, NMAX)
                self.PC = sb("PC%d" % i, 2 * 520, F32, 520)
                self.SM = sb("SM%d" % i, 256, F32, 16)
                self.TM = sb("TM%d" % i, 8 * 256, F32, 256)
                self.TMH = sb("TMH%d" % i, 8 * 256, BF16, 256)
                self.IOS = sb("IOS%d" % i, 512, F32, 256)
                self.f = [psb("PF%d_%d" % (i, j), 512, F32, 512) for j in range(3)]
                self.t = psb("PTT%d" % i, 1024, BF16, 1024)
                self.ops = []
                self.sfc = 0
                self.shc = 0

            def sfb(self, j, N):
                return self.SF.v(j * NMAX, j * NMAX + N)

            def shb(self, j, N):
                return self.SH.v(j * NMAX, j * NMAX + N)

            def sf(self, N):
                self.sfc = (self.sfc + 1) % self.NF
                return self.sfb(self.sfc, N)

            def sh(self, N):
                self.shc = (self.shc + 1) % self.NH
                return self.shb(self.shc, N)

        CX = [Ctx(0), Ctx(1)]
        PALL = [CX[0].f[0], CX[0].f[1], CX[0].f[2], CX[1].f[0], CX[1].f[1], CX[1].f[2]]

        st = dict(pa=0, wb=0)

        def psA():
            st["pa"] = (st["pa"] + 1) % 6
            return PALL[st["pa"]]

        def sf(N):
            return CX[0].sf(N)

        def sh(N):
            return CX[0].sh(N)

        def run_streams(bodies):
            k = 0
            while k < len(bodies):
                pair = bodies[k:k + 2]
                lists = []
                for j, body in enumerate(pair):
                    main = R.ops
                    R.ops = []
                    body(CX[j])
                    lists.append(R.ops)
                    R.ops = main
                glists = []
                for x in lists:
                    gl_ = []
                    for op in x:
                        if gl_ and gl_[-1][0].grp == op.grp:
                            gl_[-1].append(op)
                        else:
                            gl_.append([op])
                    glists.append(gl_)
                m = max(len(x) for x in glists)
                for idx in range(m):
                    for x in glists:
                        if idx < len(x):
                            R.ops.extend(x[idx])
                k += 2

        ident_f = lambda p, n: CT.v(0, n, 0, p)
        ones_f = lambda p, n: CT.v(128, 128 + n, 0, p)
        ident_b = lambda p, n: CTB.v(0, n, 0, p)
        ones_b = lambda p, n: CTB.v(128, 128 + n, 0, p)

        def ctv(name, q, p0=0, p1=None):
            o, n = COFF[(q, name)]
            return CT.v(o, o + n, p0, 2 * q if p1 is None else p1)

        class WP:
            def __init__(self, b):
                self.b = b

            def t(self, kc, c0, n):
                return WB.v(self.b * 4096 + kc * 512 + c0, self.b * 4096 + kc * 512 + c0 + n)

        def wload(parts):
            st["wb"] = (st["wb"] + 1) % NWB
            b = st["wb"]
            for src, nk, dc in parts:
                n = src.shape[1]
                dst = WB.v(b * 4096, (b + 1) * 4096).r("p (kc n) -> p kc n", n=512)[:, 0:nk, dc:dc + n]
                R.dma("pool", dst, DV(src.rearrange("(kc p) n -> p kc n", p=128)))
            return WP(b)

        class WDT:
            def __init__(self, b):
                self.b = b

            def t(self, kc, c0, n):
                return self.b.v(kc * 16 + c0, kc * 16 + c0 + n)

        def win(l, c0, n):
            return (w_in[l, :, c0:c0 + n], 8, None)

        def wl(*parts):
            pl = []
            dc = 0
            for src, nk, _ in parts:
                pl.append((src, nk, dc))
                dc += src.shape[1]
            assert dc <= 512
            return wload(pl)

        R.memset(CBT.v(0, 1), 1.0)
        R.memset(CBT.v(1, 2), EPS)
        b_one = lambda p=128: CBT.v(0, 1, 0, p)
        b_eps = lambda p=128: CBT.v(1, 2, 0, p)
        R.dma("sp", CT.v(0, NCT), DV(ctab))
        R.dma("pool", CTB.v(0, 128), DV(ctab[:, 0:128]))
        R.dma("pool", CTB.v(128, 256), DV(ctab[:, 128:256]))

        def fm_load(dst, src1d, ncol):
            R.dma("sp", dst.v(0, ncol), DV(src1d.rearrange("(c p) -> p c", p=128)), slow=True)

        def bc_load(dst, src1d, n):
            R.dma("sp", dst.v(0, n), DV(src1d.partition_broadcast(128)), slow=True)

        for l in (range(nlayers) if stop >= 1 else ()):
            c = LC[l]
            fm_load(c["nmix"], norm_mix[l], 8)
            fm_load(c["nffn"], norm_ffn[l], 8)
            for k in range(4):
                R.dma("sp", c["mcw"].v(0, 64).r("p (c k) -> p c k", k=4)[:, :, k], DV(mb_conv_w[l, k].rearrange("(c p) -> p c", p=128)), slow=True)
                R.dma("sp", c["rcw"].v(0, 32).r("p (c k) -> p c k", k=4)[:, :, k], DV(rg_conv_w[l, k].rearrange("(c p) -> p c", p=128)), slow=True)
            fm_load(c["mcb"], mb_conv_b[l], 16)
            fm_load(c["rcb"], rg_conv_b[l], 8)
            fm_load(c["rba"], rg_b_a[l], 8)
            fm_load(c["rbi"], rg_b_i[l], 8)
            fm_load(c["rc1"], rg_lambda[l], 8)
            t0 = ctmp.v(0, 8)
            R.act(t0, c["rc1"].v(0, 8), AF.Exp, scale=-1.0)
            R.act(t0, t0, AF.Ln, bias=b_one())
            R.ts(c["rc1"].v(0, 8), t0, -8.0, None, ALU.mult)
            R.ts(c["rc2"].v(0, 8), t0, -16.0, None, ALU.mult)
            if l == 0:
                R.memset(c["lb"].v(0, 8), 0.0)
            else:
                t1 = ctmp.v(8, 16)
                t2 = ctmp.v(16, 24)
                fm_load_v = lambda dstv, src: R.dma("sp", dstv, DV(src.rearrange("(c p) -> p c", p=128)), slow=True)
                fm_load_v(t1, hg_lb_logits[0])
                fm_load_v(t2, hg_lb_logits[1])
                R.tt(t2, t2, t1, ALU.subtract)
                R.act(c["lb"].v(0, 8), t2, AF.Sigmoid)
            R.ts(c["oml"].v(0, 8), c["lb"].v(0, 8), -1.0, 1.0, ALU.mult, ALU.add)
            R.ts(c["noml"].v(0, 8), c["oml"].v(0, 8), -1.0, None, ALU.mult)
            bc_load(c["hnw"], hg_norm_w[l], 128)
            bc_load(c["dtb"], mb_dt_bias[l], 16)
            bc_load(c["dd"], mb_d[l], 16)
            bc_load(c["aneg"], mb_a_log[l], 16)
            R.act(c["aneg"].v(0, 16), c["aneg"].v(0, 16), AF.Exp)
            R.ts(c["aneg"].v(0, 16), c["aneg"].v(0, 16), -1.0, None, ALU.mult)
            for nm, wsrc in (("bdA", rg_w_a), ("bdI", rg_w_i)):
                R.memset(c[nm].v(0, 1024), 0.0, eng="pool")
                for par in range(2):
                    dst = c[nm].v(0, 1024, par * 64, par * 64 + 64).r("p (m j) -> p m j", j=128)[:, :, par * 64:par * 64 + 64]
                    src = wsrc[l].rearrange("(m two) i j -> two i m j", two=2)[par]
                    R.dma("pool", dst, DV(src))
            R.dma("pool", c["wdt"].v(0, 128).r("p (kc n) -> p kc n", n=16), DV(w_in[l, :, OFF_DT:OFF_DT + 16].rearrange("(kc p) n -> p kc n", p=128)))
        fm_load(nfin, norm_final, 8)

        def dense_fm(ps, wp, coff, src, N, nk=8, first=True, last=True, kbase=0):
            with R.atomic():
                for k in range(nk):
                    R.mm(ps, wp.t(k, coff, 128), src.v((kbase + k) * NMAX, (kbase + k) * NMAX + N),
                         start=(first and k == 0), stop=(last and k == nk - 1))

        def dense_tm(ps, wp, coff, n, src, tok0, P):
            with R.atomic():
                for k in range(8):
                    R.mm(ps, src.v(k * NMAX + tok0, k * NMAX + tok0 + P), wp.t(k, coff, n), start=(k == 0), stop=(k == 7))

        def rmsnorm_fm(nw, N, dst):
            ps = psA().v(0, N)
            for cch in range(8):
                sq = sh(N)
                R.act(sq, xT.v(cch * NMAX, cch * NMAX + N), AF.Square)
                R.mm(ps, ones_b(128, 128), sq, start=(cch == 0), stop=(cch == 7))
            rs = sf(N)
            R.act(rs, ps, AF.Sqrt, bias=b_eps(), scale=1.0 / D)
            R.recip(rs, rs)
            for cch in range(8):
                R.stt(dst.v(cch * NMAX, cch * NMAX + N), xT.v(cch * NMAX, cch * NMAX + N), nw.v(cch, cch + 1), rs, ALU.mult, ALU.mult)

        def conv_fm(cx, ps, seg, hist, ci, cw, cb, out_f32):
            npc, n = seg.npc, seg.n_piece
            W = 3 + n
            pcb = cx.PC.v((ci % 2) * 520, (ci % 2) * 520 + npc * W)
            pc3 = pcb.r("p (a w) -> p a w", w=W)
            R.copy(pc3[:, :, 3:W], ps.r("p (a n) -> p a n", n=n), eng="act")
            h3 = hist.v(ci * 6, ci * 6 + npc * 3).r("p (a k) -> p a k", k=3)
            R.copy(pc3[:, :, 0:3], h3)
            o3 = out_f32.r("p (a n) -> p a n", n=n)
            R.ts(o3, pc3[:, :, 0:n], cw.v(ci * 4, ci * 4 + 1), cb.v(ci, ci + 1), ALU.mult, ALU.add)
            for k in range(1, 4):
                R.stt(o3, pc3[:, :, k:k + n], cw.v(ci * 4 + k, ci * 4 + k + 1), o3, ALU.mult, ALU.add)
            R.copy(h3, pc3[:, :, n:n + 3])

        def layer_pass(seg, l):
            c = LC[l]
            N, P, q, NT = seg.N, seg.P, seg.q, seg.NT
            og = outs[seg.kind]
            if seg.first:
                for pc in range(seg.npc):
                    sq_ = seg.seqs[pc]
                    hm = histM[l].v(0, 96).r("p (c a k) -> p c a k", a=2, k=3)[:, :, pc, :]
                    hr = histR[l].v(0, 48).r("p (c a k) -> p c a k", a=2, k=3)[:, :, pc, :]
                    rh = rgH[l].v(0, 16).r("p (m a) -> p m a", a=2)[:, :, pc]
                    if seg.kind == "p":
                        R.memset(hm, 0.0)
                        R.memset(hr, 0.0)
                        R.memset(rh, 0.0)
                    else:
                        for k in range(3):
                            R.dma("sp", hm[:, :, k], DV(st_sc[l, sq_, k].rearrange("(c p) -> p c", p=128)), slow=True)
                            R.dma("sp", hr[:, :, k], DV(st_rc[l, sq_, k].rearrange("(c p) -> p c", p=128)), slow=True)
                        R.dma("sp", rh, DV(st_rg[l, sq_].rearrange("(m p) -> p m", p=128)), slow=True)
            bc_load(MNW, mb_norm_w[l], 1024)
            rmsnorm_fm(c["nmix"], N, hT)
            if stop <= 3:
                return

            def merge(k):
                for half in range(2):
                    wg = wl(win(l, OFF_MG + k * 1024 + half * 512, 512))
                    wb_ = wl((w_branch[l, k, :, half * 512:(half + 1) * 512], 8, None))
                    for j in range(4):
                        dc = half * 4 + j
                        psg = psA().v(0, N)
                        dense_fm(psg, wg, j * 128, hT, N)
                        sg = sf(N)
                        R.act(sg, psg, AF.Sigmoid)
                        psp = psA().v(0, N)
                        dense_fm(psp, wb_, j * 128, ykT, N)
                        md = mergedT.v(dc * NMAX, dc * NMAX + N)
                        if k == 0:
                            R.tt(md, psp, sg, ALU.mult)
                        else:
                            R.tt(sg, psp, sg, ALU.mult)
                            R.tt(md, md, sg, ALU.add)

            def smv(tt, o, n, p1=None):
                return SMG.v(tt * 64 + o, tt * 64 + o + n, 0, P if p1 is None else p1)
            for tt in (range(NT) if "B" in phases else ()):
                ps = psA().v(0, 16, 0, P)
                dense_tm(ps, WDT(c["wdt"]), 0, 16, hT, tt * P, P)
                dt_ = smv(tt, 0, 16)
                R.tt(dt_, ps, c["dtb"].v(0, 16, 0, P), ALU.add)
                R.act(dt_, dt_, AF.Exp)
                R.act(dt_, dt_, AF.Ln, bias=b_one(P))
                dtA = smv(tt, 16, 16)
                R.tt(dtA, dt_, c["aneg"].v(0, 16, 0, P), ALU.mult)
                ps2 = psA().v(0, 16, 0, P)
                R.mm(ps2, ctv("U2", q), dtA)
                acs = smv(tt, 32, 16)
                R.copy(acs, ps2)
                R.act(smv(tt, 48, 16), acs, AF.Exp)

            def mamba_group(cx, g):
                TM, TMH, SM = cx.TM, cx.TMH, cx.SM
                F0, F1, F2, PTb = cx.f[0], cx.f[1], cx.f[2], cx.t
                wa = wl(win(l, OFF_XS + g * 256, 256), win(l, OFF_B + g * 128, 128), win(l, OFF_C + g * 128, 128))
                wz = wl(win(l, OFF_Z + g * 256, 256))
                xa = []
                for j, ci in enumerate((2 * g, 2 * g + 1, 8 + g, 12 + g)):
                    ps = (F1 if j % 2 == 0 else F2).v(0, N)
                    dense_fm(ps, wa, j * 128, hT, N)
                    cf = cx.sfb(j % 2, N)
                    conv_fm(cx, ps, seg, histM[l], ci, c["mcw"], c["mcb"], cf)
                    xb_ = cx.shb(j, N)
                    R.act(xb_, cf, AF.Silu)
                    xa.append(xb_)
                BT, CTt = xa[2], xa[3]
                for tt in range(NT):
                    t0_ = tt * P
                    for j in range(3):
                        R.tr(PTb.v(j * 128, j * 128 + 128, 0, P), xa[j][:, t0_:t0_ + P], ident_b(128, 128))
                    xtm = TMH.v(0, 256, 0, P)
                    R.copy(TMH.v(0, 384, 0, P), PTb.v(0, 384, 0, P))
                    psz = F2.v(0, 256, 0, P)
                    dense_tm(psz, wz, 0, 256, hT, t0_, P)
                    zs = TM.v(0, 256, 0, P)
                    R.act(zs, psz, AF.Silu)
                    dt_g = smv(tt, 4 * g, 4)
                    dtA_g = smv(tt, 16 + 4 * g, 4)
                    acs_g = smv(tt, 32 + 4 * g, 4)
                    eacs_g = smv(tt, 48 + 4 * g, 4)
                    rhsU = TM.v(256, 256 + 4 * q, 0, P)
                    R.tt(rhsU.r("p (h i) -> p h i", i=q), ctv("Uq", q).us(1).bc([P, 4, q]), dtA_g.us(2).bc([P, 4, q]), ALU.mult)
                    pab = F1.v(0, 4 * q, 0, P)
                    R.mm(pab, ctv("BD1", q), rhsU)
                    segm = TM.v(512, 512 + 4 * q, 0, P)
                    seg3 = segm.r("p (h i) -> p h i", i=q)
                    R.tt(seg3, pab.r("p (h i) -> p h i", i=q), acs_g.us(2).bc([P, 4, q]), ALU.subtract)
                    wend = SM.v(0, 4, 0, P)
                    R.tt(wend, pab.r("p (h i) -> p h i", i=q)[:, :, q - 1], acs_g, ALU.subtract)
                    R.tt(seg3, seg3, ctv("neg", q).us(1).bc([P, 4, q]), ALU.add)
                    R.act(segm, segm, AF.Exp)
                    R.act(wend, wend, AF.Exp)
                    R.tt(wend, wend, dt_g, ALU.mult)
                    pcb = F2.v(256, 256 + q, 0, P)
                    for cc in range(2):
                        R.mm(F2.v(256, 256 + q, cc * q, cc * q + q), BT[:, t0_ + cc * q:t0_ + cc * q + q], CTt[:, t0_ + cc * q:t0_ + cc * q + q])
                    R.tt(seg3, seg3, pcb.us(1).bc([P, 4, q]), ALU.mult)
                    MT = TMH.v(512, 512 + 4 * q, 0, P)
                    R.tt(MT.r("p (h i) -> p h i", i=q), seg3, dt_g.us(2).bc([P, 4, q]), ALU.mult)
                    xw = TMH.v(768, 1024, 0, P)
                    R.tt(xw.r("p (h d) -> p h d", d=64), xtm.r("p (h d) -> p h d", d=64), wend.us(2).bc([P, 4, 64]), ALU.mult)
                    for cc in range(2):
                        for h in range(4):
                            R.mm(F0.v(h * 64, h * 64 + 64, cc * q, cc * q + q),
                                 TMH.v(512 + h * q, 512 + h * q + q, cc * q, cc * q + q),
                                 TMH.v(h * 64, h * 64 + 64, cc * q, cc * q + q))
                    sT = ssmT[l].v(g * 256, g * 256 + 256)
                    for cc in range(2):
                        sq_, isf, isl = seg.chunk_info(tt, cc)
                        if isf:
                            if seg.kind == "p":
                                R.memset(sT, 0.0)
                            else:
                                stg = cx.IOS.v(0, 256)
                                R.dma("sp", stg.r("p (a n) -> p a n", n=128), DV(st_ssm[l, sq_, 4 * g:4 * g + 4].rearrange("(a two) d n -> (two d) a n", two=2)))
                                for a in range(2):
                                    R.tr(F1.v(a * 128, a * 128 + 128), stg[:, a * 128:a * 128 + 128], ident_f(128, 128))
                                R.copy(sT, F1.v(0, 256))
                        sB = ssmB.v((2 * cx.i + cc) * 256, (2 * cx.i + cc) * 256 + 256)
                        R.copy(sB, sT, eng="act")
                        R.mm(F0.v(256, 512, cc * q, cc * q + q), CTt[:, t0_ + cc * q:t0_ + cc * q + q], sB)
                        pst = F1.v(0, 256)
                        R.mm(pst, TMH.v(256, 384, cc * q, cc * q + q), TMH.v(768, 1024, cc * q, cc * q + q))
                        pdc = F2.v(384, 388)
                        R.mm(pdc, ctv("sel%d" % cc, q), dtA_g)
                        dcy = SM.v(16, 20)
                        R.act(dcy, pdc, AF.Exp)
                        R.tt(sT.r("p (h d) -> p h d", d=64), sT.r("p (h d) -> p h d", d=64), dcy.us(2).bc([128, 4, 64]), ALU.mult)
                        R.tt(sT, sT, pst, ALU.add)
                        if isl:
                            for a in range(2):
                                R.tr(F1.v(a * 128, a * 128 + 128), sT[:, a * 128:a * 128 + 128], ident_f(128, 128))
                            stg = cx.IOS.v(256, 512)
                            R.copy(stg, F1.v(0, 256))
                            R.dma("sp", DV(og["ssm"][l, sq_, 4 * g:4 * g + 4].rearrange("(a two) d n -> (two d) a n", two=2)), stg.r("p (a n) -> p a n", n=128))
                    y1 = TM.v(768, 1024, 0, P)
                    R.tt(y1.r("p (h d) -> p h d", d=64), F0.v(256, 512, 0, P).r("p (h d) -> p h d", d=64), eacs_g.us(2).bc([P, 4, 64]), ALU.mult)
                    R.tt(y1, y1, F0.v(0, 256, 0, P), ALU.add)
                    y2 = TM.v(1024, 1280, 0, P)
                    R.tt(y2.r("p (h d) -> p h d", d=64), xtm.r("p (h d) -> p h d", d=64), c["dd"].v(4 * g, 4 * g + 4, 0, P).us(2).bc([P, 4, 64]), ALU.mult)
                    R.tt(y1, y1, y2, ALU.add)
                    R.tt(y1, y1, zs, ALU.mult)
                    ssq = SM.v(32, 33, 0, P)
                    R.act(y2, y1, AF.Square, accum=ssq)
                    R.act(ssq, ssq, AF.Sqrt, bias=b_eps(P), scale=1.0 / 256)
                    R.recip(ssq, ssq)
                    yn = TMH.v(1024, 1280, 0, P)
                    R.stt(yn, y1, ssq, MNW.v(g * 256, g * 256 + 256, 0, P), ALU.mult, ALU.mult)
                    for a in range(2):
                        R.tr(PTb.v(512 + a * 128, 512 + a * 128 + P), yn[:, a * 128:a * 128 + 128], ident_b(P, P))
                    for a in range(2):
                        R.copy(ykT.v((2 * g + a) * NMAX + t0_, (2 * g + a) * NMAX + t0_ + P), PTb.v(512 + a * 128, 512 + a * 128 + P), eng="act")

            if "B" in phases:
                run_streams([(lambda cx, g=g: mamba_group(cx, g)) for g in range(4)])
                merge(0)

            NCH = N // q

            def hgrn_head(cx, hh):
                TM, TMH, SM = cx.TM, cx.TMH, cx.SM
                F0, F1, F2, PTb = cx.f[0], cx.f[1], cx.f[2], cx.t
                wh = wl(win(l, OFF_HQ + hh * 128, 128), win(l, OFF_HF + hh * 128, 128), win(l, OFF_HI + hh * 128, 128), win(l, OFF_HG + hh * 128, 128))
                psq = F1.v(0, N)
                dense_fm(psq, wh, 0, hT, N)
                qs = cx.sfb(0, N)
                R.act(qs, psq, AF.Silu)
                psf = F2.v(0, N)
                dense_fm(psf, wh, 128, hT, N)
                sig = cx.sfb(1, N)
                R.act(sig, psf, AF.Sigmoid)
                gl = cx.sfb(2, N)
                R.act(gl, sig, AF.Ln, bias=c["lb"].v(hh, hh + 1), scale=c["oml"].v(hh, hh + 1))
                kk = cx.sfb(3, N)
                R.ts(kk, sig, c["noml"].v(hh, hh + 1), c["oml"].v(hh, hh + 1), ALU.mult, ALU.add)
                gxb = cx.PC.v(0, 1 + N)
                R.memset(gxb[:, 0:1], 0.0)
                R.scan(gxb[:, 1:1 + N], ones_f(128, N), gl, 0.0, ALU.mult, ALU.add)
                gx3 = gxb[:, 1:1 + N].r("p (c i) -> p c i", i=q)
                ref = gxb[:, q // 2:q // 2 + (NCH - 1) * q + 1:q] if NCH > 1 else gxb[:, q // 2:q // 2 + 1]
                beg = gxb[:, 0:(NCH - 1) * q + 1:q] if NCH > 1 else gxb[:, 0:1]
                end = gxb[:, q:q + (NCH - 1) * q + 1:q] if NCH > 1 else gxb[:, q:q + 1]
                A1 = cx.sfb(4, N)
                R.tt(A1.r("p (c i) -> p c i", i=q), gx3, ref.us(2).bc([128, NCH, q]), ALU.subtract)
                E1 = cx.sfb(1, N)
                R.act(E1, A1, AF.Exp)
                E2 = cx.sfb(2, N)
                R.act(E2, A1, AF.Exp, scale=-1.0)
                qe = cx.shb(0, N)
                R.tt(qe, qs, E1, ALU.mult)
                ke = cx.shb(1, N)
                R.tt(ke, kk, E2, ALU.mult)
                sm0 = SM.v(64, 64 + NCH)
                sm1 = SM.v(80, 80 + NCH)
                sm2 = SM.v(96, 96 + NCH)
                R.tt(sm0, ref, beg, ALU.subtract)
                R.act(sm0, sm0, AF.Exp)
                R.tt(sm1, end, ref, ALU.subtract)
                R.act(sm1, sm1, AF.Exp)
                R.tt(sm2, end, beg, ALU.subtract)
                R.act(sm2, sm2, AF.Exp)
                R.tt(E1.r("p (c i) -> p c i", i=q), E1.r("p (c i) -> p c i", i=q), sm0.us(2).bc([128, NCH, q]), ALU.mult)
                qg = cx.shb(2, N)
                R.tt(qg, qs, E1, ALU.mult)
                R.tt(E2.r("p (c i) -> p c i", i=q), E2.r("p (c i) -> p c i", i=q), sm1.us(2).bc([128, NCH, q]), ALU.mult)
                kl = cx.shb(3, N)
                R.tt(kl, kk, E2, ALU.mult)
                S_ = hgS[l].v(hh * 128, hh * 128 + 128)
                for tt in range(NT):
                    psg = (F2 if tt % 2 == 0 else F1).v(0, 128, 0, P)
                    dense_tm(psg, wh, 384, 128, hT, tt * P, P)
                    R.act(TM.v(tt * 128, tt * 128 + 128, 0, P), psg, AF.Silu)
                for tt in range(NT):
                    psv = (F2 if tt % 2 == 0 else F1).v(0, 128, 0, P)
                    dense_tm(psv, wh, 256, 128, hT, tt * P, P)
                    R.copy(TMH.v(tt * 128, tt * 128 + 128, 0, P), psv, eng="act")
                for tt in range(NT):
                    R.tr(PTb.v(tt * 128, tt * 128 + 128, 0, P), kl[:, tt * P:tt * P + P], ident_b(128, 128))
                R.copy(TMH.v(512, 512 + NT * 128, 0, P), PTb.v(0, NT * 128, 0, P))
                for tt in range(NT):
                    for cc in range(2):
                        tk = tt * P + cc * q
                        R.mm(F1.v(tt * q, tt * q + q, cc * q, cc * q + q), ke[:, tk:tk + q], qe[:, tk:tk + q])
                attm = TMH.v(1024, 1024 + NT * q, 0, P)
                R.tt(attm.r("p (t i) -> p t i", i=q), F1.v(0, NT * q, 0, P).r("p (t i) -> p t i", i=q), ctv("m01", q).us(1).bc([P, NT, q]), ALU.mult)
                for tt in range(NT):
                    for cc in range(2):
                        tk = tt * P + cc * q
                        sq_, isf, isl = seg.chunk_info(tt, cc)
                        if isf:
                            if seg.kind == "p":
                                R.memset(S_, 0.0)
                            else:
                                R.dma("sp", S_, DV(st_hg[l, sq_, hh]))
                        sB = hgB.v((2 * cx.i + cc) * 128, (2 * cx.i + cc) * 128 + 128)
                        R.copy(sB, S_, eng="act")
                        po = F0.v(tt * 128, tt * 128 + 128, cc * q, cc * q + q)
                        with R.atomic():
                            R.mm(po, TMH.v(1024 + tt * q, 1024 + tt * q + q, cc * q, cc * q + q),
                                 TMH.v(tt * 128, tt * 128 + 128, cc * q, cc * q + q), start=True, stop=False)
                            R.mm(po, qg[:, tk:tk + q], sB, start=False, stop=True)
                        pss = F2.v(0, 128)
                        R.mm(pss, TMH.v(512 + tt * 128, 512 + tt * 128 + 128, cc * q, cc * q + q),
                             TMH.v(tt * 128, tt * 128 + 128, cc * q, cc * q + q))
                        cg = tt * 2 + cc
                        R.stt(S_, S_, sm2[:, cg:cg + 1], pss, ALU.mult, ALU.add)
                        if isl:
                            R.dma("sp", DV(og["hg"][l, sq_, hh]), S_)
                gsl = TM.v(0, NT * 128, 0, P)
                po_ = F0.v(0, NT * 128, 0, P)
                sqv = TM.v(512, 512 + NT * 128, 0, P)
                R.act(sqv, po_, AF.Square)
                ssq = SM.v(40, 40 + NT, 0, P)
                R.add("dve", (lambda o_, i_: (lambda e: e.tensor_reduce(out=o_.ap, in_=i_.ap, axis=mybir.AxisListType.X, op=ALU.add)))(ssq, sqv.r("p (t v) -> p t v", v=128)), [sqv], [ssq])
                R.act(ssq, ssq, AF.Sqrt, bias=b_eps(P), scale=1.0 / 128)
                R.recip(ssq, ssq)
                on = TM.v(1024, 1024 + NT * 128, 0, P)
                R.tt(on.r("p (t v) -> p t v", v=128), po_.r("p (t v) -> p t v", v=128), ssq.us(2).bc([P, NT, 128]), ALU.mult)
                R.tt(on.r("p (t v) -> p t v", v=128), on.r("p (t v) -> p t v", v=128), c["hnw"].v(0, 128, 0, P).us(1).bc([P, NT, 128]), ALU.mult)
                yh = TMH.v(1536, 1536 + NT * 128, 0, P)
                R.tt(yh, on, gsl, ALU.mult)
                for tt in range(NT):
                    R.tr(PTb.v(512 + tt * P, 512 + tt * P + P), yh[:, tt * 128:tt * 128 + 128], ident_b(P, P))
                R.copy(ykT.v(hh * NMAX, hh * NMAX + N), PTb.v(512, 512 + N), eng="act")

            if "C" in phases:
                run_streams([(lambda cx, hh=hh: hgrn_head(cx, hh)) for hh in range(8)])
                merge(1)

            def rg_tile(cx, wrg, wrx, j, m):
                F0, F1, F2 = cx.f[0], cx.f[1], cx.f[2]
                psx = F1.v(0, N)
                dense_fm(psx, wrx, j * 128, hT, N)
                xc = cx.sfb(0, N)
                conv_fm(cx, psx, seg, histR[l], m, c["rcw"], c["rcb"], xc)
                xcb = cx.shb(0, N)
                R.copy(xcb, xc, eng="act")
                psa = F2.v(0, N)
                R.mm(psa, c["bdA"].v(m * 128, m * 128 + 128), xcb)
                psi = F0.v(0, N)
                R.mm(psi, c["bdI"].v(m * 128, m * 128 + 128), xcb)
                rr = cx.sfb(1, N)
                R.act(rr, psa, AF.Sigmoid, bias=c["rba"].v(m, m + 1))
                ig = cx.sfb(2, N)
                R.act(ig, psi, AF.Sigmoid, bias=c["rbi"].v(m, m + 1))
                aa = cx.sfb(3, N)
                R.act(aa, rr, AF.Exp, scale=c["rc1"].v(m, m + 1))
                R.act(rr, rr, AF.Exp, scale=c["rc2"].v(m, m + 1))
                R.act(rr, rr, AF.Sqrt, bias=b_one(), scale=-1.0)
                R.tt(ig, ig, xc, ALU.mult)
                R.tt(ig, ig, rr, ALU.mult)
                hs = cx.sfb(4, N)
                for pc in range(seg.npc):
                    n = seg.n_piece
                    hst = rgH[l].v(m * 2 + pc, m * 2 + pc + 1)
                    R.scan(hs[:, pc * n:(pc + 1) * n], aa[:, pc * n:(pc + 1) * n], ig[:, pc * n:(pc + 1) * n], hst, ALU.mult, ALU.add)
                    R.copy(hst, hs[:, (pc + 1) * n - 1:(pc + 1) * n])
                psr = F1.v(0, N)
                dense_fm(psr, wrg, j * 128, hT, N)
                ge = cx.sfb(1, N)
                R.act(ge, psr, AF.Gelu_apprx_tanh)
                R.tt(ykT.v(m * NMAX, m * NMAX + N), ge, hs, ALU.mult)

            for half in (range(2) if "D" in phases else ()):
                wrg = wl(win(l, OFF_RGATE + half * 512, 512))
                wrx = wl(win(l, OFF_RX + half * 512, 512))
                run_streams([(lambda cx, j=j: rg_tile(cx, wrg, wrx, j, half * 4 + j)) for j in range(4)])
            if "D" in phases:
                merge(2)

            if seg.last:
                for pc in range(seg.npc):
                    sq_ = seg.seqs[pc]
                    hm = histM[l].v(0, 96).r("p (c a k) -> p c a k", a=2, k=3)[:, :, pc, :]
                    hr = histR[l].v(0, 48).r("p (c a k) -> p c a k", a=2, k=3)[:, :, pc, :]
                    rh = rgH[l].v(0, 16).r("p (m a) -> p m a", a=2)[:, :, pc]
                    for k in range(3):
                        R.dma("sp", DV(og["sc"][l, sq_, k].rearrange("(c p) -> p c", p=128)), hm[:, :, k], slow=True)
                        R.dma("sp", DV(og["rc"][l, sq_, k].rearrange("(c p) -> p c", p=128)), hr[:, :, k], slow=True)
                    R.dma("sp", DV(og["rg"][l, sq_].rearrange("(m p) -> p m", p=128)), rh, slow=True)

            for cch in range(8):
                R.copy(hT.v(cch * NMAX, cch * NMAX + N), mergedT.v(cch * NMAX, cch * NMAX + N), eng="act")
            for half in (range(2) if "E" in phases else ()):
                wo = wl((w_out[l, :, half * 512:(half + 1) * 512], 8, None))
                for j in range(4):
                    dc = half * 4 + j
                    ps = psA().v(0, N)
                    dense_fm(ps, wo, j * 128, hT, N)
                    xv = xT.v(dc * NMAX, dc * NMAX + N)
                    R.tt(xv, xv, ps, ALU.add)

            rmsnorm_fm(c["nffn"], N, hT)
            for fh in (range(2) if "F" in phases else ()):
                fb = fh * 11
                fc = 0
                while fc < 11:
                    nf = min(4, 11 - fc)
                    wg_ = wl((w_ffn_in[l, :, (fb + fc) * 128:(fb + fc + nf) * 128], 8, None))
                    wu_ = wl((w_ffn_in[l, :, FF + (fb + fc) * 128:FF + (fb + fc + nf) * 128], 8, None))
                    for j in range(nf):
                        psg = psA().v(0, N)
                        dense_fm(psg, wg_, j * 128, hT, N)
                        sg = sf(N)
                        R.act(sg, psg, AF.Silu)
                        psu = psA().v(0, N)
                        dense_fm(psu, wu_, j * 128, hT, N)
                        R.tt(actT.v((fc + j) * NMAX, (fc + j) * NMAX + N), sg, psu, ALU.mult)
                    fc += nf
                for half in range(2):
                    pss = [PALL[j].v(0, N) for j in range(4)]
                    f0 = 0
                    while f0 < 11:
                        nf = min(8, 11 - f0)
                        wo = wl((w_ffn_out[l, (fb + f0) * 128:(fb + f0 + nf) * 128, half * 512:(half + 1) * 512], nf, None))
                        for j in range(4):
                            dense_fm(pss[j], wo, j * 128, actT, N, nk=nf, first=(f0 == 0), last=(f0 + nf == 11), kbase=f0)
                        f0 += nf
                    for j in range(4):
                        dc = half * 4 + j
                        xv = xT.v(dc * NMAX, dc * NMAX + N)
                        R.tt(xv, xv, pss[j], ALU.add)

        def run_segment(seg):
            N, P, NT = seg.N, seg.P, seg.NT
            for tt in range(NT):
                pc = (tt * P) // seg.n_piece
                tk = seg.tok0[pc] + (tt * P) % seg.n_piece
                io = IO.v(0, 1024, 0, P)
                if seg.kind == "p":
                    R.dma("sp", io, DV(xp[seg.seqs[0], tk:tk + P, :]))
                else:
                    for a in range(seg.npc):
                        R.dma("sp", IO.v(0, 1024, a * 32, a * 32 + 32), DV(xs[seg.seqs[a], :, :]))
                for half in (range(2) if "x" not in phases else ()):
                    pf = psA()
                    for j in range(4):
                        cch = half * 4 + j
                        R.tr(pf.v(j * 128, j * 128 + P), io[:, cch * 128:cch * 128 + 128], ident_f(P, P))
                    for j in (range(4) if "y" not in phases else ()):
                        cch = half * 4 + j
                        R.copy(xT.v(cch * NMAX + tt * P, cch * NMAX + tt * P + P), pf.v(j * 128, j * 128 + P), eng=("act" if (j % 2 and "v" not in phases) else "dve"))
            if stop <= 2:
                return
            for l in range(nlayers):
                layer_pass(seg, l)
            if stop <= 3:
                return
            rmsnorm_fm(nfin, N, mergedT)
            for tt in range(NT):
                io = IO.v(0, 1024, 0, P)
                for half in range(2):
                    pf = psA()
                    for j in range(4):
                        cch = half * 4 + j
                        R.tr(pf.v(j * 128, j * 128 + 128, 0, P), mergedT.v(cch * NMAX + tt * P, cch * NMAX + tt * P + P), ident_f(128, 128))
                    R.copy(io[:, half * 512:half * 512 + 512], pf.v(0, 512, 0, P), eng=("act" if half else "dve"))
                if seg.kind == "p":
                    tk = seg.tok0[0] + tt * P
                    R.dma("sp", DV(yp[seg.seqs[0], tk:tk + P, :]), io)
                else:
                    for a in range(seg.npc):
                        R.dma("sp", DV(ys[seg.seqs[a], :, :]), IO.v(0, 1024, a * 32, a * 32 + 32))

        segs = []
        for s in range(nseq_p):
            for j in range(nseg_p):
                segs.append(Seg("p", [s], [j * 512], 512, 64, j == 0, j == nseg_p - 1))
        if with_sample:
            segs.append(Seg("s", [0, 1], [0, 0], 32, 32, True, True))
        for sg_ in (segs if stop >= 2 else ()):
            run_segment(sg_)
        if dbg_out is not None:
            pass
        R.finalize(nc, es)
    return nc, R


_WNAMES = ["norm_mix", "w_in", "mb_conv_w", "mb_conv_b", "mb_dt_bias", "mb_a_log", "mb_d", "mb_norm_w",
           "hg_lb_logits", "hg_norm_w", "rg_conv_w", "rg_conv_b", "rg_w_a", "rg_b_a", "rg_w_i", "rg_b_i",
           "rg_lambda", "w_branch", "w_out", "norm_ffn", "w_ffn_in", "w_ffn_out", "norm_final"]


def make_in_maps(inputs, ncores=8):
    f = lambda a: np.ascontiguousarray(np.asarray(a, dtype=np.float32))
    wts = {k: f(inputs[k]) for k in _WNAMES}
    maps = []
    for c in range(ncores):
        s = slice(2 * c, 2 * c + 2)
        m = dict(wts)
        m["xp"] = f(inputs["x_prompt"][s])
        m["xs"] = f(inputs["x_sample"][s])
        m["st_ssm"] = f(inputs["state_ssm"][:, s])
        m["st_sc"] = f(inputs["state_ssm_conv"][:, s])
        m["st_hg"] = f(inputs["state_hgrn"][:, s])
        m["st_rg"] = f(inputs["state_rglru"][:, s])
        m["st_rc"] = f(inputs["state_rglru_conv"][:, s])
        m["ctab"] = CTAB
        maps.append(m)
    return maps


def gather(results):
    cat = lambda k, ax: np.concatenate([np.asarray(r[k], dtype=np.float32) for r in results], axis=ax)
    out = [cat("yp", 0), cat("ys", 0)]
    for g in ("p", "s"):
        for nm in ("ssm", "sc", "hg", "rg", "rc"):
            out.append(cat("o_%s_%s" % (nm, g), 1))
    return tuple(out)


def kernel(**inputs):
    nc, _ = build()
    maps = make_in_maps(inputs)
    res = run_bass_kernel_spmd(nc, maps, core_ids=list(range(8)))
    return gather(res.results)
```
